# Optimizing a Trainium2 kernel written in Bass

```python
import math
import numpy as np
import jax
import jax.numpy as jnp
from jax import lax

D_MODEL = 2048
BATCH = 16
SEQ = 2048
DEPTH = 1
DEC_BATCH = 32
DEC_SEQ = 1
PAST_LEN = 16384
PAGE_SIZE = 128

HEAD_DIM = 128
NSA_HEADS = 8
NSA_KV_HEADS = 2
NSA_GROUP = NSA_HEADS // NSA_KV_HEADS
NSA_WIDTH = NSA_HEADS * HEAD_DIM
NSA_KV_WIDTH = NSA_KV_HEADS * HEAD_DIM
CMP_LEN = 32
CMP_STRIDE = 16
CMP_R = CMP_LEN // CMP_STRIDE
CMP_HIDDEN = 128
SEL_BLOCK = 64
SEL_TOPK = 16
N_LOCAL_SEL = 2
WINDOW = 512
Q_BLOCK = 128
FORCE_SCORE = 1e4
ALIBI_MAX = 8.0
GDN_HEADS = 8
GDN_DK = 128
GDN_DV = 128
GDN_QK_WIDTH = GDN_HEADS * GDN_DK
GDN_V_WIDTH = GDN_HEADS * GDN_DV
GDN_CONV_CH = 2 * GDN_QK_WIDTH + GDN_V_WIDTH
CONV_W = 4
GDN_CHUNK = 64
EPS = 1e-6
SCALE = HEAD_DIM ** -0.5
SPLITS = (NSA_WIDTH, 2 * NSA_KV_WIDTH, 2 * NSA_KV_WIDTH, 2 * NSA_KV_WIDTH, 3 * NSA_HEADS, NSA_WIDTH,
          GDN_CONV_CH, GDN_HEADS, GDN_HEADS, GDN_V_WIDTH, D_MODEL, D_MODEL)
IN_WIDTH = sum(SPLITS)

kernel_name = "nsa_gated_deltanet_parallel_hybrid_step"


def rms_norm(x, w):
    xf = x.astype(jnp.float32)
    y = xf * lax.rsqrt(jnp.mean(xf * xf, axis=-1, keepdims=True) + EPS)
    return (y * w.astype(jnp.float32)).astype(x.dtype)


def l2_norm(x):
    return x * lax.rsqrt(jnp.sum(x * x, axis=-1, keepdims=True) + EPS)


def alibi_slopes():
    h = np.arange(1, NSA_HEADS + 1, dtype=np.float32)
    s = np.power(np.float32(2.0), -ALIBI_MAX * h / NSA_HEADS).astype(np.float32)
    return jnp.asarray(s, dtype=jnp.float32).reshape(NSA_KV_HEADS, NSA_GROUP)


def split_proj(h):
    bounds = np.cumsum(np.array(SPLITS))[:-1].tolist()
    return jnp.split(h, bounds, axis=-1)


def masked_softmax(s, mask, axis):
    s = jnp.where(mask, s, -jnp.inf)
    m = jnp.max(s, axis=axis, keepdims=True)
    m = jnp.where(jnp.isfinite(m), m, 0.0)
    e = jnp.where(mask, jnp.exp(s - m), 0.0)
    den = jnp.sum(e, axis=axis, keepdims=True)
    return e / jnp.where(den > 0, den, 1.0)


def dense_gqa(q, k, v, mask, dist, slopes):
    s = jnp.einsum('bqgrd,bkgd->bgrqk', q, k, preferred_element_type=jnp.float32) * SCALE
    s = s - slopes[None, :, :, None, None] * dist[:, None, None]
    p = masked_softmax(s, mask[:, None, None], axis=-1)
    o = jnp.einsum('bgrqk,bkgd->bqgrd', p.astype(v.dtype), v)
    return o, p


def nsa_project(q_a, kv_cmp, kv_slc, kv_win, q_norm_w, k_norm_slc_w, k_norm_win_w):
    B, T, _ = q_a.shape
    q = rms_norm(q_a.reshape(B, T, NSA_KV_HEADS, NSA_GROUP, HEAD_DIM), q_norm_w)
    kv_shape = (B, T, 2, NSA_KV_HEADS, HEAD_DIM)

    def knorm(kv, w):
        kv = kv.reshape(kv_shape)
        return jnp.stack([rms_norm(kv[:, :, 0], w), kv[:, :, 1]], axis=2)

    return q, kv_cmp.reshape(kv_shape), knorm(kv_slc, k_norm_slc_w), knorm(kv_win, k_norm_win_w)


def cmp_partials(rows, w1):
    b, n, g, d = rows.shape
    ch = rows.reshape(b, n // CMP_STRIDE, CMP_STRIDE, g, d)
    return jnp.einsum('bcsgd,rsde->bcrge', ch, w1)


def cmp_finish(part, b1, w2):
    n_cmp = part.shape[1] - CMP_R + 1
    h = sum(part[:, r:r + n_cmp, r] for r in range(CMP_R))
    return jnp.einsum('bngh,hd->bngd', jax.nn.silu(h + b1), w2)


def compress_partials(rows_kv, cw):
    return cmp_partials(rows_kv[:, :, 0], cw[0]), cmp_partials(rows_kv[:, :, 1], cw[3])


def compress_finish(pk, pv, cw):
    kc = rms_norm(cmp_finish(pk, cw[1], cw[2]), cw[6])
    vc = cmp_finish(pv, cw[4], cw[5])
    return kc, vc


def cmp_attend(q, kc, vc, qpos, slopes):
    kend = jnp.arange(kc.shape[1]) * CMP_STRIDE + CMP_LEN - 1
    d = qpos[:, None] - kend[None]
    return dense_gqa(q, kc, vc, (d >= 0)[None], d[None].astype(jnp.float32), slopes)


def sel_map(n_cmp, n_sel):
    cs = np.arange(n_cmp) * CMP_STRIDE
    ss = np.arange(n_sel) * SEL_BLOCK
    ov = np.minimum(cs[:, None] + CMP_LEN, ss[None] + SEL_BLOCK) - np.maximum(cs[:, None], ss[None])
    return jnp.asarray(np.clip(ov, 0, None).astype(np.float32) / CMP_LEN, dtype=jnp.float32)


def select_blocks(p_cmp, qpos, n_cmp, n_sel):
    imp = jnp.einsum('bgrqn,nj->bqgj', p_cmp, sel_map(n_cmp, n_sel))
    j = jnp.arange(n_sel)[None]
    cur = (qpos // SEL_BLOCK)[:, None]
    visible = (j * SEL_BLOCK <= qpos[:, None])[None, :, None, :]
    forced = ((j == 0) | ((cur - j >= 0) & (cur - j < N_LOCAL_SEL)))[None, :, None, :]
    score = jnp.where(visible, jnp.where(forced, FORCE_SCORE, imp), -FORCE_SCORE)
    _, idx = lax.top_k(score, min(SEL_TOPK, n_sel))
    valid = idx * SEL_BLOCK <= qpos[None, :, None, None]
    return idx, valid


def slc_attend(q, kg, vg, tok, valid, qpos, slopes):
    s = jnp.einsum('bqgrd,bqgnkd->bqgrnk', q, kg, preferred_element_type=jnp.float32) * SCALE
    dist = (qpos[None, :, None, None, None] - tok).astype(jnp.float32)
    s = s - slopes[None, None, :, :, None, None] * dist[:, :, :, None]
    mask = (valid[..., None] & (dist >= 0))[:, :, :, None]
    p = masked_softmax(s, mask, axis=(-2, -1))
    return jnp.einsum('bqgrnk,bqgnkd->bqgrd', p.astype(vg.dtype), vg)


def slc_prompt(q, kv_slc, idx, valid, slopes):
    B, T, G, R, D = q.shape
    nqb = T // Q_BLOCK

    def blocks(a):
        return a.reshape((B * nqb, Q_BLOCK) + a.shape[2:])

    bid = jnp.repeat(jnp.arange(B), nqb)
    qpos = jnp.tile(jnp.arange(T).reshape(nqb, Q_BLOCK), (B, 1))
    gidx = jnp.arange(G)[None, :, None, None]

    def step(args):
        qb, ib, vb, b, pos = args
        tok = ib[..., None] * SEL_BLOCK + jnp.arange(SEL_BLOCK)
        rows = kv_slc[b]
        kg = rows[tok, 0, gidx]
        vg = rows[tok, 1, gidx]
        return slc_attend(qb[None], kg[None], vg[None], tok[None], vb[None], pos, slopes)[0]

    o = lax.map(step, (blocks(q), blocks(idx), blocks(valid), bid, qpos))
    return o.reshape(B, T, G, R, D)


def slc_sample(q, kv_new, cache_slc, page_table, idx, valid, qpos, past_len, slopes):
    DB, DS, G = q.shape[0], q.shape[1], q.shape[2]
    tok = idx[..., None] * SEL_BLOCK + jnp.arange(SEL_BLOCK)
    b = jnp.arange(DB)[:, None, None, None, None]
    g = jnp.arange(G)[None, None, :, None, None]
    tp = jnp.minimum(tok, past_len - 1)
    phys = page_table[b, tp // PAGE_SIZE]
    off = tp % PAGE_SIZE
    tn = jnp.clip(tok - past_len, 0, DS - 1)
    in_past = (tok < past_len)[..., None]
    kg = jnp.where(in_past, cache_slc[phys, off, 0, g], kv_new[b, tn, 0, g])
    vg = jnp.where(in_past, cache_slc[phys, off, 1, g], kv_new[b, tn, 1, g])
    return slc_attend(q, kg, vg, tok, valid, qpos, slopes)


def win_prompt(q, kv_win, slopes):
    B, T, G, R, D = q.shape
    nqb = T // Q_BLOCK
    span = Q_BLOCK + WINDOW
    kv_pad = jnp.pad(kv_win, ((0, 0), (WINDOW, 0), (0, 0), (0, 0), (0, 0)))
    qb = jnp.swapaxes(q.reshape(B, nqb, Q_BLOCK, G, R, D), 0, 1)

    def step(args):
        i, qblk = args
        start = i * Q_BLOCK
        kvb = lax.dynamic_slice_in_dim(kv_pad, start, span, axis=1)
        kpos = start - WINDOW + jnp.arange(span)
        qpos = start + jnp.arange(Q_BLOCK)
        d = qpos[:, None] - kpos[None]
        mask = (d >= 0) & (d <= WINDOW) & (kpos >= 0)[None]
        o, _ = dense_gqa(qblk, kvb[:, :, 0], kvb[:, :, 1], mask[None], d[None].astype(jnp.float32), slopes)
        return o

    o = lax.map(step, (jnp.arange(nqb), qb))
    return jnp.swapaxes(o, 0, 1).reshape(B, T, G, R, D)


def nsa_prompt(q, kv_cmp, kv_slc, kv_win, cw, slopes):
    T = q.shape[1]
    qpos = jnp.arange(T)
    pk, pv = compress_partials(kv_cmp, cw)
    kc, vc = compress_finish(pk, pv, cw)
    o_cmp, p_cmp = cmp_attend(q, kc, vc, qpos, slopes)
    idx, valid = select_blocks(p_cmp, qpos, kc.shape[1], -(-T // SEL_BLOCK))
    o_slc = slc_prompt(q, kv_slc, idx, valid, slopes)
    o_win = win_prompt(q, kv_win, slopes)
    return o_cmp, o_slc, o_win


def nsa_sample(q, kv_cmp, kv_slc, kv_win, cache_cmp, cache_slc, st_win, page_table, cw, slopes):
    DB, DS = q.shape[0], q.shape[1]
    past_len = page_table.shape[1] * PAGE_SIZE
    qpos = past_len + jnp.arange(DS)
    past = cache_cmp[page_table].reshape(DB, past_len, 2, NSA_KV_HEADS, HEAD_DIM)
    pk_past, pv_past = compress_partials(past, cw)
    n_new = (DS // CMP_STRIDE) * CMP_STRIDE
    pk_new, pv_new = compress_partials(kv_cmp[:, :n_new], cw)
    kc, vc = compress_finish(jnp.concatenate([pk_past, pk_new], axis=1),
                             jnp.concatenate([pv_past, pv_new], axis=1), cw)
    o_cmp, p_cmp = cmp_attend(q, kc, vc, qpos, slopes)
    idx, valid = select_blocks(p_cmp, qpos, kc.shape[1], -(-(past_len + DS) // SEL_BLOCK))
    o_slc = slc_sample(q, kv_slc, cache_slc, page_table, idx, valid, qpos, past_len, slopes)
    wb = st_win.shape[1]
    kv_all = jnp.concatenate([st_win.astype(kv_win.dtype), kv_win], axis=1)
    kpos = past_len - wb + jnp.arange(wb + DS)
    d = qpos[:, None] - kpos[None]
    mask = (d >= 0) & (d <= WINDOW)
    o_win, _ = dense_gqa(q, kv_all[:, :, 0], kv_all[:, :, 1], mask[None], d[None].astype(jnp.float32), slopes)
    return o_cmp, o_slc, o_win, kv_all[:, -wb:]


def nsa_merge(o_cmp, o_slc, o_win, g_nsa, z_a, w_pa):
    B, T, _ = z_a.shape
    g = jax.nn.sigmoid(g_nsa.astype(jnp.float32)).reshape(B, T, 3, NSA_KV_HEADS, NSA_GROUP, 1)
    o = g[:, :, 0] * o_cmp + g[:, :, 1] * o_slc + g[:, :, 2] * o_win
    o = o.reshape(B, T, NSA_WIDTH).astype(z_a.dtype) * jax.nn.silu(z_a)
    return o @ w_pa


def causal_conv(buf, w):
    T = buf.shape[1] - (CONV_W - 1)
    return sum(buf[:, i:i + T] * w[i] for i in range(CONV_W))


def gdn_chunked(q, k, v, g, beta, S0):
    B, T, H, DK = q.shape
    C = GDN_CHUNK
    Tp = -(-T // C) * C
    nc = Tp // C

    def prep(a):
        a = jnp.pad(a, ((0, 0), (0, Tp - T)) + ((0, 0),) * (a.ndim - 2))
        a = a.reshape((B, nc, C) + a.shape[2:])
        return jnp.moveaxis(jnp.moveaxis(a, 1, 0), 3, 2)

    xs = tuple(prep(a) for a in (q, k, v, g, beta))
    tril = jnp.tril(jnp.ones((C, C), dtype=bool))
    stril = jnp.tril(jnp.ones((C, C), dtype=bool), -1)
    eye = jnp.eye(C, dtype=jnp.float32)

    def step(S, xc):
        qc, kc, vc, gc, bc = xc
        dec = jnp.cumsum(gc, axis=-1)
        L = jnp.exp(jnp.where(tril, dec[..., :, None] - dec[..., None, :], -jnp.inf))
        kb = kc * bc[..., None]
        vb = vc * bc[..., None]
        A = jnp.where(stril, jnp.einsum('bhik,bhjk->bhij', kb, kc) * L, 0.0)
        Tm = lax.linalg.triangular_solve(eye + A, jnp.broadcast_to(eye, A.shape),
                                         left_side=True, lower=True, unit_diagonal=True)
        u = Tm @ vb
        w = Tm @ (kb * jnp.exp(dec)[..., None])
        vn = u - w @ S
        att = jnp.where(tril, jnp.einsum('bhik,bhjk->bhij', qc, kc) * L, 0.0)
        o = (qc * jnp.exp(dec)[..., None]) @ S + att @ vn
        dl = dec[..., -1:]
        S = S * jnp.exp(dl)[..., None] + jnp.einsum('bhck,bhcv->bhkv', kc * jnp.exp(dl - dec)[..., None], vn)
        return S, o

    S, o = lax.scan(step, S0, xs)
    o = o.transpose(1, 0, 3, 2, 4).reshape(B, Tp, H, -1)[:, :T]
    return o, S


def gdn_branch(qkv, a, b, z, conv_buf, S0, conv_w, a_log, dt_bias, gdn_norm_w, w_pb):
    B, T, _ = qkv.shape
    buf = jnp.concatenate([conv_buf.astype(qkv.dtype), qkv], axis=1)
    c = jax.nn.silu(causal_conv(buf, conv_w)).astype(jnp.float32)
    qg, kg, vg = jnp.split(c, [GDN_QK_WIDTH, 2 * GDN_QK_WIDTH], axis=-1)
    q = l2_norm(qg.reshape(B, T, GDN_HEADS, GDN_DK)) * (GDN_DK ** -0.5)
    k = l2_norm(kg.reshape(B, T, GDN_HEADS, GDN_DK))
    v = vg.reshape(B, T, GDN_HEADS, GDN_DV)
    beta = jax.nn.sigmoid(b.astype(jnp.float32))
    g = -jnp.exp(a_log.astype(jnp.float32)) * jax.nn.softplus(a.astype(jnp.float32) + dt_bias.astype(jnp.float32))
    o, S = gdn_chunked(q, k, v, g, beta, S0.astype(jnp.float32))
    o = rms_norm(o, gdn_norm_w).reshape(B, T, GDN_V_WIDTH).astype(z.dtype) * jax.nn.silu(z)
    return o @ w_pb, S.astype(S0.dtype), buf[:, -(CONV_W - 1):]


def merge_residual(x, y_a, y_b, gate_a, gate_b, w_o):
    m = jax.nn.sigmoid(gate_a) * y_a + jax.nn.sigmoid(gate_b) * y_b
    return x + m @ w_o


def decoder_layer(x_p, x_s, cache_cmp, cache_slc, st_win, st_S, st_conv, page_table, w):
    (norm_w, w_in, q_norm_w, k_norm_cmp_w, k_norm_slc_w, k_norm_win_w,
     cmp_w1_k, cmp_b1_k, cmp_w2_k, cmp_w1_v, cmp_b1_v, cmp_w2_v,
     conv_w, a_log, dt_bias, gdn_norm_w, w_pa, w_pb, w_o) = w
    cw = (cmp_w1_k, cmp_b1_k, cmp_w2_k, cmp_w1_v, cmp_b1_v, cmp_w2_v, k_norm_cmp_w)
    gw = (conv_w, a_log, dt_bias, gdn_norm_w, w_pb)
    slopes = alibi_slopes()

    B, T, _ = x_p.shape
    (q_a, kvc, kvs, kvw, g_nsa, z_a, qkv_b, a_b, b_b, z_b, gate_a, gate_b) = split_proj(rms_norm(x_p, norm_w) @ w_in)
    q, kvc, kvs, kvw = nsa_project(q_a, kvc, kvs, kvw, q_norm_w, k_norm_slc_w, k_norm_win_w)
    o_cmp, o_slc, o_win = nsa_prompt(q, kvc, kvs, kvw, cw, slopes)
    y_a = nsa_merge(o_cmp, o_slc, o_win, g_nsa, z_a, w_pa)
    zero_conv = jnp.zeros((B, CONV_W - 1, GDN_CONV_CH), x_p.dtype)
    zero_S = jnp.zeros((B, GDN_HEADS, GDN_DK, GDN_DV), x_p.dtype)
    y_b, S_p, conv_p = gdn_branch(qkv_b, a_b, b_b, z_b, zero_conv, zero_S, *gw)
    y_p = merge_residual(x_p, y_a, y_b, gate_a, gate_b, w_o)
    win_p = kvw[:, -min(WINDOW, T):]

    (q_a, kvc_s, kvs_s, kvw_s, g_nsa, z_a, qkv_b, a_b, b_b, z_b, gate_a, gate_b) = split_proj(rms_norm(x_s, norm_w) @ w_in)
    q, kvc_s, kvs_s, kvw_s = nsa_project(q_a, kvc_s, kvs_s, kvw_s, q_norm_w, k_norm_slc_w, k_norm_win_w)
    o_cmp, o_slc, o_win, win_s = nsa_sample(q, kvc_s, kvs_s, kvw_s, cache_cmp, cache_slc, st_win, page_table, cw, slopes)
    y_a = nsa_merge(o_cmp, o_slc, o_win, g_nsa, z_a, w_pa)
    y_b, S_s, conv_s = gdn_branch(qkv_b, a_b, b_b, z_b, st_conv, st_S, *gw)
    y_s = merge_residual(x_s, y_a, y_b, gate_a, gate_b, w_o)
    return y_p, y_s, (kvc, kvs, win_p, S_p, conv_p, kvc_s, kvs_s, win_s, S_s, conv_s)


def setup_inputs(seed: int = 0) -> dict:
    key = jax.random.key(seed)
    ks = jax.random.split(key, 40)
    f32 = jnp.float32
    n_pages = PAST_LEN // PAGE_SIZE
    n_phys = (5 * DEC_BATCH * n_pages) // 4
    win_buf = min(WINDOW, PAST_LEN)

    def nrm(k, shape, scale):
        return scale * jax.random.normal(k, shape, f32)

    def gain(k, shape):
        return 1.0 + 0.02 * jax.random.normal(k, shape, f32)

    kv_pool = (DEPTH, n_phys, PAGE_SIZE, 2, NSA_KV_HEADS, HEAD_DIM)
    perm = jax.random.permutation(ks[7], n_phys)
    page_table = perm[: DEC_BATCH * n_pages].reshape(DEC_BATCH, n_pages).astype(jnp.int32)
    dt = jnp.exp(jax.random.uniform(ks[25], (DEPTH, GDN_HEADS), f32, math.log(1e-3), math.log(1e-1)))
    return {
        "x_prompt": nrm(ks[0], (BATCH, SEQ, D_MODEL), 1.0),
        "x_sample": nrm(ks[1], (DEC_BATCH, DEC_SEQ, D_MODEL), 1.0),
        "cache_cmp_kv": nrm(ks[2], kv_pool, 1.0),
        "cache_slc_kv": nrm(ks[3], kv_pool, 1.0),
        "state_win_kv": nrm(ks[4], (DEPTH, DEC_BATCH, win_buf, 2, NSA_KV_HEADS, HEAD_DIM), 1.0),
        "state_gdn_S": nrm(ks[5], (DEPTH, DEC_BATCH, GDN_HEADS, GDN_DK, GDN_DV), 0.1),
        "state_gdn_conv": nrm(ks[6], (DEPTH, DEC_BATCH, CONV_W - 1, GDN_CONV_CH), 1.0),
        "page_table": page_table,
        "norm_w": gain(ks[8], (DEPTH, D_MODEL)),
        "w_in": nrm(ks[9], (DEPTH, D_MODEL, IN_WIDTH), D_MODEL ** -0.5),
        "q_norm_w": gain(ks[10], (DEPTH, HEAD_DIM)),
        "k_norm_cmp_w": gain(ks[11], (DEPTH, HEAD_DIM)),
        "k_norm_slc_w": gain(ks[12], (DEPTH, HEAD_DIM)),
        "k_norm_win_w": gain(ks[13], (DEPTH, HEAD_DIM)),
        "cmp_w1_k": nrm(ks[14], (DEPTH, CMP_R, CMP_STRIDE, HEAD_DIM, CMP_HIDDEN), (CMP_LEN * HEAD_DIM) ** -0.5),
        "cmp_b1_k": nrm(ks[15], (DEPTH, CMP_HIDDEN), 0.02),
        "cmp_w2_k": nrm(ks[16], (DEPTH, CMP_HIDDEN, HEAD_DIM), CMP_HIDDEN ** -0.5),
        "cmp_w1_v": nrm(ks[17], (DEPTH, CMP_R, CMP_STRIDE, HEAD_DIM, CMP_HIDDEN), (CMP_LEN * HEAD_DIM) ** -0.5),
        "cmp_b1_v": nrm(ks[18], (DEPTH, CMP_HIDDEN), 0.02),
        "cmp_w2_v": nrm(ks[19], (DEPTH, CMP_HIDDEN, HEAD_DIM), CMP_HIDDEN ** -0.5),
        "conv_w": nrm(ks[20], (DEPTH, CONV_W, GDN_CONV_CH), CONV_W ** -0.5),
        "a_log": jnp.log(jax.random.uniform(ks[21], (DEPTH, GDN_HEADS), f32, 1.0, 16.0)),
        "dt_bias": dt + jnp.log(-jnp.expm1(-dt)),
        "gdn_norm_w": gain(ks[22], (DEPTH, GDN_DV)),
        "w_pa": nrm(ks[23], (DEPTH, NSA_WIDTH, D_MODEL), NSA_WIDTH ** -0.5),
        "w_pb": nrm(ks[24], (DEPTH, GDN_V_WIDTH, D_MODEL), GDN_V_WIDTH ** -0.5),
        "w_o": nrm(ks[26], (DEPTH, D_MODEL, D_MODEL), D_MODEL ** -0.5),
    }


def reference(x_prompt, x_sample, cache_cmp_kv, cache_slc_kv, state_win_kv, state_gdn_S, state_gdn_conv, page_table,
              norm_w, w_in, q_norm_w, k_norm_cmp_w, k_norm_slc_w, k_norm_win_w,
              cmp_w1_k, cmp_b1_k, cmp_w2_k, cmp_w1_v, cmp_b1_v, cmp_w2_v,
              conv_w, a_log, dt_bias, gdn_norm_w, w_pa, w_pb, w_o):
    weights = (norm_w, w_in, q_norm_w, k_norm_cmp_w, k_norm_slc_w, k_norm_win_w,
               cmp_w1_k, cmp_b1_k, cmp_w2_k, cmp_w1_v, cmp_b1_v, cmp_w2_v,
               conv_w, a_log, dt_bias, gdn_norm_w, w_pa, w_pb, w_o)
    y_p, y_s = x_prompt, x_sample
    per_layer = []
    for layer in range(DEPTH):
        y_p, y_s, st = decoder_layer(y_p, y_s, cache_cmp_kv[layer], cache_slc_kv[layer], state_win_kv[layer],
                                     state_gdn_S[layer], state_gdn_conv[layer], page_table,
                                     tuple(wt[layer] for wt in weights))
        per_layer.append(st)
    (cmp_p, slc_p, win_p, S_p, conv_p, cmp_s, slc_s, win_s, S_s, conv_s) = [jnp.stack(a) for a in zip(*per_layer)]
    return (y_p, y_s, cmp_p, slc_p, win_p, S_p, conv_p, cmp_s, slc_s, win_s, S_s, conv_s)
```

```python
import numpy as np
from contextlib import ExitStack
import concourse.bass as bass
import concourse.mybir as mybir
from concourse.bass_utils import run_bass_kernel_spmd

F32 = mybir.dt.float32
BF16 = mybir.dt.bfloat16
I32 = mybir.dt.int32
AF = mybir.ActivationFunctionType
ALU = mybir.AluOpType
AX = mybir.AxisListType

ENGS = ("pe", "act", "dve", "pool", "sp")
NCORES = 8
T = 2048
DM = 2048
INW = 11816
NT = T // 128
EPS = 1e-6
SCALE = 128 ** -0.5
O_Q, O_KVC, O_KVS, O_KVW, O_GN, O_ZA, O_QKV, O_A, O_B, O_ZB, O_GA, O_GB = (
    0, 1024, 1536, 2048, 2560, 2584, 3608, 6680, 6688, 6696, 7720, 9768)
NTOK = 2 * T + 128


class Buf:
    __slots__ = ("name", "w", "r")

    def __init__(self, name):
        self.name = name
        self.w = None
        self.r = {}


class KB:
    def __init__(self):
        self.nc = bass.Bass("TRN2", target_bir_lowering=False)
        self.es = ExitStack()
        self.q = {e: [] for e in ENGS}
        self.cnt = {e: 0 for e in ENGS}
        self.esem = {}
        self.dsem = {}
        self.dcnt = {}
        self.seen = {e: {} for e in ENGS}
        self.nbuf = 0
        self.rr = 0
        for e in ENGS:
            self.esem[e] = self.es.enter_context(self.nc.semaphore("s_" + e))

    def sb(self, name, shape, dt=F32):
        return self.es.enter_context(self.nc.sbuf_tensor(name, list(shape), dt))

    def ps(self, name, shape, dt=F32):
        return self.es.enter_context(self.nc.psum_tensor(name, list(shape), dt))

    def buf(self, name=None):
        self.nbuf += 1
        return Buf(name or "b%d" % self.nbuf)

    def bufs(self, n):
        return [self.buf() for _ in range(n)]

    def _sem(self, key):
        return self.esem[key] if key in self.esem else self.dsem[key]

    def _wait(self, eng, k, v):
        if k in self.dcnt:
            v = max(v, self.dcnt[k])
        if self.seen[eng].get(k, 0) >= v:
            return
        self.seen[eng][k] = v
        sem = self._sem(k)
        self.q[eng].append(lambda e, sem=sem, v=v: e.wait_ge(sem, v))

    def _deps(self, eng, reads, writes):
        need = {}

        def add(k, v):
            if need.get(k, 0) < v:
                need[k] = v
        for b in reads:
            if b.w is not None:
                add(*b.w)
        for b in writes:
            if b.w is not None:
                add(*b.w)
            for k, v in b.r.items():
                add(k, v)
        for k, v in need.items():
            if k == eng and eng == "pe":
                continue
            self._wait(eng, k, v)

    def _mark(self, ev, reads, writes):
        k, v = ev
        for b in reads:
            b.r[k] = v
        for b in writes:
            b.w = ev
            b.r = {}

    def op(self, eng, fns, reads=(), writes=()):
        if callable(fns):
            fns = [fns]
        self._deps(eng, reads, writes)
        self.cnt[eng] += 1
        sem = self.esem[eng]
        for f in fns[:-1]:
            self.q[eng].append(f)
        last = fns[-1]
        self.q[eng].append(lambda e, f=last, sem=sem: f(e).then_inc(sem, 1))
        self._mark((eng, self.cnt[eng]), reads, writes)

    def dma(self, eng, fn, reads=(), writes=(), chan=None):
        if chan not in self.dsem:
            self.dsem[chan] = self.es.enter_context(self.nc.semaphore("d_" + chan))
            self.dcnt[chan] = 0
        self._deps(eng, reads, writes)
        self.dcnt[chan] += 16
        sem = self.dsem[chan]
        self.q[eng].append(lambda e, f=fn, sem=sem: f(e).then_inc(sem, 16))
        self._mark((chan, self.dcnt[chan]), reads, writes)

    def barrier(self):
        for e in ENGS:
            for k in ENGS:
                if k != e and self.cnt[k] > 0:
                    self._wait(e, k, self.cnt[k])
            for k in list(self.dcnt):
                if self.dcnt[k] > 0:
                    self._wait(e, k, self.dcnt[k])

    def finish(self):
        self.barrier()
        nc = self.nc
        with nc.Block() as block:
            @block.sync
            def _(e):
                for f in self.q["sp"]:
                    f(e)

            @block.tensor
            def _(e):
                for f in self.q["pe"]:
                    f(e)

            @block.scalar
            def _(e):
                for f in self.q["act"]:
                    f(e)

            @block.vector
            def _(e):
                for f in self.q["dve"]:
                    f(e)

            @block.gpsimd
            def _(e):
                for f in self.q["pool"]:
                    f(e)
        self.es.close()
        return nc


class Arena:
    def __init__(self, t, words):
        self.t = t
        self.words = words
        self.off = 0

    def reset(self):
        self.off = 0

    def get(self, shape, dt=F32, parts=128):
        n = 1
        for s in shape:
            n *= s
        w = n if dt == F32 or dt == I32 else (n + 1) // 2
        w = (w + 1) // 2 * 2
        assert self.off + w <= self.words, ("arena overflow", self.off, w, self.words)
        v = self.t[0:parts, self.off:self.off + w]
        self.off += w
        if dt != F32:
            v = v.bitcast(dt)
        v = v[:, 0:n]
        if len(shape) == 2:
            return v.rearrange("p (a b) -> p a b", a=shape[0])
        if len(shape) == 3:
            return v.rearrange("p (a b c) -> p a b c", a=shape[0], b=shape[1])
        if len(shape) == 4:
            return v.rearrange("p (a b c d) -> p a b c d", a=shape[0], b=shape[1], c=shape[2])
        return v


NBIG = -400000.0
SLOPES = [2.0 ** -(h + 1) for h in range(8)]


def host_consts():
    f32 = np.float32
    sl = np.array(SLOPES, dtype=np.float64)
    C32 = {}
    CB = {}
    C32["ident"] = np.eye(128)
    kend = 16 * np.arange(128) + 31
    abc = np.full((128, 16, 8), -30000.0)
    maskc = np.zeros((128, 16, 128))
    for qt in range(16):
        vis = (kend <= 128 * qt + 127) & (np.arange(128) < 127)
        for h in range(8):
            abc[:, qt, h] = np.where(vis, sl[h] * (kend - 128 * qt), -30000.0)
        maskc[:, qt, :] = (kend[:, None] <= 128 * qt + np.arange(128)[None, :]) & (np.arange(128) < 127)[:, None]
    C32["abc"] = abc.reshape(128, 128)
    a1 = np.zeros((128, 16, 32))
    a2 = np.zeros((128, 16, 32))
    for qt in range(16):
        q = 128 * qt + np.arange(128)[:, None]
        j = np.arange(32)[None, :]
        vis = (j * 64 <= q)
        cur = q // 64
        forced = (j == 0) | ((cur - j >= 0) & (cur - j < 2))
        a1[:, qt, :] = (vis & ~forced)
        a2[:, qt, :] = np.where(vis, np.where(forced, 1e4, 0.0), -1e4)
    C32["a1"] = a1.reshape(128, 512)
    C32["a2"] = a2.reshape(128, 512)
    ab2 = np.zeros((128, 16, 8))
    for rel in range(-15, 1):
        for h in range(8):
            ab2[:, rel + 15, h] = sl[h] * (np.arange(128) + 128 * rel)
    C32["ab2"] = ab2.reshape(128, 128)
    ii = np.arange(128)
    C32["ut"] = (ii[:, None] <= ii[None, :]).astype(f32)
    C32["ones"] = np.ones((128, 128))
    C32["e4"] = np.tile(np.eye(4).reshape(1, 16), (128, 1))
    C32["mskl"] = np.tile(np.where(ii[:, None] <= ii[None, :], 30000.0, 0.0), (1, 4))
    C32["msklt"] = np.tile(np.where(ii[:, None] > ii[None, :], -30000.0, 0.0), (1, 4))
    CB["maskc"] = maskc.reshape(128, 2048)
    cs = np.arange(127) * 16
    ss = np.arange(32) * 64
    ov = np.minimum(cs[:, None] + 32, ss[None] + 64) - np.maximum(cs[:, None], ss[None])
    sm = np.zeros((128, 32))
    sm[:127] = np.clip(ov, 0, None) / 32.0
    CB["selmap"] = sm
    ex = np.zeros((128, 2048))
    ex[:32] = (np.arange(2048)[None, :] // 64 == np.arange(32)[:, None])
    CB["expd"] = ex
    k = np.arange(128)[:, None]
    i = np.arange(128)[None, :]
    CB["caus"] = np.tile(np.where(k > i, NBIG, 0.0), (1, 4))
    CB["winm"] = np.tile(np.where(k < i, NBIG, 0.0), (1, 4))
    CB["identb"] = np.eye(128)
    CB["onesb"] = np.ones((128, 2))
    o32, ob = {}, {}
    off = 0
    for kname, v in C32.items():
        o32[kname] = (off, v.shape[1])
        off += v.shape[1]
    n32 = off
    off = 0
    for kname, v in CB.items():
        ob[kname] = (off, v.shape[1])
        off += v.shape[1]
    nb = off
    c32 = np.concatenate([v for v in C32.values()], axis=1).astype(f32)
    cb = np.concatenate([v for v in CB.values()], axis=1).astype(f32)
    return c32, cb, o32, ob, n32, nb


def host_consts_s():
    f32 = np.float32
    sl = np.array(SLOPES, dtype=np.float64)
    C = {}
    abs_ = np.full((128, 8, 8), -30000.0)
    for it in range(8):
        n = 128 * it + np.arange(128)
        kend = 16 * n + 31
        for h in range(8):
            abs_[:, it, h] = np.where(n <= 1022, -sl[h] * (16384 - kend), -30000.0)
    C["abs"] = abs_.reshape(128, 64)
    pp = np.arange(128)[:, None]
    tl = np.arange(128)[None, :]
    C["distd"] = (16384 - (512 * (tl // 4) + 4 * pp + (tl % 4))).astype(np.float64)
    C["distw"] = (512 - (128 * np.arange(4)[None, :] + pp)).astype(np.float64)
    rowc = np.zeros((128, 514 + 1024))
    j = np.arange(257)
    forced = (j == 0) | (j == 255) | (j == 256)
    rowc[:, 0:257] = 1.0 - forced
    rowc[:, 257:514] = 1e4 * forced
    for c_ in range(8):
        rowc[:, 514 + 128 * c_:514 + 128 * (c_ + 1)] = (np.arange(128) // 16 == c_)
    C["rowc"] = rowc
    C["iotap"] = (np.arange(128) % 32).astype(np.float64)[:, None]
    off = 0
    o = {}
    for kn, v in C.items():
        o[kn] = (off, v.shape[1])
        off += v.shape[1]
    cs32 = np.concatenate(list(C.values()), axis=1).astype(f32)
    n = np.arange(1024)
    cs_ = 16 * n
    ss = 64 * np.arange(257)
    ov = np.minimum(cs_[:, None] + 32, ss[None] + 64) - np.maximum(cs_[:, None], ss[None])
    sm = np.clip(ov, 0, None) / 32.0
    sm[1023] = 0.0
    sels = sm.reshape(8, 128, 257).transpose(1, 0, 2).reshape(128, 8 * 257).astype(f32)
    return cs32, sels, o, off


_HS = host_consts_s()

_HC = host_consts()
DEBUG = False
PHASES = {"A", "NSA", "GDN", "SAMPLE", "F"}


def build():
    kb = KB()
    nc = kb.nc
    c32_h, cb_h, o32, ob, n32, nb = _HC
    cs32_h, sels_h, os32, ns32 = _HS
    mult, add, sub = ALU.mult, ALU.add, ALU.subtract

    def inp(name, shape, dt=F32):
        return nc.dram_tensor(name, list(shape), dt, kind="ExternalInput")

    def outp(name, shape, dt=F32):
        return nc.dram_tensor(name, list(shape), dt, kind="ExternalOutput")

    x_p = inp("x_p", [2 * T, DM])
    x_s = inp("x_s", [4, DM])
    w_in = inp("w_in", [DM, INW])
    norm_w = inp("norm_w", [1, DM])
    c32_d = inp("c32", [128, n32])
    cb_d = inp("cbs", [128, nb])
    qnw_d = inp("q_norm_w", [1, 128])
    knwc_d = inp("k_norm_cmp_w", [1, 128])
    knws_d = inp("k_norm_slc_w", [1, 128])
    knww_d = inp("k_norm_win_w", [1, 128])
    w1k_d = inp("cmp_w1_k", [32, 128, 128])
    w1v_d = inp("cmp_w1_v", [32, 128, 128])
    b1k_d = inp("cmp_b1_k", [1, 128])
    b1v_d = inp("cmp_b1_v", [1, 128])
    w2k_d = inp("cmp_w2_k", [128, 128])
    w2v_d = inp("cmp_w2_v", [128, 128])
    convw_d = inp("conv_w", [1, 4 * 3072])
    alog_d = inp("a_log", [1, 8])
    dtb_d = inp("dt_bias", [1, 8])
    gnw_d = inp("gdn_norm_w", [1, 128])
    wpa_d = inp("w_pa", [1024, 2048])
    wpb_d = inp("w_pb", [1024, 2048])
    wo_d = inp("w_o", [2048, 2048])
    ccmp_d = inp("ccmp", [5120 * 32, 2048])
    cslc_d = inp("cslc", [5120 * 32, 2048])
    stwin_d = inp("stwin", [4 * 512, 512])
    pt_d = inp("ptab", [4, 128], I32)
    stconv_d = inp("stconv", [4, 3, 3072])
    S0_d = inp("S0", [4, 8, 128, 128])
    cs32_d = inp("cs32", [128, ns32])
    csb_d = inp("csb", [128, 2056])

    y_p = outp("y_p", [2 * T, DM])
    cmp_p = outp("cmp_p", [2 * T, 512])
    slc_p = outp("slc_p", [2 * T, 512])
    win_p = outp("win_p", [2 * 512, 512])
    conv_p = outp("conv_p", [2 * 3, 3072])
    S_p = outp("S_p", [2, 8, 128, 128])
    y_s = outp("y_s", [4, DM])
    cmp_s = outp("cmp_s", [4, 512])
    slc_s = outp("slc_s", [4, 512])
    win_s = outp("win_s", [4 * 512, 512])
    conv_s = outp("conv_s", [4, 3, 3072])
    S_s = outp("S_s", [4, 8, 128, 128])
    NK = nc.dram_tensor("NKs", [4, 1536], F32, kind="Internal")
    USC = nc.dram_tensor("USCs", [4, 3, 8, 128], F32, kind="Internal")
    H = nc.dram_tensor("Hs", [NTOK, INW], F32, kind="Internal")
    OA = nc.dram_tensor("OAs", [NTOK, 1024], F32, kind="ExternalOutput" if DEBUG else "Internal")
    OB = nc.dram_tensor("OBs", [NTOK, 1024], F32, kind="ExternalOutput" if DEBUG else "Internal")

    def cp(eng, out, in_, r, w):
        if eng == "act":
            kb.op("act", lambda e: e.copy(out=out, in_=in_), r, w)
        else:
            kb.op(eng, lambda e: e.tensor_copy(out=out, in_=in_), r, w)

    def tt(eng, out, a, b, op, r, w):
        kb.op(eng, lambda e: e.tensor_tensor(out=out, in0=a, in1=b, op=op), r, w)

    def ts(eng, out, a, s1, s2, op0, op1, r, w):
        if s2 is None:
            kb.op(eng, lambda e: e.tensor_scalar(out=out, in0=a, scalar1=s1, scalar2=None, op0=op0), r, w)
        else:
            kb.op(eng, lambda e: e.tensor_scalar(out=out, in0=a, scalar1=s1, scalar2=s2, op0=op0, op1=op1), r, w)

    def stt(eng, out, a, s, b, op0, op1, r, w):
        kb.op(eng, lambda e: e.scalar_tensor_tensor(out=out, in0=a, scalar=s, in1=b, op0=op0, op1=op1), r, w)

    def actf(out, in_, func, r, w, bias=None, scale=1.0):
        if bias is None:
            kb.op("act", lambda e: e.activation(out=out, in_=in_, func=func, scale=scale), r, w)
        else:
            kb.op("act", lambda e: e.activation(out=out, in_=in_, func=func, bias=bias, scale=scale), r, w)

    def red(out, in_, r, w, op=ALU.add):
        kb.op("dve", lambda e: e.tensor_reduce(out=out, in_=in_, axis=AX.X, op=op), r, w)

    def recip(out, in_, r, w):
        kb.op("dve", lambda e: e.reciprocal(out=out, in_=in_), r, w)

    def rsq(ss, tmp, out, c, B):
        ts("dve", tmp, ss, c, EPS, mult, add, [B], [B])
        actf(tmp, tmp, AF.Sqrt, [B], [B])
        recip(out, tmp, [B], [B])

    def mm(out, lhsT, rhs, start=True, stop=True):
        return lambda e: e.matmul(out, lhsT=lhsT, rhs=rhs, start=start, stop=stop)

    def dma(eng, out, in_, r, w, chan):
        kb.dma(eng, lambda e: e.dma_start(out=out, in_=in_), r, w, chan)

    Bc = kb.buf("consts")
    c32 = kb.sb("c32s", [128, n32])
    cbt = kb.sb("cbts", [128, nb], BF16)
    dma("sp", c32[:], c32_d.ap(), [], [Bc], "c0")

    def C(name):
        o, n = o32[name]
        return c32[:, o:o + n]

    def CBv(name):
        o, n = ob[name]
        return cbt[:, o:o + n]
    ident = C("ident")
    identb = CBv("identb")
    qnw = kb.sb("qnw", [128, 1])
    b1k = kb.sb("b1k", [128, 1])
    b1v = kb.sb("b1v", [128, 1])
    knwc = kb.sb("knwc", [128, 128])
    knws = kb.sb("knws", [128, 128])
    knww = kb.sb("knww", [128, 128])
    w1k = kb.sb("w1k", [128, 32, 128], BF16)
    w1v = kb.sb("w1v", [128, 32, 128], BF16)
    w2k = kb.sb("w2k", [128, 128], BF16)
    w2v = kb.sb("w2v", [128, 128], BF16)
    dma("sp", qnw[:], qnw_d.ap().rearrange("o d -> d o"), [], [Bc], "c0")
    dma("sp", b1k[:], b1k_d.ap().rearrange("o d -> d o"), [], [Bc], "c0")
    dma("sp", b1v[:], b1v_d.ap().rearrange("o d -> d o"), [], [Bc], "c0")
    dma("sp", knwc[:], knwc_d.ap().partition_broadcast(128), [], [Bc], "c0")
    dma("sp", knws[:], knws_d.ap().partition_broadcast(128), [], [Bc], "c0")
    dma("sp", knww[:], knww_d.ap().partition_broadcast(128), [], [Bc], "c0")
    gnw = kb.sb("gnw", [128, 128])
    nea = kb.sb("nea", [128, 8])
    dtb = kb.sb("dtb", [128, 8])
    dma("sp", gnw[:], gnw_d.ap().partition_broadcast(128), [], [Bc], "c0")
    dma("sp", nea[:], alog_d.ap().partition_broadcast(128), [], [Bc], "c0")
    dma("sp", dtb[:], dtb_d.ap().partition_broadcast(128), [], [Bc], "c0")
    kb.op("act", lambda e: e.activation(out=nea[:], in_=nea[:], func=AF.Exp), [Bc], [Bc])
    kb.op("dve", lambda e: e.tensor_scalar(out=nea[:], in0=nea[:], scalar1=-1.0, scalar2=None, op0=ALU.mult), [Bc], [Bc])

    ARW = 43008
    arena_t = kb.sb("arena", [128, ARW])
    ar = Arena(arena_t, ARW)
    PS = [kb.ps("ps%d" % i, [128, 512]) for i in range(8)]
    PB = kb.bufs(8)

    ar.reset()
    stg = ar.get([nb])
    Bstg = kb.buf()
    dma("sp", stg, cb_d.ap(), [], [Bstg], "c1")
    cp("dve", cbt[:], stg, [Bstg], [Bc])
    stw = ar.get([32, 128])
    for (wd, wt_) in ((w1k_d, w1k), (w1v_d, w1v)):
        dma("sp", stw, wd.ap().rearrange("s d e -> d s e"), [], [Bstg], "c1")
        cp("dve", wt_[:], stw, [Bstg], [Bc])
    for (wd, wt_) in ((w2k_d, w2k), (w2v_d, w2v)):
        dma("sp", stw[:, 0, :], wd.ap(), [], [Bstg], "c1")
        cp("dve", wt_[:], stw[:, 0, :], [Bstg], [Bc])
    kb.barrier()

    B_H = kb.buf("H")
    B_out = kb.buf("outs")
    B_OA = kb.buf("OA")
    B_OB = kb.buf("OB")

    def phase_A(seq):
        ar.reset()
        ntile = NT + (1 if seq == 0 else 0)
        ncol = ntile * 128
        xnT = ar.get([16, ncol], BF16)
        B_xnT = kb.buf()
        nwb = ar.get([DM])
        B_nwb = kb.buf()
        dma("sp", nwb, norm_w.ap().partition_broadcast(128), [], [B_nwb], "c0")
        xt = [ar.get([DM]) for _ in range(2)]
        B_xt = kb.bufs(2)
        junk = ar.get([DM], BF16)
        B_junk = kb.buf()
        st = [ar.get([4]) for _ in range(2)]
        B_st = kb.bufs(2)
        diag = [ar.get([128]) for _ in range(2)]
        B_dg = kb.bufs(2)
        wst = [ar.get([4, 512]) for _ in range(2)]
        B_wst = kb.bufs(2)
        wb = [ar.get([16, 512], BF16) for _ in range(2)]
        B_wb = kb.bufs(2)
        ho = [ar.get([512]) for _ in range(4)]
        B_ho = kb.bufs(4)
        for tti in range(ntile):
            i = tti % 2
            n_ = 128 if tti < NT else 4
            src = x_p.ap()[seq * T + tti * 128: seq * T + tti * 128 + 128, :] if tti < NT else x_s.ap()
            dma("sp", xt[i][0:n_, :], src, [], [B_xt[i]], "xt%d" % i)
            kb.op("dve", lambda e, i=i: e.memset(st[i][:, 0:1], 0.0), [], [B_st[i]])
            kb.op("act", lambda e, i=i, n_=n_: e.activation(out=junk[0:n_, :], in_=xt[i][0:n_, :], func=AF.Square,
                                                             accum_out=st[i][0:n_, 0:1]),
                  [B_xt[i], B_st[i]], [B_junk, B_st[i]])
            rsq(st[i][0:n_, 0:1], st[i][0:n_, 1:2], st[i][0:n_, 3:4], 1.0 / DM, B_st[i])
            ts("dve", diag[i][0:n_, 0:n_], ident[0:n_, 0:n_], st[i][0:n_, 3:4], None, mult, None, [B_st[i], Bc], [B_dg[i]])
            tt("pool", xt[i][0:n_, :], xt[i][0:n_, :], nwb[0:n_, :], mult, [B_xt[i], B_nwb], [B_xt[i]])
            for j in range(4):
                pj = j
                kb.op("pe", [mm(PS[pj][:, k * n_:(k + 1) * n_], xt[i][0:n_, (4 * j + k) * 128:(4 * j + k + 1) * 128],
                                diag[i][0:n_, 0:n_]) for k in range(4)],
                      [B_xt[i], B_dg[i]], [PB[pj]])
                dst = xnT[:, 4 * j:4 * j + 4, tti * 128:tti * 128 + n_]
                srcp = PS[pj][:, 0:4 * n_].rearrange("p (a b) -> p a b", a=4)
                cp("act" if j % 2 == 0 else "dve", dst, srcp, [PB[pj]], [B_xnT])
        ncb = (INW + 511) // 512
        hcnt = 0
        for cb in range(ncb):
            c0 = cb * 512
            cw = min(512, INW - c0)
            wi = cb % 2
            for qf in range(4):
                hf = qf % 2
                srcw = w_in.ap()[qf * 512:(qf + 1) * 512, c0:c0 + cw].rearrange("(kc p) n -> p kc n", p=128)
                dma("sp" if hf == 0 else "act", wst[hf][:, :, 0:cw], srcw, [], [B_wst[hf]], "wst%d" % hf)
                cp("dve" if hf == 0 else "pool", wb[wi][:, qf * 4:(qf + 1) * 4, 0:cw], wst[hf][:, :, 0:cw], [B_wst[hf]], [B_wb[wi]])
            for tti in range(ntile):
                n_ = 128 if tti < NT else 4
                pj = 4 + (hcnt % 4)
                hi = hcnt % 4
                hcnt += 1
                kb.op("pe", [mm(PS[pj][0:n_, 0:cw], xnT[:, kc, tti * 128:tti * 128 + n_], wb[wi][:, kc, 0:cw],
                                start=(kc == 0), stop=(kc == 15)) for kc in range(16)],
                      [B_xnT, B_wb[wi]], [PB[pj]])
                cp("act" if hcnt % 2 == 0 else "dve", ho[hi][0:n_, 0:cw], PS[pj][0:n_, 0:cw], [PB[pj]], [B_ho[hi]])
                r0 = seq * T + tti * 128 if tti < NT else 2 * T
                dma("sp", H.ap()[r0:r0 + n_, c0:c0 + cw], ho[hi][0:n_, 0:cw], [B_ho[hi]], [B_H], "ho%d" % hi)

    def phase_NSA(seq):
        ar.reset()
        QT = ar.get([8, T], BF16)
        CT = ar.get([4, T], BF16)
        KsT = ar.get([2, T], BF16)
        KwT = ar.get([2, T], BF16)
        VS = ar.get([NT, 2, 128], BF16)
        VW = ar.get([NT, 2, 128], BF16)
        GS = ar.get([NT, 24])
        B_QT = kb.bufs(NT)
        B_CT = kb.buf()
        B_KV = kb.bufs(NT)
        hq = [ar.get([2584]) for _ in range(2)]
        B_hq = kb.bufs(2)
        sqb = ar.get([1536])
        B_sq = kb.buf()
        stt_ = [ar.get([40]) for _ in range(2)]
        B_stt = kb.bufs(2)
        KcT = ar.get([2, 128], BF16)
        VCX = ar.get([2, 162], BF16)
        B_KC = kb.buf()
        for t_ in range(NT):
            i = t_ % 2
            r0 = seq * T + t_ * 128
            c_ = slice(t_ * 128, (t_ + 1) * 128)
            h = hq[i]
            Bh = B_hq[i]
            S = stt_[i]
            Bs = B_stt[i]
            dma("sp", h, H.ap()[r0:r0 + 128, 0:2584], [B_H], [Bh], "hq%d" % i)
            dma("sp", cmp_p.ap()[r0:r0 + 128, :], h[:, O_KVC:O_KVC + 512], [Bh], [B_out], "o%d" % i)
            tt("dve", sqb[:, 0:1024], h[:, 0:1024], h[:, 0:1024], mult, [Bh], [B_sq])
            red(S[:, 0:8], sqb[:, 0:1024].rearrange("p (a d) -> p a d", a=8), [B_sq], [Bs])
            tt("pool", sqb[:, 1024:1280], h[:, O_KVS:O_KVS + 256], h[:, O_KVS:O_KVS + 256], mult, [Bh], [B_sq])
            tt("pool", sqb[:, 1280:1536], h[:, O_KVW:O_KVW + 256], h[:, O_KVW:O_KVW + 256], mult, [Bh], [B_sq])
            red(S[:, 8:12], sqb[:, 1024:1536].rearrange("p (a d) -> p a d", a=4), [B_sq], [Bs])
            rsq(S[:, 0:12], S[:, 12:24], S[:, 24:36], 1.0 / 128, Bs)
            tt("dve", h[:, 0:1024].rearrange("p (a d) -> p a d", a=8), h[:, 0:1024].rearrange("p (a d) -> p a d", a=8),
               S[:, 24:32].unsqueeze(2).to_broadcast([128, 8, 128]), mult, [Bh, Bs], [Bh])
            for g in range(2):
                stt("dve", h[:, O_KVS + 128 * g:O_KVS + 128 * g + 128], h[:, O_KVS + 128 * g:O_KVS + 128 * g + 128],
                    S[:, 32 + g:33 + g], knws[:], mult, mult, [Bh, Bs, Bc], [Bh])
                stt("dve", h[:, O_KVW + 128 * g:O_KVW + 128 * g + 128], h[:, O_KVW + 128 * g:O_KVW + 128 * g + 128],
                    S[:, 34 + g:35 + g], knww[:], mult, mult, [Bh, Bs, Bc], [Bh])
            dma("sp", slc_p.ap()[r0:r0 + 128, :], h[:, O_KVS:O_KVS + 512], [Bh], [B_out], "o%d" % i)
            if t_ >= NT - 4:
                rw = seq * 512 + (t_ - (NT - 4)) * 128
                dma("sp", win_p.ap()[rw:rw + 128, :], h[:, O_KVW:O_KVW + 512], [Bh], [B_out], "o%d" % i)
            for j in range(2):
                kb.op("pe", [mm(PS[j][:, k * 128:(k + 1) * 128], h[:, (4 * j + k) * 128:(4 * j + k + 1) * 128], ident) for k in range(4)],
                      [Bh, Bc], [PB[j]])
                ts("dve", QT[:, 4 * j:4 * j + 4, c_], PS[j][:].rearrange("p (a b) -> p a b", a=4), qnw[:, 0:1], None, mult, None,
                   [PB[j], Bc], [B_QT[t_]])
            kb.op("pe", [mm(PS[2][:, k * 128:(k + 1) * 128], h[:, O_KVC + k * 128:O_KVC + (k + 1) * 128], ident) for k in range(4)],
                  [Bh, Bc], [PB[2]])
            cp("act", CT[:, 0:4, c_], PS[2][:].rearrange("p (a b) -> p a b", a=4), [PB[2]], [B_CT])
            kb.op("pe", [mm(PS[3][:, k * 128:(k + 1) * 128], h[:, (O_KVS if k < 2 else O_KVW) + (k % 2) * 128:(O_KVS if k < 2 else O_KVW) + (k % 2) * 128 + 128], ident)
                         for k in range(4)], [Bh, Bc], [PB[3]])
            cp("act", KsT[:, 0:2, c_], PS[3][:, 0:256].rearrange("p (a b) -> p a b", a=2), [PB[3]], [B_KV[t_]])
            cp("act", KwT[:, 0:2, c_], PS[3][:, 256:512].rearrange("p (a b) -> p a b", a=2), [PB[3]], [B_KV[t_]])
            cp("pool", VS[:, t_, :, :], h[:, O_KVS + 256:O_KVS + 512].rearrange("p (a b) -> p a b", a=2), [Bh], [B_KV[t_]])
            cp("pool", VW[:, t_, :, :], h[:, O_KVW + 256:O_KVW + 512].rearrange("p (a b) -> p a b", a=2), [Bh], [B_KV[t_]])
            actf(GS[:, t_, :], h[:, O_GN:O_GN + 24], AF.Sigmoid, [Bh], [B_KV[t_]])
        hs = ar.get([127], BF16)
        B_hs = kb.buf()
        kcw = ar.get([128])
        B_kcw = kb.buf()
        cst = ar.get([8])
        B_cst = kb.buf()
        for kvg in range(4):
            isk = kvg < 2
            g = kvg % 2
            w1 = w1k if isk else w1v
            kb.op("pe", [mm(PS[4][:, 0:127], w1[:, s_, :], CT[:, kvg, s_:s_ + 2017:16], start=(s_ == 0), stop=(s_ == 31)) for s_ in range(32)],
                  [B_CT, Bc], [PB[4]])
            actf(hs[:, :], PS[4][:, 0:127], AF.Silu, [PB[4], Bc], [B_hs], bias=(b1k if isk else b1v)[:, 0:1])
            kb.op("pe", mm(PS[5][0:127, 0:128], hs[:, :], (w2k if isk else w2v)[:]), [B_hs, Bc], [PB[5]])
            if isk:
                cp("act", kcw[0:127, :], PS[5][0:127, 0:128], [PB[5]], [B_kcw])
                tt("dve", sqb[0:127, 0:128], kcw[0:127, :], kcw[0:127, :], mult, [B_kcw], [B_sq])
                red(cst[0:127, 0:1], sqb[0:127, 0:128], [B_sq], [B_cst])
                rsq(cst[0:127, 0:1], cst[0:127, 1:2], cst[0:127, 2:3], 1.0 / 128, B_cst)
                stt("dve", kcw[0:127, :], kcw[0:127, :], cst[0:127, 2:3], knwc[0:127, :], mult, mult, [B_kcw, B_cst, Bc], [B_kcw])
                kb.op("pe", mm(PS[6][:, 0:127], kcw[0:127, :], ident[0:127, 0:127]), [B_kcw, Bc], [PB[6]])
                cp("act", KcT[:, g, 0:127], PS[6][:, 0:127], [PB[6]], [B_KC])
            else:
                cp("act", VCX[0:127, g, 0:128], PS[5][0:127, 0:128], [PB[5]], [B_KC])
                cp("dve", VCX[0:127, g, 128:160], CBv("selmap")[0:127, :], [Bc], [B_KC])
                cp("dve", VCX[0:127, g, 160:162], CBv("onesb")[0:127, :], [Bc], [B_KC])
        oacc = ar.get([8, 128])
        B_oacc = kb.buf()
        ec = ar.get([4, 128], BF16)
        B_ec = kb.buf()
        es = [ar.get([NT, 4, 128], BF16) for _ in range(2)]
        B_es = kb.bufs(2)
        ew = ar.get([5, 4, 128], BF16)
        B_ew = kb.buf()
        NS = ar.get([2, 4, 128], BF16)
        B_NS = kb.buf()
        kb.op("pool", lambda e: e.memset(NS, 0.0), [], [B_NS])
        za = ar.get([1024])
        B_za = kb.buf()
        sm = ar.get([256])
        B_sm = kb.buf()
        imp = sm[:, 0:64].rearrange("p (a b) -> p a b", a=2)
        sc = sm[:, 64:128].rearrange("p (a b) -> p a b", a=2)
        sc2 = sm[:, 128:160]
        m8 = sm[:, 160:176]
        selv = sm[:, 176:208]
        rd = sm[:, 208:216]
        fac = sm[:, 216:224]
        abc = C("abc").rearrange("p (a b) -> p a b", a=16)
        ab2 = C("ab2").rearrange("p (a b) -> p a b", a=16)
        a1 = C("a1").rearrange("p (a b) -> p a b", a=16)
        a2 = C("a2").rearrange("p (a b) -> p a b", a=16)
        maskc = CBv("maskc").rearrange("p (a b) -> p a b", a=16)
        expd = CBv("expd")
        caus = CBv("caus").rearrange("p (a b) -> p a b", a=4)
        winm = CBv("winm").rearrange("p (a b) -> p a b", a=4)
        onesb = CBv("onesb")
        for qt in range(NT):
            q_ = slice(qt * 128, (qt + 1) * 128)
            r0 = seq * T + qt * 128
            dma("sp", za, H.ap()[r0:r0 + 128, O_ZA:O_ZA + 1024], [B_H], [B_za], "za")
            actf(za, za, AF.Silu, [B_za], [B_za])
            for g in range(2):
                kb.op("pe", mm(PS[0][0:127, :], KcT[:, g, 0:127], QT[:, 4 * g:4 * g + 4, q_]), [B_KC, B_QT[qt]], [PB[0]])
                for r_ in range(4):
                    actf(ec[0:127, r_, :], PS[0][0:127, r_ * 128:(r_ + 1) * 128], AF.Exp, [PB[0], Bc], [B_ec],
                         bias=abc[0:127, qt, 4 * g + r_:4 * g + r_ + 1], scale=SCALE)
                tt("pool", ec[0:127, :, :], ec[0:127, :, :], maskc[0:127, qt:qt + 1, :].to_broadcast([127, 4, 128]), mult, [B_ec, Bc], [B_ec])
                for half in range(2):
                    pj = 1 + half
                    kb.op("pe", [mm(PS[pj][:, k * 162:k * 162 + 162], ec[0:127, 2 * half + k, :], VCX[0:127, g, :]) for k in range(2)],
                          [B_ec, B_KC], [PB[pj]])
                    for k in range(2):
                        r_ = 2 * half + k
                        hh = 4 * g + r_
                        u = PS[pj][:, k * 162:k * 162 + 162]
                        ts("dve", rd[:, 0:1], u[:, 160:161], 1e-30, None, ALU.max, None, [PB[pj]], [B_sm])
                        recip(rd[:, 0:1], rd[:, 0:1], [B_sm], [B_sm])
                        tt("dve", fac[:, 0:1], rd[:, 0:1], GS[:, qt, hh:hh + 1], mult, [B_sm, B_KV[qt]], [B_sm])
                        ts("dve", oacc[:, hh, :], u[:, 0:128], fac[:, 0:1], None, mult, None, [PB[pj], B_sm], [B_oacc])
                        if r_ == 0:
                            ts("dve", imp[:, g, :], u[:, 128:160], rd[:, 0:1], None, mult, None, [PB[pj], B_sm], [B_sm])
                        else:
                            stt("dve", imp[:, g, :], u[:, 128:160], rd[:, 0:1], imp[:, g, :], mult, add, [PB[pj], B_sm], [B_sm])
            for g in range(2):
                tt("dve", sc[:, g, :], imp[:, g, :], a1[:, qt, :], mult, [B_sm, Bc], [B_sm])
                tt("dve", sc[:, g, :], sc[:, g, :], a2[:, qt, :], add, [B_sm, Bc], [B_sm])
                kb.op("dve", lambda e, g=g: e.max(out=m8[:, 0:8], in_=sc[:, g, :]), [B_sm], [B_sm])
                kb.op("dve", lambda e, g=g: e.match_replace(out=sc2, in_to_replace=m8[:, 0:8], in_values=sc[:, g, :], imm_value=-30000.0),
                      [B_sm], [B_sm])
                kb.op("dve", lambda e: e.max(out=m8[:, 8:16], in_=sc2), [B_sm], [B_sm])
                ts("dve", selv, sc[:, g, :], m8[:, 15:16], None, ALU.is_ge, None, [B_sm], [B_sm])
                ts("dve", selv, selv, -1.0, -NBIG, add, mult, [B_sm], [B_sm])
                kb.op("pe", mm(PS[3][0:32, 0:128], selv, ident), [B_sm, Bc], [PB[3]])
                cp("act", NS[0:32, g, :, :], PS[3][0:32, 0:128].unsqueeze(1).to_broadcast([32, 4, 128]), [PB[3]], [B_NS])
            for g in range(2):
                E = es[g]
                Be = B_es[g]
                for kt in range(qt + 1):
                    pj = 4 + (kt % 2)
                    k_ = slice(kt * 128, (kt + 1) * 128)
                    ops = [mm(PS[pj][:, :], KsT[:, g, k_], QT[:, 4 * g:4 * g + 4, q_], start=True, stop=False),
                           mm(PS[pj][:, :], expd[:, k_], NS[:, g, :, :], start=False, stop=(kt != qt))]
                    if kt == qt:
                        ops.append(mm(PS[pj][:, :], identb, caus, start=False, stop=True))
                    kb.op("pe", ops, [B_KV[kt], B_QT[qt], B_NS, Bc], [PB[pj]])
                    for r_ in range(4):
                        hh = 4 * g + r_
                        actf(E[:, kt, r_, :], PS[pj][:, r_ * 128:(r_ + 1) * 128], AF.Exp, [PB[pj], Bc], [Be],
                             bias=ab2[:, kt - qt + 15, hh:hh + 1], scale=SCALE)
                kb.op("pe", [mm(PS[6][:, r_ * 128:(r_ + 1) * 128], E[:, kt, r_, :], VS[:, kt, g, :], start=(kt == 0), stop=(kt == qt))
                             for r_ in range(4) for kt in range(qt + 1)], [Be] + [B_KV[kt] for kt in range(qt + 1)], [PB[6]])
                kb.op("pe", [mm(PS[7][:, r_ * 2:(r_ + 1) * 2], E[:, kt, r_, :], onesb, start=(kt == 0), stop=(kt == qt))
                             for r_ in range(4) for kt in range(qt + 1)], [Be, Bc], [PB[7]])
                recip(rd[:, 0:4], PS[7][:, 0:8:2], [PB[7]], [B_sm])
                tt("dve", fac[:, 0:4], rd[:, 0:4], GS[:, qt, 8 + 4 * g:12 + 4 * g], mult, [B_sm, B_KV[qt]], [B_sm])
                for r_ in range(4):
                    hh = 4 * g + r_
                    stt("dve", oacc[:, hh, :], PS[6][:, r_ * 128:(r_ + 1) * 128], fac[:, r_:r_ + 1], oacc[:, hh, :], mult, add,
                        [PB[6], B_sm, B_oacc], [B_oacc])
                kts = list(range(max(0, qt - 4), qt + 1))
                for ki, kt in enumerate(kts):
                    pj = 4 + (ki % 2)
                    k_ = slice(kt * 128, (kt + 1) * 128)
                    ops = [mm(PS[pj][:, :], KwT[:, g, k_], QT[:, 4 * g:4 * g + 4, q_], start=True, stop=(kt != qt and kt != qt - 4))]
                    if kt == qt:
                        ops.append(mm(PS[pj][:, :], identb, caus, start=False, stop=True))
                    if kt == qt - 4:
                        ops.append(mm(PS[pj][:, :], identb, winm, start=False, stop=True))
                    kb.op("pe", ops, [B_KV[kt], B_QT[qt], Bc], [PB[pj]])
                    for r_ in range(4):
                        hh = 4 * g + r_
                        actf(ew[:, ki, r_, :], PS[pj][:, r_ * 128:(r_ + 1) * 128], AF.Exp, [PB[pj], Bc], [B_ew],
                             bias=ab2[:, kt - qt + 15, hh:hh + 1], scale=SCALE)
                nk = len(kts)
                kb.op("pe", [mm(PS[6][:, r_ * 128:(r_ + 1) * 128], ew[:, ki, r_, :], VW[:, kts[ki], g, :], start=(ki == 0), stop=(ki == nk - 1))
                             for r_ in range(4) for ki in range(nk)], [B_ew] + [B_KV[kt] for kt in kts], [PB[6]])
                kb.op("pe", [mm(PS[7][:, r_ * 2:(r_ + 1) * 2], ew[:, ki, r_, :], onesb, start=(ki == 0), stop=(ki == nk - 1))
                             for r_ in range(4) for ki in range(nk)], [B_ew, Bc], [PB[7]])
                recip(rd[:, 0:4], PS[7][:, 0:8:2], [PB[7]], [B_sm])
                tt("dve", fac[:, 0:4], rd[:, 0:4], GS[:, qt, 16 + 4 * g:20 + 4 * g], mult, [B_sm, B_KV[qt]], [B_sm])
                for r_ in range(4):
                    hh = 4 * g + r_
                    stt("dve", oacc[:, hh, :], PS[6][:, r_ * 128:(r_ + 1) * 128], fac[:, r_:r_ + 1], oacc[:, hh, :], mult, add,
                        [PB[6], B_sm, B_oacc], [B_oacc])
            tt("pool", za, za, oacc.rearrange("p a b -> p (a b)"), mult, [B_za, B_oacc], [B_za])
            dma("sp", OA.ap()[r0:r0 + 128, :], za, [B_za], [B_OA], "oa")

    def phase_GDN(seq):
        ar.reset()
        CW = ar.get([4, 3072])
        B_CW = kb.buf()
        dma("sp", CW.rearrange("p a b -> p (a b)"), convw_d.ap().partition_broadcast(128), [], [B_CW], "c0")
        hw = ar.get([4, 512])
        B_hw = kb.bufs(4)
        cqs = [ar.get([8, 128]) for _ in range(2)]
        cks = [ar.get([8, 128]) for _ in range(2)]
        cvs = [ar.get([8, 128]) for _ in range(2)]
        B_cs2 = [kb.bufs(3) for _ in range(2)]
        kT = ar.get([8, 128])
        qT = ar.get([8, 128])
        B_kT = kb.buf()
        B_qT = kb.buf()
        S = ar.get([8, 128])
        B_S = kb.buf()
        L = ar.get([8, 128])
        LT = ar.get([8, 128])
        B_L = kb.buf()
        B_LT = kb.buf()
        Pb = [ar.get([8, 128]) for _ in range(2)]
        Qb = [ar.get([8, 128]) for _ in range(2)]
        B_P = kb.bufs(2)
        B_Q = kb.bufs(2)
        Rm = ar.get([8, 128])
        B_R = kb.buf()
        vb = ar.get([8, 128])
        kbd = ar.get([8, 128])
        kd = ar.get([8, 128])
        B_vb, B_kbd, B_kd = kb.bufs(3)
        nwT = L
        B_nwT = B_L
        vn = LT
        B_vn = B_LT
        attT = ar.get([8, 128])
        B_att = kb.buf()
        qSs = ar.get([8, 128])
        B_qSs = kb.buf()
        zb = ar.get([1024])
        B_zb = kb.buf()
        BD = ar.get([8, 128])
        B_BD = kb.buf()
        decT = ar.get([128])
        B_decT = kb.buf()
        st = ar.get([160])
        B_st = kb.buf()
        ab = st[:, 0:16]
        beta = st[:, 16:24]
        nbeta = st[:, 24:32]
        gg = st[:, 32:40]
        dec = st[:, 40:48]
        ndec = st[:, 48:56]
        edec = st[:, 56:64]
        edl = st[:, 64:72]
        edld = st[:, 72:80]
        bed = st[:, 80:88]
        ss = st[:, 88:112]
        tmp = st[:, 112:136]
        rn = st[:, 136:160]
        ut = C("ut")
        ones = C("ones")
        mskl = C("mskl")
        msklt = C("msklt")
        pc = [0]

        def pair():
            p = pc[0] % 4
            pc[0] += 1
            return (2 * p, 2 * p + 1)

        def v3(x):
            return x.rearrange("p (a b) -> p a b", a=4)

        def bc8(x):
            return x.unsqueeze(2).to_broadcast([128, 8, 128])

        def permm(lhs_fn, rhs_fn, reads, extra=None):
            pa = pair()
            for half in range(2):
                ops = []
                for k in range(4):
                    h = 4 * half + k
                    o_ = PS[pa[half]][:, k * 128:(k + 1) * 128]
                    if extra is None:
                        ops.append(mm(o_, lhs_fn(h), rhs_fn(h)))
                    else:
                        ops.append(mm(o_, lhs_fn(h), rhs_fn(h), start=True, stop=False))
                        ops.append(mm(o_, extra[0](h), extra[1](h), start=False, stop=True))
                kb.op("pe", ops, reads, [PB[pa[half]]])
            return pa

        def evac(pa, dst, Bd, engs=("act", "dve")):
            for half in range(2):
                cp(engs[half], dst[:, 4 * half:4 * half + 4, :], v3(PS[pa[half]][:, :]), [PB[pa[half]]], [Bd])

        kb.op("pool", lambda e: e.memset(S, 0.0), [], [B_S])
        kb.op("pool", lambda e: e.memset(BD, 0.0), [], [B_BD])
        identb3 = identb.unsqueeze(1).to_broadcast([128, 8, 128])

        def conv_part(t_):
            r0 = seq * T + t_ * 128
            cq, ck, cv = cqs[t_ % 2], cks[t_ % 2], cvs[t_ % 2]
            B_c = B_cs2[t_ % 2]
            for grp6 in range(6):
                grp = grp6 // 2
                dst, Bd = ((cq, B_c[0]), (ck, B_c[1]), (cv, B_c[2]))[grp]
                c0 = O_QKV + grp6 * 512
                for i in range(4):
                    sh = 3 - i
                    if t_ == 0 and sh > 0:
                        kb.op("pool", lambda e, i=i, sh=sh: e.memset(hw[0:sh, i, :], 0.0), [], [B_hw[i]])
                        dma("sp", hw[sh:128, i, :], H.ap()[r0:r0 + 128 - sh, c0:c0 + 512], [B_H], [B_hw[i]], "hw%d" % i)
                    else:
                        dma("sp", hw[:, i, :], H.ap()[r0 - sh:r0 - sh + 128, c0:c0 + 512], [B_H], [B_hw[i]], "hw%d" % i)
                    tt("pool", hw[:, i, :], hw[:, i, :], CW[:, i, grp6 * 512:(grp6 + 1) * 512], mult, [B_hw[i], B_CW], [B_hw[i]])
                d2 = dst.rearrange("p a b -> p (a b)")[:, (grp6 % 2) * 512:(grp6 % 2 + 1) * 512]
                tt("dve", d2, hw[:, 0, :], hw[:, 1, :], add, [B_hw[0], B_hw[1]], [Bd])
                tt("dve", d2, d2, hw[:, 2, :], add, [Bd, B_hw[2]], [Bd])
                tt("dve", d2, d2, hw[:, 3, :], add, [Bd, B_hw[3]], [Bd])
                actf(d2, d2, AF.Silu, [Bd], [Bd])

        def chain_part(t_):
            r0 = seq * T + t_ * 128
            cq, ck, cv = cqs[t_ % 2], cks[t_ % 2], cvs[t_ % 2]
            B_c = B_cs2[t_ % 2]
            for n_, (src, Bs_) in enumerate(((cq, B_c[0]), (ck, B_c[1]))):
                tt("pool", zb.rearrange("p (a b) -> p a b", a=8), src, src, mult, [Bs_], [B_zb])
                red(ss[:, 8 * n_:8 * n_ + 8], zb.rearrange("p (a b) -> p a b", a=8), [B_zb], [B_st])
            rsq(ss[:, 0:16], tmp[:, 0:16], rn[:, 0:16], 1.0, B_st)
            ts("dve", rn[:, 0:8], rn[:, 0:8], 128 ** -0.5, None, mult, None, [B_st], [B_st])
            tt("dve", cq, cq, bc8(rn[:, 0:8]), mult, [B_c[0], B_st], [B_c[0]])
            tt("dve", ck, ck, bc8(rn[:, 8:16]), mult, [B_c[1], B_st], [B_c[1]])
            dma("act", ab, H.ap()[r0:r0 + 128, O_A:O_A + 16], [B_H], [B_st], "ab")
            actf(beta, ab[:, 8:16], AF.Sigmoid, [B_st], [B_st])
            ts("dve", nbeta, beta, -1.0, None, mult, None, [B_st], [B_st])
            tt("dve", gg, ab[:, 0:8], dtb[:], add, [B_st, Bc], [B_st])
            actf(gg, gg, AF.Exp, [B_st], [B_st])
            actf(gg, gg, AF.Ln, [B_st], [B_st], bias=1.0)
            tt("dve", gg, gg, nea[:], mult, [B_st, Bc], [B_st])
            kb.op("pe", [mm(PS[6][:, 0:8], ut, gg), mm(PS[6][:, 8:16], ones, gg)], [B_st, Bc], [PB[6]])
            cp("dve", dec, PS[6][:, 0:8], [PB[6]], [B_st])
            ts("dve", ndec, dec, -1.0, None, mult, None, [B_st], [B_st])
            actf(edec, dec, AF.Exp, [B_st], [B_st])
            actf(edl, PS[6][:, 8:16], AF.Exp, [PB[6]], [B_st])
            tt("dve", edld, PS[6][:, 8:16], dec, sub, [PB[6], B_st], [B_st])
            actf(edld, edld, AF.Exp, [B_st], [B_st])
            tt("dve", bed, beta, edec, mult, [B_st], [B_st])
            kb.op("pe", mm(PS[7][0:8, 0:128], dec, ident), [B_st, Bc], [PB[7]])
            cp("dve", decT[0:8, :], PS[7][0:8, 0:128], [PB[7]], [B_decT])
            tt("dve", BD[0:8, :, :], decT[0:8, :].unsqueeze(1).to_broadcast([8, 8, 128]),
               ident[0:8, 0:8].unsqueeze(2).to_broadcast([8, 8, 128]), mult, [B_decT, Bc], [B_BD])
            for (msk, dst, Bd, sc_, bcol) in ((mskl, L, B_L, -1.0, dec), (msklt, LT, B_LT, 1.0, ndec)):
                pa = pair()
                for half in range(2):
                    kb.op("pe", [mm(PS[pa[half]][:, :], ones, BD[:, 4 * half:4 * half + 4, :], start=True, stop=False),
                                 mm(PS[pa[half]][:, :], ident, msk, start=False, stop=True)], [B_BD, Bc], [PB[pa[half]]])
                    for k in range(4):
                        h = 4 * half + k
                        actf(dst[:, h, :], PS[pa[half]][:, k * 128:(k + 1) * 128], AF.Exp, [PB[pa[half]], B_st], [Bd],
                             bias=bcol[:, h:h + 1], scale=sc_)
            pa = permm(lambda h: ck[:, h, :], lambda h: ident, [B_c[1], Bc])
            evac(pa, kT, B_kT)
            pa = permm(lambda h: cq[:, h, :], lambda h: ident, [B_c[0], Bc])
            evac(pa, qT, B_qT)
            pa = permm(lambda h: kT[:, h, :], lambda h: kT[:, h, :], [B_kT])
            X = Qb[0]
            for h in range(8):
                stt("dve", X[:, h, :], PS[pa[h // 4]][:, (h % 4) * 128:(h % 4 + 1) * 128], nbeta[:, h:h + 1], L[:, h, :], mult, mult,
                    [PB[pa[h // 4]], B_st, B_L], [B_Q[0]])
            pa = permm(lambda h: kT[:, h, :], lambda h: qT[:, h, :], [B_kT, B_qT])
            for half in range(2):
                tt("dve", attT[:, 4 * half:4 * half + 4, :], v3(PS[pa[half]][:, :]), LT[:, 4 * half:4 * half + 4, :], mult,
                   [PB[pa[half]], B_LT], [B_att])
            pa = permm(lambda h: X[:, h, :], lambda h: ident, [B_Q[0], Bc])
            evac(pa, Pb[0], B_P[0])
            tt("pool", Rm, Pb[0], ident.unsqueeze(1).to_broadcast([128, 8, 128]), add, [B_P[0], Bc], [B_R])
            cur = 0
            for lev in range(1, 7):
                nxt = 1 - cur
                Pc, Qc, Pn, Qn = Pb[cur], Qb[cur], Pb[nxt], Qb[nxt]
                pa = permm(lambda h: Pc[:, h, :], lambda h: Qc[:, h, :], [B_P[cur], B_Q[cur]])
                evac(pa, Qn, B_Q[nxt])
                if lev <= 5:
                    pa = permm(lambda h: Qc[:, h, :], lambda h: Pc[:, h, :], [B_P[cur], B_Q[cur]])
                    evac(pa, Pn, B_P[nxt])
                pa = permm(lambda h: Qn[:, h, :], lambda h: Rm[:, h, :], [B_Q[nxt], B_R])
                for half in range(2):
                    tt("dve", Rm[:, 4 * half:4 * half + 4, :], Rm[:, 4 * half:4 * half + 4, :], v3(PS[pa[half]][:, :]), add,
                       [PB[pa[half]], B_R], [B_R])
                cur = nxt
            tt("pool", vb, cv, bc8(beta), mult, [B_c[2], B_st], [B_vb])
            tt("pool", kbd, ck, bc8(bed), mult, [B_c[1], B_st], [B_kbd])
            tt("pool", kd, ck, bc8(edld), mult, [B_c[1], B_st], [B_kd])
            pa = permm(lambda h: kbd[:, h, :], lambda h: Rm[:, h, :], [B_kbd, B_R])
            for half in range(2):
                ts("dve", nwT[:, 4 * half:4 * half + 4, :], v3(PS[pa[half]][:, :]), -1.0, None, mult, None, [PB[pa[half]]], [B_nwT])
            pa = permm(lambda h: Rm[:, h, :], lambda h: vb[:, h, :], [B_R, B_vb, B_nwT, B_S],
                       extra=(lambda h: nwT[:, h, :], lambda h: S[:, h, :]))
            evac(pa, vn, B_vn)
            pa = permm(lambda h: qT[:, h, :], lambda h: S[:, h, :], [B_qT, B_S])
            for half in range(2):
                tt("dve", qSs[:, 4 * half:4 * half + 4, :], v3(PS[pa[half]][:, :]),
                   edec[:, 4 * half:4 * half + 4].unsqueeze(2).to_broadcast([128, 4, 128]), mult, [PB[pa[half]], B_st], [B_qSs])
            pa = permm(lambda h: attT[:, h, :], lambda h: vn[:, h, :], [B_att, B_vn])
            for half in range(2):
                tt("dve", qSs[:, 4 * half:4 * half + 4, :], qSs[:, 4 * half:4 * half + 4, :], v3(PS[pa[half]][:, :]), add,
                   [PB[pa[half]], B_qSs], [B_qSs])
            pa = permm(lambda h: kd[:, h, :], lambda h: vn[:, h, :], [B_kd, B_vn])
            for h in range(8):
                stt("dve", S[:, h, :], S[:, h, :], edl[:, h:h + 1], PS[pa[h // 4]][:, (h % 4) * 128:(h % 4 + 1) * 128], mult, add,
                    [PB[pa[h // 4]], B_st, B_S], [B_S])
            tt("pool", vb, qSs, qSs, mult, [B_qSs], [B_vb])
            red(ss[:, 16:24], vb, [B_vb], [B_st])
            rsq(ss[:, 16:24], tmp[:, 16:24], rn[:, 16:24], 1.0 / 128, B_st)
            tt("dve", qSs, qSs, bc8(rn[:, 16:24]), mult, [B_qSs, B_st], [B_qSs])
            tt("pool", qSs, qSs, gnw[:].unsqueeze(1).to_broadcast([128, 8, 128]), mult, [B_qSs, Bc], [B_qSs])
            dma("act", zb, H.ap()[r0:r0 + 128, O_ZB:O_ZB + 1024], [B_H], [B_zb], "zb")
            actf(zb, zb, AF.Silu, [B_zb], [B_zb])
            tt("dve", zb, zb, qSs.rearrange("p a b -> p (a b)"), mult, [B_zb, B_qSs], [B_zb])
            dma("act", OB.ap()[r0:r0 + 128, :], zb, [B_zb], [B_OB], "ob")
        conv_part(0)
        for t_ in range(NT):
            if t_ + 1 < NT:
                conv_part(t_ + 1)
            chain_part(t_)
        dma("sp", S_p.ap()[seq].rearrange("h k v -> k h v"), S, [B_S], [B_out], "o0")

    def phase_F():
        ar.reset()
        wpa = ar.get([8, 2048], BF16)
        wpb = ar.get([8, 2048], BF16)
        wo = ar.get([16, 2048], BF16)
        B_w = kb.buf()
        m = ar.get([2048])
        B_m = kb.buf()
        stg = [m[:, 0:1024], m[:, 1024:2048]]
        B_stg = kb.bufs(2)
        oa = ar.get([1024])
        ob_ = ar.get([1024])
        B_oa, B_ob = kb.bufs(2)
        oT = ar.get([16, 128], BF16)
        B_oT = kb.buf()
        mT = ar.get([16, 128], BF16)
        B_mT = kb.buf()
        ga = [ar.get([512]) for _ in range(2)]
        gb = [ar.get([512]) for _ in range(2)]
        B_ga = kb.bufs(2)
        B_gb = kb.bufs(2)
        xy = [ar.get([512]) for _ in range(2)]
        B_xy = kb.bufs(2)
        n = 0
        for (wd, wt_, nk) in ((wpa_d, wpa, 8), (wpb_d, wpb, 8), (wo_d, wo, 16)):
            for kc in range(nk):
                for hf in range(2):
                    i = n % 2
                    n += 1
                    dma("sp" if i == 0 else "act", stg[i], wd.ap()[kc * 128:(kc + 1) * 128, hf * 1024:(hf + 1) * 1024], [], [B_stg[i]], "fst%d" % i)
                    cp("dve" if i == 0 else "pool", wt_[:, kc, hf * 1024:(hf + 1) * 1024], stg[i], [B_stg[i]], [B_w])
        kb.barrier()
        for tile in range(2 * NT + 1):
            n_ = 128 if tile < 2 * NT else 4
            r0 = tile * 128
            dma("sp", oa[0:n_, :], OA.ap()[r0:r0 + n_, :], [B_OA], [B_oa], "foa")
            dma("act", ob_[0:n_, :], OB.ap()[r0:r0 + n_, :], [B_OB], [B_ob], "fob")
            for wh, (src, Bs_) in enumerate(((oa, B_oa), (ob_, B_ob))):
                for j in range(2):
                    pj = 2 * wh + j
                    kb.op("pe", [mm(PS[pj][:, k * n_:(k + 1) * n_], src[0:n_, (4 * j + k) * 128:(4 * j + k + 1) * 128], ident[0:n_, 0:n_])
                                 for k in range(4)], [Bs_, Bc], [PB[pj]])
                    cp("act" if j == 0 else "dve", oT[:, 8 * wh + 4 * j:8 * wh + 4 * j + 4, 0:n_],
                       PS[pj][:, 0:4 * n_].rearrange("p (a b) -> p a b", a=4), [PB[pj]], [B_oT])
            for cb in range(4):
                i = cb % 2
                c_ = slice(cb * 512, (cb + 1) * 512)
                rr = slice(r0, r0 + n_)
                dma("sp", ga[i][0:n_, :], H.ap()[rr, O_GA + cb * 512:O_GA + (cb + 1) * 512], [B_H], [B_ga[i]], "fga%d" % i)
                dma("act", gb[i][0:n_, :], H.ap()[rr, O_GB + cb * 512:O_GB + (cb + 1) * 512], [B_H], [B_gb[i]], "fgb%d" % i)
                actf(ga[i][0:n_, :], ga[i][0:n_, :], AF.Sigmoid, [B_ga[i]], [B_ga[i]])
                actf(gb[i][0:n_, :], gb[i][0:n_, :], AF.Sigmoid, [B_gb[i]], [B_gb[i]])
                pa_, pb_ = 4 + i, 6 + i
                kb.op("pe", [mm(PS[pa_][0:n_, :], oT[:, kc, 0:n_], wpa[:, kc, c_], start=(kc == 0), stop=(kc == 7)) for kc in range(8)],
                      [B_oT, B_w], [PB[pa_]])
                kb.op("pe", [mm(PS[pb_][0:n_, :], oT[:, 8 + kc, 0:n_], wpb[:, kc, c_], start=(kc == 0), stop=(kc == 7)) for kc in range(8)],
                      [B_oT, B_w], [PB[pb_]])
                tt("dve", m[0:n_, c_], ga[i][0:n_, :], PS[pa_][0:n_, :], mult, [B_ga[i], PB[pa_]], [B_m])
                tt("dve", gb[i][0:n_, :], gb[i][0:n_, :], PS[pb_][0:n_, :], mult, [B_gb[i], PB[pb_]], [B_gb[i]])
                tt("pool", m[0:n_, c_], m[0:n_, c_], gb[i][0:n_, :], add, [B_m, B_gb[i]], [B_m])
            for j in range(4):
                pj = j
                kb.op("pe", [mm(PS[pj][:, k * n_:(k + 1) * n_], m[0:n_, (4 * j + k) * 128:(4 * j + k + 1) * 128], ident[0:n_, 0:n_])
                             for k in range(4)], [B_m, Bc], [PB[pj]])
                cp("act" if j % 2 == 0 else "dve", mT[:, 4 * j:4 * j + 4, 0:n_],
                   PS[pj][:, 0:4 * n_].rearrange("p (a b) -> p a b", a=4), [PB[pj]], [B_mT])
            for cb in range(4):
                i = cb % 2
                c_ = slice(cb * 512, (cb + 1) * 512)
                if tile < 2 * NT:
                    dma("sp", xy[i][0:n_, :], x_p.ap()[r0:r0 + n_, c_], [], [B_xy[i]], "fx%d" % i)
                else:
                    dma("sp", xy[i][0:n_, :], x_s.ap()[:, c_], [], [B_xy[i]], "fx%d" % i)
                pj = 4 + i
                kb.op("pe", [mm(PS[pj][0:n_, :], mT[:, kc, 0:n_], wo[:, kc, c_], start=(kc == 0), stop=(kc == 15)) for kc in range(16)],
                      [B_mT, B_w], [PB[pj]])
                tt("dve", xy[i][0:n_, :], xy[i][0:n_, :], PS[pj][0:n_, :], add, [B_xy[i], PB[pj]], [B_xy[i]])
                if tile < 2 * NT:
                    dma("sp", y_p.ap()[r0:r0 + n_, c_], xy[i][0:n_, :], [B_xy[i]], [B_out], "fy%d" % i)
                else:
                    dma("sp", y_s.ap()[:, c_], xy[i][0:n_, :], [B_xy[i]], [B_out], "fy%d" % i)

    def phase_SNSA():
        ar.reset()
        R0 = 2 * T
        big = ar.get([32768])
        SEG = big.bitcast(BF16).rearrange("p (a b) -> p a b", a=4)
        KsTs = big[:, 0:16384].bitcast(BF16).rearrange("p (a b) -> p a b", a=2)
        VSs = big[:, 16384:32768].bitcast(BF16).rearrange("p (a b c) -> p a b c", a=128, b=2)
        B_big = kb.buf()
        cs = ar.get([ns32])
        B_cs = kb.buf()
        sels = ar.get([8, 257], BF16)
        dma("sp", cs, cs32_d.ap(), [], [B_cs], "c0")
        dma("sp", big[:, 0:2056], csb_d.ap(), [], [B_big], "c1")
        cp("dve", sels.rearrange("p a b -> p (a b)"), big[:, 0:2056], [B_big], [B_cs])

        def CS(name):
            o, n = os32[name]
            return cs[:, o:o + n]
        abs_ = CS("abs").rearrange("p (a b) -> p a b", a=8)
        distd = CS("distd")
        distw = CS("distw")
        rowc = CS("rowc")
        a1s = rowc[0:1, 0:257]
        a2s = rowc[0:1, 257:514]
        hsel = [rowc[0:1, 514 + 128 * c_:514 + 128 * (c_ + 1)] for c_ in range(8)]
        iotap = CS("iotap")
        ptb = big[:, 4200:4328].bitcast(I32)
        ptf = big[:, 4800:4928]
        IDX = ar.get([128], I32)
        B_IDX = kb.buf()
        for c_ in range(4):
            dma("sp", ptb[32 * c_:32 * c_ + 32, :], pt_d.ap()[c_:c_ + 1, :].partition_broadcast(32), [], [B_IDX], "c0")
        cp("dve", ptf, ptb, [B_IDX], [B_IDX])
        ts("dve", ptf, ptf, 32.0, iotap[:, 0:1], mult, add, [B_IDX, B_cs], [B_IDX])
        cp("dve", IDX, ptf, [B_IDX], [B_IDX])
        hs4 = big[:, 0:2584]
        B_h4 = kb.buf()
        sq4b = big[:, 2600:4136]
        sq4 = ar.get([128])
        S4 = ar.get([40])
        B_S4 = kb.buf()
        gs4 = ar.get([24])
        qTs = ar.get([8, 4], BF16)
        kTn = ar.get([4, 4], BF16)
        B_qTs = kb.buf()
        r1 = ar.get([2048])
        r2 = ar.get([2048])
        pg4 = [r1, r2]
        pg = [r1[:, 0:512], r1[:, 512:1024]]
        B_pg = kb.bufs(2)
        hsb = r1[:, 1024:1536].bitcast(BF16)
        B_hsb = kb.buf()
        KcTs = r2[:, 0:1024].bitcast(BF16).rearrange("p (a b) -> p a b", a=2)
        VCs = r2[:, 1024:2048].bitcast(BF16).rearrange("p (a b c) -> p a b c", a=8, b=2)
        B_KCs = kb.buf()
        kcw = ar.get([128])
        B_kcw = kb.buf()
        cst = ar.get([8])
        B_cst = kb.buf()
        tmpc = ar.get([8, 4])
        ecs = ar.get([8, 4], BF16)
        B_ecs = kb.buf()
        impn = ar.get([258])
        rows = ar.get([800])
        B_rows = kb.buf()
        score = rows[0:1, 0:257]
        sc2 = rows[0:1, 258:515]
        m8 = rows[0:1, 516:532]
        negsel = rows[0:1, 534:791]
        negm = [ar.get([32]) for _ in range(2)]
        B_negm = kb.bufs(2)
        tmpd = r1[:, 1536:2048].rearrange("p (a b) -> p a b", a=128)
        esd = ar.get([128, 4], BF16)
        B_esd = kb.buf()
        enew = ar.get([4], BF16)
        vnew = ar.get([128], BF16)
        vnf = ar.get([128])
        B_new = kb.buf()
        usb = ar.get([128])
        rdn = ar.get([4])
        B_usb = kb.buf()
        usc_t = big[:, 0:3072].rearrange("p (a b c) -> p a b c", a=3, b=8)
        za4 = big[:, 3072:4096]
        B_fin = kb.buf()
        B_NK = kb.buf()
        B_USC = kb.buf()
        onesb = CBv("onesb")
        ones = C("ones")
        kb.op("pool", lambda e: e.memset(enew, 0.0), [], [B_new])
        kb.op("pool", lambda e: e.memset(vnew, 0.0), [], [B_new])
        kb.barrier()
        h = hs4[0:4, :]
        dma("sp", h, H.ap()[R0:R0 + 4, 0:2584], [B_H], [B_h4], "hq0")
        dma("sp", cmp_s.ap(), h[:, O_KVC:O_KVC + 512], [B_h4], [B_out], "o0")
        tt("dve", sq4b[0:4, 0:1024], h[:, 0:1024], h[:, 0:1024], mult, [B_h4], [B_S4])
        red(S4[0:4, 0:8], sq4b[0:4, 0:1024].rearrange("p (a d) -> p a d", a=8), [B_S4], [B_S4])
        tt("dve", sq4b[0:4, 1024:1280], h[:, O_KVS:O_KVS + 256], h[:, O_KVS:O_KVS + 256], mult, [B_h4], [B_S4])
        tt("dve", sq4b[0:4, 1280:1536], h[:, O_KVW:O_KVW + 256], h[:, O_KVW:O_KVW + 256], mult, [B_h4], [B_S4])
        red(S4[0:4, 8:12], sq4b[0:4, 1024:1536].rearrange("p (a d) -> p a d", a=4), [B_S4], [B_S4])
        rsq(S4[0:4, 0:12], S4[0:4, 12:24], S4[0:4, 24:36], 1.0 / 128, B_S4)
        tt("dve", h[:, 0:1024].rearrange("p (a d) -> p a d", a=8), h[:, 0:1024].rearrange("p (a d) -> p a d", a=8),
           S4[0:4, 24:32].unsqueeze(2).to_broadcast([4, 8, 128]), mult, [B_h4, B_S4], [B_h4])
        for g in range(2):
            stt("dve", h[:, O_KVS + 128 * g:O_KVS + 128 * g + 128], h[:, O_KVS + 128 * g:O_KVS + 128 * g + 128],
                S4[0:4, 32 + g:33 + g], knws[0:4, :], mult, mult, [B_h4, B_S4, Bc], [B_h4])
            stt("dve", h[:, O_KVW + 128 * g:O_KVW + 128 * g + 128], h[:, O_KVW + 128 * g:O_KVW + 128 * g + 128],
                S4[0:4, 34 + g:35 + g], knww[0:4, :], mult, mult, [B_h4, B_S4, Bc], [B_h4])
        dma("sp", slc_s.ap(), h[:, O_KVS:O_KVS + 512], [B_h4], [B_out], "o0")
        dma("sp", NK.ap(), h[:, O_KVC:O_KVC + 1536], [B_h4], [B_NK], "o1")
        dma("sp", win_s.ap().rearrange("(b t) c -> b t c", b=4)[:, 511, :], h[:, O_KVW:O_KVW + 512], [B_h4], [B_out], "o0")
        for b in range(4):
            dma("act", win_s.ap()[b * 512:b * 512 + 511, :], stwin_d.ap()[b * 512 + 1:b * 512 + 512, :], [], [B_out], "o0")
        kb.op("pe", [mm(PS[0][:, k * 4:(k + 1) * 4], h[:, k * 128:(k + 1) * 128], ident[0:4, 0:4]) for k in range(8)], [B_h4, Bc], [PB[0]])
        ts("dve", qTs, PS[0][:, 0:32].rearrange("p (a b) -> p a b", a=8), qnw[:, 0:1], None, mult, None, [PB[0], Bc], [B_qTs])
        kb.op("pe", [mm(PS[1][:, k * 4:(k + 1) * 4], h[:, (O_KVS if k < 2 else O_KVW) + (k % 2) * 128:(O_KVS if k < 2 else O_KVW) + (k % 2) * 128 + 128],
                        ident[0:4, 0:4]) for k in range(4)], [B_h4, Bc], [PB[1]])
        cp("act", kTn, PS[1][:, 0:16].rearrange("p (a b) -> p a b", a=4), [PB[1]], [B_qTs])
        actf(gs4[0:4, :], h[:, O_GN:O_GN + 24], AF.Sigmoid, [B_h4], [B_fin])

        kb.barrier()

        def gather_pages(b, cache_d, ntr, cmp_mode):
            kb.barrier()
            for q in range(32):
                i = q % 2
                kb.dma("pool", lambda e, i=i, col=b * 32 + q: e.indirect_dma_start(
                    out=pg4[i], out_offset=None, in_=cache_d.ap(),
                    in_offset=bass.IndirectOffsetOnAxis(ap=IDX[:, col:col + 1], axis=0)),
                    [B_IDX], [B_pg[i]], "g%d" % i)
                for j in range(4):
                    pj = 6 + (j % 2)
                    kb.op("pe", [mm(PS[pj][:, k * 128:(k + 1) * 128], pg4[i][:, j * 512 + k * 128:j * 512 + (k + 1) * 128], ident)
                                 for k in range(ntr)], [B_pg[i], Bc], [PB[pj]])
                    if cmp_mode:
                        dst = SEG[:, 0:4, 512 * q + j:512 * q + 512:4]
                    else:
                        dst = KsTs[:, 0:2, (4 * q + j) * 128:(4 * q + j + 1) * 128]
                    cp("act" if j % 2 == 0 else "dve", dst, PS[pj][:, 0:ntr * 128].rearrange("p (a b) -> p a b", a=ntr), [PB[pj]], [B_big])
                    if not cmp_mode:
                        cp("pool", VSs[:, 4 * q + j, :, :], pg4[i][:, j * 512 + 256:j * 512 + 512].rearrange("p (a b) -> p a b", a=2),
                           [B_pg[i]], [B_big])
            kb.barrier()

        def dense(b, br, npg, ABv, negms, kcol0, nk_off):
            for g in range(2):
                kb.op("pe", [mm(PS[0][:, p * 4:(p + 1) * 4], KsTs[:, g, p * 128:(p + 1) * 128], qTs[:, 4 * g:4 * g + 4, b]) for p in range(npg)],
                      [B_big, B_qTs], [PB[0]])
                td = tmpd[:, 0:npg, :]
                ts("dve", td, PS[0][:, 0:npg * 4].rearrange("p (a b) -> p a b", a=npg), SCALE, None, mult, None, [PB[0]], [B_esd])
                for r_ in range(4):
                    stt("dve", td[:, :, r_], ABv[:, 0:npg], -SLOPES[4 * g + r_], td[:, :, r_], mult, add, [B_esd, B_cs], [B_esd])
                if negms is not None:
                    td4 = tmpd.rearrange("p (q j) r -> p q j r", j=4)
                    tt("dve", td4, td4, negms[g].unsqueeze(2).unsqueeze(3).to_broadcast([128, 32, 4, 4]), add, [B_esd, B_negm[g]], [B_esd])
                actf(esd[:, 0:npg, :], td, AF.Exp, [B_esd], [B_esd])
                kb.op("pe", mm(PS[1][0:1, 0:4], kTn[:, kcol0 + g, b:b + 1], qTs[:, 4 * g:4 * g + 4, b]), [B_qTs], [PB[1]])
                actf(enew[0:1, 0:4], PS[1][0:1, 0:4], AF.Exp, [PB[1]], [B_new], scale=SCALE)
                dma("sp", vnf[0:1, :], NK.ap()[b:b + 1, nk_off + 256 + 128 * g:nk_off + 256 + 128 * g + 128], [B_NK], [B_new], "vn")
                cp("dve", vnew[0:1, :], vnf[0:1, :], [B_new], [B_new])
                kb.op("pe", [mm(PS[2][0:4, 0:128], enew, vnew, start=True, stop=False)] +
                      [mm(PS[2][0:4, 0:128], esd[:, p, :], VSs[:, p, g, :], start=False, stop=(p == npg - 1)) for p in range(npg)],
                      [B_new, B_esd, B_big], [PB[2]])
                kb.op("pe", [mm(PS[3][0:4, 0:2], enew, onesb, start=True, stop=False)] +
                      [mm(PS[3][0:4, 0:2], esd[:, p, :], onesb, start=False, stop=(p == npg - 1)) for p in range(npg)],
                      [B_new, B_esd, Bc], [PB[3]])
                recip(rdn[0:4, 0:1], PS[3][0:4, 0:1], [PB[3]], [B_usb])
                ts("dve", usb[0:4, :], PS[2][0:4, 0:128], rdn[0:4, 0:1], None, mult, None, [PB[2], B_usb], [B_usb])
                dma("sp", USC.ap()[b, br, 4 * g:4 * g + 4, :], usb[0:4, :], [B_usb], [B_USC], "us")

        for b in range(4):
            gather_pages(b, ccmp_d, 4, True)
            kb.op("pool", lambda e: e.memset(KcTs[:, :, 1023:1024], 0.0), [], [B_KCs])
            kb.op("pool", lambda e: e.memset(VCs[96:128, 7, :, :], 0.0), [], [B_KCs])
            for kvg in range(4):
                isk = kvg < 2
                g = kvg % 2
                w1 = w1k if isk else w1v
                for nbk in range(2):
                    N = 512 if nbk == 0 else 511
                    base = nbk * 8192
                    kb.op("pe", [mm(PS[2 + nbk][:, 0:N], w1[:, s_, :], SEG[:, kvg, base + s_:base + s_ + 16 * (N - 1) + 1:16],
                                    start=(s_ == 0), stop=(s_ == 31)) for s_ in range(32)], [B_big, Bc], [PB[2 + nbk]])
                    actf(hsb[:, nbk * 512:nbk * 512 + N], PS[2 + nbk][:, 0:N], AF.Silu, [PB[2 + nbk], Bc], [B_hsb],
                         bias=(b1k if isk else b1v)[:, 0:1])
                for it in range(8):
                    nb_ = 128 if it < 7 else 127
                    pj = 4 + (it % 2)
                    kb.op("pe", mm(PS[pj][0:nb_, 0:128], hsb[:, it * 128:it * 128 + nb_], (w2k if isk else w2v)[:]), [B_hsb, Bc], [PB[pj]])
                    if isk:
                        cp("act", kcw[0:nb_, :], PS[pj][0:nb_, 0:128], [PB[pj]], [B_kcw])
                        tt("dve", sq4[0:nb_, 0:128], kcw[0:nb_, :], kcw[0:nb_, :], mult, [B_kcw], [B_S4])
                        red(cst[0:nb_, 0:1], sq4[0:nb_, 0:128], [B_S4], [B_cst])
                        rsq(cst[0:nb_, 0:1], cst[0:nb_, 1:2], cst[0:nb_, 2:3], 1.0 / 128, B_cst)
                        stt("dve", kcw[0:nb_, :], kcw[0:nb_, :], cst[0:nb_, 2:3], knwc[0:nb_, :], mult, mult, [B_kcw, B_cst, Bc], [B_kcw])
                        kb.op("pe", mm(PS[6][:, 0:nb_], kcw[0:nb_, :], ident[0:nb_, 0:nb_]), [B_kcw, Bc], [PB[6]])
                        cp("act", KcTs[:, g, it * 128:it * 128 + nb_], PS[6][:, 0:nb_], [PB[6]], [B_KCs])
                    else:
                        cp("act", VCs[0:nb_, it, g, :], PS[pj][0:nb_, 0:128], [PB[pj]], [B_KCs])
            for g in range(2):
                kb.op("pe", [mm(PS[0][:, it * 4:(it + 1) * 4], KcTs[:, g, it * 128:(it + 1) * 128], qTs[:, 4 * g:4 * g + 4, b]) for it in range(8)],
                      [B_KCs, B_qTs], [PB[0]])
                stt("dve", tmpc, PS[0][:, 0:32].rearrange("p (a b) -> p a b", a=8), SCALE, abs_[:, :, 4 * g:4 * g + 4], mult, add,
                    [PB[0], B_cs], [B_ecs])
                actf(ecs, tmpc, AF.Exp, [B_ecs], [B_ecs])
                kb.op("pe", [mm(PS[1][0:4, 0:257], ecs[:, it, :], sels[:, it, :], start=(it == 0), stop=(it == 7)) for it in range(8)],
                      [B_ecs, B_cs], [PB[1]])
                kb.op("pe", [mm(PS[2][0:4, 0:128], ecs[:, it, :], VCs[:, it, g, :], start=(it == 0), stop=(it == 7)) for it in range(8)] +
                      [mm(PS[2][0:4, 128:130], ecs[:, it, :], onesb, start=(it == 0), stop=(it == 7)) for it in range(8)],
                      [B_ecs, B_KCs, Bc], [PB[2]])
                recip(rdn[0:4, 0:1], PS[2][0:4, 128:129], [PB[2]], [B_usb])
                ts("dve", usb[0:4, :], PS[2][0:4, 0:128], rdn[0:4, 0:1], None, mult, None, [PB[2], B_usb], [B_usb])
                dma("sp", USC.ap()[b, 0, 4 * g:4 * g + 4, :], usb[0:4, :], [B_usb], [B_USC], "us")
                ts("dve", impn[0:4, 0:257], PS[1][0:4, 0:257], rdn[0:4, 0:1], None, mult, None, [PB[1], B_usb], [B_rows])
                kb.op("pe", mm(PS[3][0:1, 0:257], ones[0:4, 0:1], impn[0:4, 0:257]), [B_rows, Bc], [PB[3]])
                tt("dve", score, PS[3][0:1, 0:257], a1s, mult, [PB[3], B_cs], [B_rows])
                tt("dve", score, score, a2s, add, [B_rows, B_cs], [B_rows])
                kb.op("dve", lambda e: e.max(out=m8[:, 0:8], in_=score), [B_rows], [B_rows])
                kb.op("dve", lambda e: e.match_replace(out=sc2, in_to_replace=m8[:, 0:8], in_values=score, imm_value=-30000.0), [B_rows], [B_rows])
                kb.op("dve", lambda e: e.max(out=m8[:, 8:16], in_=sc2), [B_rows], [B_rows])
                ts("dve", negsel, score, m8[:, 15:16], None, ALU.is_ge, None, [B_rows], [B_rows])
                ts("dve", negsel, negsel, -1.0, 30000.0, add, mult, [B_rows], [B_rows])
                kb.op("pe", [mm(PS[3][:, 0:32], hsel[c_], negsel[:, c_:256:8], start=(c_ == 0), stop=(c_ == 7)) for c_ in range(8)],
                      [B_rows, B_cs], [PB[3]])
                cp("act", negm[g], PS[3][:, 0:32], [PB[3]], [B_negm[g]])
            kb.barrier()
            gather_pages(b, cslc_d, 2, False)
            dense(b, 1, 128, distd, negm, 0, 512)
            for t_ in range(4):
                i = t_ % 2
                dma("sp", pg[i], stwin_d.ap()[b * 512 + t_ * 128:b * 512 + (t_ + 1) * 128, :], [], [B_pg[i]], "g%d" % i)
                pj = 6 + (t_ % 2)
                kb.op("pe", [mm(PS[pj][:, k * 128:(k + 1) * 128], pg[i][:, k * 128:(k + 1) * 128], ident) for k in range(2)],
                      [B_pg[i], Bc], [PB[pj]])
                cp("act", KsTs[:, 0:2, t_ * 128:(t_ + 1) * 128], PS[pj][:, 0:256].rearrange("p (a b) -> p a b", a=2), [PB[pj]], [B_big])
                cp("pool", VSs[:, t_, :, :], pg[i][:, 256:512].rearrange("p (a b) -> p a b", a=2), [B_pg[i]], [B_big])
            dense(b, 2, 4, distw, None, 2, 1024)
            kb.barrier()
        dma("sp", usc_t[0:4, :, :, :], USC.ap(), [B_USC], [B_fin], "us")
        for br in range(3):
            tt("dve", usc_t[0:4, br, :, :], usc_t[0:4, br, :, :], gs4[0:4, br * 8:(br + 1) * 8].unsqueeze(2).to_broadcast([4, 8, 128]), mult,
               [B_fin], [B_fin])
        tt("dve", usc_t[0:4, 0, :, :], usc_t[0:4, 0, :, :], usc_t[0:4, 1, :, :], add, [B_fin], [B_fin])
        tt("dve", usc_t[0:4, 0, :, :], usc_t[0:4, 0, :, :], usc_t[0:4, 2, :, :], add, [B_fin], [B_fin])
        dma("sp", za4[0:4, :], H.ap()[R0:R0 + 4, O_ZA:O_ZA + 1024], [B_H], [B_fin], "za")
        actf(za4[0:4, :], za4[0:4, :], AF.Silu, [B_fin], [B_fin])
        tt("dve", za4[0:4, :], za4[0:4, :], usc_t[0:4, 0, :, :].rearrange("p a b -> p (a b)"), mult, [B_fin], [B_fin])
        dma("sp", OA.ap()[R0:R0 + 4, :], za4[0:4, :], [B_fin], [B_OA], "oa")

    def phase_SGDN():
        ar.reset()
        R0 = 2 * T
        hc = ar.get([4, 3072])
        CW = ar.get([4, 3072])
        B_hc = kb.buf()
        B_CW = kb.buf()
        cs_ = ar.get([3072])
        B_cs = kb.buf()
        kTs = ar.get([8, 4])
        qT2 = ar.get([8, 4])
        kTz = ar.get([8, 4, 4])
        qTz = ar.get([8, 4, 4])
        B_T = kb.buf()
        S0t = ar.get([4, 8, 128])
        B_S0 = kb.buf()
        vn = ar.get([8, 128])
        o_ = ar.get([8, 128])
        B_vn = kb.buf()
        B_o = kb.buf()
        Sn = [ar.get([8, 128]) for _ in range(2)]
        B_Sn = kb.bufs(2)
        EGB = ar.get([32])
        B_EGB = kb.buf()
        st = ar.get([128])
        B_st = kb.buf()
        sqs = hc[0:4, 0, 0:2048]
        t1 = hc[0:4, 1, 0:1024].rearrange("p (a b) -> p a b", a=8)
        t2 = hc[0:4, 1, 1024:2048].rearrange("p (a b) -> p a b", a=8)
        vnm = hc[0:4, 2, 0:1024]
        zb4 = hc[0:4, 2, 1024:2048]
        ab = st[0:4, 0:16]
        beta = st[0:4, 16:24]
        gg = st[0:4, 24:32]
        eg = st[0:4, 32:40]
        ss = st[0:4, 40:64]
        tmp = st[0:4, 64:88]
        rn = st[0:4, 88:112]
        qk = st[0:4, 112:120]
        egd_t = ar.get([32])
        ones = C("ones")
        e4 = C("e4").rearrange("p (a b) -> p a b", a=4)
        dma("sp", CW[0:4, :, :].rearrange("p a b -> p (a b)"), convw_d.ap().partition_broadcast(4), [], [B_CW], "c0")
        dma("sp", hc[0:4, 0:3, :], stconv_d.ap(), [], [B_hc], "hw0")
        dma("sp", hc[0:4, 3, :], H.ap()[R0:R0 + 4, O_QKV:O_QKV + 3072], [B_H], [B_hc], "hw0")
        dma("sp", conv_s.ap(), hc[0:4, 1:4, :], [B_hc], [B_out], "o0")
        tt("pool", hc[0:4, :, :], hc[0:4, :, :], CW[0:4, :, :], mult, [B_hc, B_CW], [B_hc])
        c4 = cs_[0:4, :]
        tt("dve", c4, hc[0:4, 0, :], hc[0:4, 1, :], add, [B_hc], [B_cs])
        tt("dve", c4, c4, hc[0:4, 2, :], add, [B_cs, B_hc], [B_cs])
        tt("dve", c4, c4, hc[0:4, 3, :], add, [B_cs, B_hc], [B_cs])
        actf(c4, c4, AF.Silu, [B_cs], [B_cs])
        q3 = c4[:, 0:1024].rearrange("p (a b) -> p a b", a=8)
        k3 = c4[:, 1024:2048].rearrange("p (a b) -> p a b", a=8)
        v3_ = c4[:, 2048:3072].rearrange("p (a b) -> p a b", a=8)
        tt("dve", sqs, c4[:, 0:2048], c4[:, 0:2048], mult, [B_cs], [B_hc])
        red(ss[:, 0:16], sqs.rearrange("p (a b) -> p a b", a=16), [B_hc], [B_st])
        rsq(ss[:, 0:16], tmp[:, 0:16], rn[:, 0:16], 1.0, B_st)
        ts("dve", rn[:, 0:8], rn[:, 0:8], 128 ** -0.5, None, mult, None, [B_st], [B_st])
        tt("dve", c4[:, 0:2048].rearrange("p (a b) -> p a b", a=16), c4[:, 0:2048].rearrange("p (a b) -> p a b", a=16),
           rn[:, 0:16].unsqueeze(2).to_broadcast([4, 16, 128]), mult, [B_cs, B_st], [B_cs])
        dma("sp", ab, H.ap()[R0:R0 + 4, O_A:O_A + 16], [B_H], [B_st], "ab")
        actf(beta, ab[:, 8:16], AF.Sigmoid, [B_st], [B_st])
        tt("dve", gg, ab[:, 0:8], dtb[0:4, :], add, [B_st, Bc], [B_st])
        actf(gg, gg, AF.Exp, [B_st], [B_st])
        actf(gg, gg, AF.Ln, [B_st], [B_st], bias=1.0)
        tt("dve", gg, gg, nea[0:4, :], mult, [B_st, Bc], [B_st])
        actf(eg, gg, AF.Exp, [B_st], [B_st])
        kb.op("pe", [mm(PS[0][:, h * 4:(h + 1) * 4], c4[:, 1024 + h * 128:1024 + (h + 1) * 128], ident[0:4, 0:4]) for h in range(8)],
              [B_cs, Bc], [PB[0]])
        cp("act", kTs, PS[0][:, 0:32].rearrange("p (a b) -> p a b", a=8), [PB[0]], [B_T])
        kb.op("pe", [mm(PS[1][:, h * 4:(h + 1) * 4], c4[:, h * 128:(h + 1) * 128], ident[0:4, 0:4]) for h in range(8)],
              [B_cs, Bc], [PB[1]])
        cp("act", qT2, PS[1][:, 0:32].rearrange("p (a b) -> p a b", a=8), [PB[1]], [B_T])
        for (src, dstz) in ((kTs, kTz), (qT2, qTz)):
            tt("dve", dstz, src.unsqueeze(3).to_broadcast([128, 8, 4, 4]), e4.unsqueeze(1).to_broadcast([128, 8, 4, 4]), mult,
               [B_T, Bc], [B_T])
        for b in range(4):
            dma("sp" if b % 2 == 0 else "act", S0t[:, b, :, :], S0_d.ap()[b].rearrange("h k v -> k h v"), [], [B_S0], "s0%d" % (b % 2))
        for (Tz, banks) in ((kTz, (2, 3)), (qTz, (4, 5))):
            for half in range(2):
                ops = []
                for k in range(4):
                    h = 4 * half + k
                    for b in range(4):
                        ops.append(mm(PS[banks[half]][0:4, k * 128:(k + 1) * 128], Tz[:, h, b, :], S0t[:, b, h, :], start=(b == 0), stop=(b == 3)))
                kb.op("pe", ops, [B_T, B_S0], [PB[banks[half]]])
        v4 = vn[0:4, :, :]
        o4 = o_[0:4, :, :]
        for half in range(2):
            hs_ = slice(4 * half, 4 * half + 4)
            tt("dve", t1[:, hs_, :], PS[2 + half][0:4, :].rearrange("p (a b) -> p a b", a=4),
               eg[:, hs_].unsqueeze(2).to_broadcast([4, 4, 128]), mult, [PB[2 + half], B_st], [B_hc])
        tt("dve", v4, v3_, t1, sub, [B_cs, B_hc], [B_vn])
        tt("dve", v4, v4, beta.unsqueeze(2).to_broadcast([4, 8, 128]), mult, [B_vn, B_st], [B_vn])
        tt("dve", t2, q3, k3, mult, [B_cs], [B_hc])
        red(qk, t2, [B_hc], [B_st])
        for half in range(2):
            hs_ = slice(4 * half, 4 * half + 4)
            tt("dve", o4[:, hs_, :], PS[4 + half][0:4, :].rearrange("p (a b) -> p a b", a=4),
               eg[:, hs_].unsqueeze(2).to_broadcast([4, 4, 128]), mult, [PB[4 + half], B_st], [B_o])
        tt("dve", t2, v4, qk.unsqueeze(2).to_broadcast([4, 8, 128]), mult, [B_vn, B_st], [B_hc])
        tt("dve", o4, o4, t2, add, [B_o, B_hc], [B_o])
        tt("dve", egd_t[0:4, :].rearrange("p (a b) -> p a b", a=4), eg.unsqueeze(1).to_broadcast([4, 4, 8]),
           ident[0:4, 0:4].unsqueeze(2).to_broadcast([4, 4, 8]), mult, [B_st, Bc], [B_EGB])
        kb.op("pe", mm(PS[6][:, 0:32], ones[0:4, :], egd_t[0:4, :]), [B_EGB, Bc], [PB[6]])
        cp("act", EGB, PS[6][:, 0:32], [PB[6]], [B_EGB])
        for b in range(4):
            ts("dve", vnm, v4.rearrange("p a b -> p (a b)"), ident[0:4, b:b + 1], None, mult, None, [B_vn, Bc], [B_hc])
            Sb = Sn[b % 2]
            for half in range(2):
                pj = 2 * (b % 2) + half
                kb.op("pe", [mm(PS[pj][:, k * 128:(k + 1) * 128], c4[:, 1024 + (4 * half + k) * 128:1024 + (4 * half + k + 1) * 128],
                                vnm[:, (4 * half + k) * 128:(4 * half + k + 1) * 128]) for k in range(4)], [B_cs, B_hc], [PB[pj]])
                for k in range(4):
                    h = 4 * half + k
                    stt("dve", Sb[:, h, :], S0t[:, b, h, :], EGB[:, b * 8 + h:b * 8 + h + 1], PS[pj][:, k * 128:(k + 1) * 128], mult, add,
                        [PB[pj], B_S0, B_EGB], [B_Sn[b % 2]])
            dma("sp", S_s.ap()[b].rearrange("h k v -> k h v"), Sb, [B_Sn[b % 2]], [B_out], "ss%d" % (b % 2))
        tt("dve", t1, o4, o4, mult, [B_o], [B_hc])
        red(ss[:, 16:24], t1, [B_hc], [B_st])
        rsq(ss[:, 16:24], tmp[:, 16:24], rn[:, 16:24], 1.0 / 128, B_st)
        tt("dve", o4, o4, rn[:, 16:24].unsqueeze(2).to_broadcast([4, 8, 128]), mult, [B_o, B_st], [B_o])
        tt("dve", o4, o4, gnw[0:4, :].unsqueeze(1).to_broadcast([4, 8, 128]), mult, [B_o, Bc], [B_o])
        dma("sp", zb4, H.ap()[R0:R0 + 4, O_ZB:O_ZB + 1024], [B_H], [B_hc], "zb")
        actf(zb4, zb4, AF.Silu, [B_hc], [B_hc])
        tt("dve", zb4, zb4, o4.rearrange("p a b -> p (a b)"), mult, [B_hc, B_o], [B_hc])
        dma("sp", OB.ap()[R0:R0 + 4, :], zb4, [B_hc], [B_OB], "ob")

    for seq in range(2):
        if "A" in PHASES:
            phase_A(seq)
            kb.barrier()
        if "NSA" in PHASES:
            phase_NSA(seq)
            kb.barrier()
        if "GDN" in PHASES:
            phase_GDN(seq)
            kb.barrier()
        dma("sp", conv_p.ap()[seq * 3:seq * 3 + 3, :], H.ap()[seq * T + T - 3:seq * T + T, O_QKV:O_QKV + 3072], [B_H], [B_out], "o0")
    if "SAMPLE" in PHASES:
        phase_SNSA()
        kb.barrier()
        phase_SGDN()
        kb.barrier()
    if "F" in PHASES:
        phase_F()
    nc = kb.finish()
    return nc


_NC = None


def kernel(**inp):
    global _NC
    f32 = np.float32
    if _NC is None:
        _NC = build()
    nc = _NC
    c32_h, cb_h = _HC[0], _HC[1]
    xp = np.asarray(inp["x_prompt"], dtype=f32)
    xs = np.asarray(inp["x_sample"], dtype=f32)
    g = lambda k: np.ascontiguousarray(np.asarray(inp[k], dtype=f32)[0])
    shared = dict(
        w_in=g("w_in"), norm_w=g("norm_w").reshape(1, DM), c32=c32_h, cbs=cb_h,
        q_norm_w=g("q_norm_w").reshape(1, 128), k_norm_cmp_w=g("k_norm_cmp_w").reshape(1, 128),
        k_norm_slc_w=g("k_norm_slc_w").reshape(1, 128), k_norm_win_w=g("k_norm_win_w").reshape(1, 128),
        cmp_w1_k=g("cmp_w1_k").reshape(32, 128, 128), cmp_w1_v=g("cmp_w1_v").reshape(32, 128, 128),
        cmp_b1_k=g("cmp_b1_k").reshape(1, 128), cmp_b1_v=g("cmp_b1_v").reshape(1, 128),
        cmp_w2_k=g("cmp_w2_k"), cmp_w2_v=g("cmp_w2_v"),
        conv_w=g("conv_w").reshape(1, 4 * 3072), a_log=g("a_log").reshape(1, 8), dt_bias=g("dt_bias").reshape(1, 8),
        gdn_norm_w=g("gdn_norm_w").reshape(1, 128), w_pa=g("w_pa"), w_pb=g("w_pb"), w_o=g("w_o"),
    )
    ccmp = np.asarray(inp["cache_cmp_kv"], dtype=f32).reshape(5120 * 32, 2048)
    cslc = np.asarray(inp["cache_slc_kv"], dtype=f32).reshape(5120 * 32, 2048)
    stw = np.asarray(inp["state_win_kv"], dtype=f32).reshape(32 * 512, 512)
    ptab = np.asarray(inp["page_table"]).astype(np.int32)
    stc = np.asarray(inp["state_gdn_conv"], dtype=f32)[0]
    sS = np.asarray(inp["state_gdn_S"], dtype=f32)[0]
    shared.update(ccmp=ccmp, cslc=cslc, cs32=_HS[0], csb=_HS[1])
    in_maps = []
    for c in range(NCORES):
        m = dict(shared)
        m["stwin"] = np.ascontiguousarray(stw[4 * c * 512:(4 * c + 4) * 512])
        m["ptab"] = np.ascontiguousarray(ptab[4 * c:4 * c + 4].reshape(4, 32, 4).transpose(2, 0, 1)).reshape(4, 128)
        m["stconv"] = np.ascontiguousarray(stc[4 * c:4 * c + 4])
        m["S0"] = np.ascontiguousarray(sS[4 * c:4 * c + 4])
        m["x_p"] = np.ascontiguousarray(xp[2 * c:2 * c + 2]).reshape(2 * T, DM)
        m["x_s"] = np.ascontiguousarray(xs[4 * c:4 * c + 4]).reshape(4, DM)
        in_maps.append(m)
    res = run_bass_kernel_spmd(nc, in_maps, core_ids=list(range(NCORES)))
    R = res.results
    cat = lambda k: np.concatenate([r[k] for r in R], axis=0)
    y_p = cat("y_p").reshape(16, T, DM)
    cmp_pv = cat("cmp_p").reshape(1, 16, T, 2, 2, 128)
    slc_pv = cat("slc_p").reshape(1, 16, T, 2, 2, 128)
    win_pv = cat("win_p").reshape(1, 16, 512, 2, 2, 128)
    conv_pv = cat("conv_p").reshape(1, 16, 3, 3072)
    if DEBUG:
        global _DBG
        _DBG = dict(OA=R[0]["OAs"], OB=R[0]["OBs"])
    S_pv = np.stack([r["S_p"] for r in R], axis=0).reshape(1, 16, 8, 128, 128)
    y_sv = cat("y_s").reshape(32, 1, DM)
    cmp_sv = cat("cmp_s").reshape(1, 32, 1, 2, 2, 128)
    slc_sv = cat("slc_s").reshape(1, 32, 1, 2, 2, 128)
    win_sv = cat("win_s").reshape(1, 32, 512, 2, 2, 128)
    S_sv = cat("S_s").reshape(1, 32, 8, 128, 128)
    conv_sv = cat("conv_s").reshape(1, 32, 3, 3072)
    return (y_p, y_sv, cmp_pv, slc_pv, win_pv, S_pv, conv_pv, cmp_sv, slc_sv, win_sv, S_sv, conv_sv)
```

```python
import numpy as np
from contextlib import ExitStack
import concourse.bass as bass
import concourse.mybir as mybir
from concourse.bass_utils import run_bass_kernel_spmd

F32 = mybir.dt.float32
BF16 = mybir.dt.bfloat16
I32 = mybir.dt.int32
AF = mybir.ActivationFunctionType
ALU = mybir.AluOpType
AX = mybir.AxisListType

ENGS = ("pe", "act", "dve", "pool", "sp")
NCORES = 8
T = 2048
DM = 2048
INW = 11816
NT = T // 128
EPS = 1e-6
SCALE = 128 ** -0.5
O_Q, O_KVC, O_KVS, O_KVW, O_GN, O_ZA, O_QKV, O_A, O_B, O_ZB, O_GA, O_GB = (
    0, 1024, 1536, 2048, 2560, 2584, 3608, 6680, 6688, 6696, 7720, 9768)
NTOK = 2 * T + 128


class Buf:
    __slots__ = ("name", "w", "r")

    def __init__(self, name):
        self.name = name
        self.w = None
        self.r = {}


class KB:
    def __init__(self):
        self.nc = bass.Bass("TRN2", target_bir_lowering=False)
        self.es = ExitStack()
        self.q = {e: [] for e in ENGS}
        self.cnt = {e: 0 for e in ENGS}
        self.esem = {}
        self.dsem = {}
        self.dcnt = {}
        self.seen = {e: {} for e in ENGS}
        self.nbuf = 0
        self.rr = 0
        for e in ENGS:
            self.esem[e] = self.es.enter_context(self.nc.semaphore("s_" + e))

    def sb(self, name, shape, dt=F32):
        return self.es.enter_context(self.nc.sbuf_tensor(name, list(shape), dt))

    def ps(self, name, shape, dt=F32):
        return self.es.enter_context(self.nc.psum_tensor(name, list(shape), dt))

    def buf(self, name=None):
        self.nbuf += 1
        return Buf(name or "b%d" % self.nbuf)

    def bufs(self, n):
        return [self.buf() for _ in range(n)]

    def _sem(self, key):
        return self.esem[key] if key in self.esem else self.dsem[key]

    def _wait(self, eng, k, v):
        if k in self.dcnt:
            v = max(v, self.dcnt[k])
        if self.seen[eng].get(k, 0) >= v:
            return
        self.seen[eng][k] = v
        sem = self._sem(k)
        self.q[eng].append(lambda e, sem=sem, v=v: e.wait_ge(sem, v))

    def _deps(self, eng, reads, writes):
        need = {}

        def add(k, v):
            if need.get(k, 0) < v:
                need[k] = v
        for b in reads:
            if b.w is not None:
                add(*b.w)
        for b in writes:
            if b.w is not None:
                add(*b.w)
            for k, v in b.r.items():
                add(k, v)
        for k, v in need.items():
            if k == eng and eng == "pe":
                continue
            self._wait(eng, k, v)

    def _mark(self, ev, reads, writes):
        k, v = ev
        for b in reads:
            b.r[k] = v
        for b in writes:
            b.w = ev
            b.r = {}

    def op(self, eng, fns, reads=(), writes=()):
        if callable(fns):
            fns = [fns]
        self._deps(eng, reads, writes)
        self.cnt[eng] += 1
        sem = self.esem[eng]
        for f in fns[:-1]:
            self.q[eng].append(f)
        last = fns[-1]
        self.q[eng].append(lambda e, f=last, sem=sem: f(e).then_inc(sem, 1))
        self._mark((eng, self.cnt[eng]), reads, writes)

    def dma(self, eng, fn, reads=(), writes=(), chan=None):
        if chan not in self.dsem:
            self.dsem[chan] = self.es.enter_context(self.nc.semaphore("d_" + chan))
            self.dcnt[chan] = 0
        self._deps(eng, reads, writes)
        self.dcnt[chan] += 16
        sem = self.dsem[chan]
        self.q[eng].append(lambda e, f=fn, sem=sem: f(e).then_inc(sem, 16))
        self._mark((chan, self.dcnt[chan]), reads, writes)

    def barrier(self):
        for e in ENGS:
            for k in ENGS:
                if k != e and self.cnt[k] > 0:
                    self._wait(e, k, self.cnt[k])
            for k in list(self.dcnt):
                if self.dcnt[k] > 0:
                    self._wait(e, k, self.dcnt[k])

    def finish(self):
        self.barrier()
        nc = self.nc
        with nc.Block() as block:
            @block.sync
            def _(e):
                for f in self.q["sp"]:
                    f(e)

            @block.tensor
            def _(e):
                for f in self.q["pe"]:
                    f(e)

            @block.scalar
            def _(e):
                for f in self.q["act"]:
                    f(e)

            @block.vector
            def _(e):
                for f in self.q["dve"]:
                    f(e)

            @block.gpsimd
            def _(e):
                for f in self.q["pool"]:
                    f(e)
        self.es.close()
        return nc


class Arena:
    def __init__(self, t, words):
        self.t = t
        self.words = words
        self.off = 0

    def reset(self):
        self.off = 0

    def get(self, shape, dt=F32, parts=128):
        n = 1
        for s in shape:
            n *= s
        w = n if dt == F32 or dt == I32 else (n + 1) // 2
        w = (w + 1) // 2 * 2
        assert self.off + w <= self.words, ("arena overflow", self.off, w, self.words)
        v = self.t[0:parts, self.off:self.off + w]
        self.off += w
        if dt != F32:
            v = v.bitcast(dt)
        v = v[:, 0:n]
        if len(shape) == 2:
            return v.rearrange("p (a b) -> p a b", a=shape[0])
        if len(shape) == 3:
            return v.rearrange("p (a b c) -> p a b c", a=shape[0], b=shape[1])
        if len(shape) == 4:
            return v.rearrange("p (a b c d) -> p a b c d", a=shape[0], b=shape[1], c=shape[2])
        return v


NBIG = -400000.0
SLOPES = [2.0 ** -(h + 1) for h in range(8)]


def host_consts():
    f32 = np.float32
    sl = np.array(SLOPES, dtype=np.float64)
    C32 = {}
    CB = {}
    C32["ident"] = np.eye(128)
    kend = 16 * np.arange(128) + 31
    abc = np.full((128, 16, 8), -30000.0)
    maskc = np.zeros((128, 16, 128))
    for qt in range(16):
        vis = (kend <= 128 * qt + 127) & (np.arange(128) < 127)
        for h in range(8):
            abc[:, qt, h] = np.where(vis, sl[h] * (kend - 128 * qt), -30000.0)
        maskc[:, qt, :] = (kend[:, None] <= 128 * qt + np.arange(128)[None, :]) & (np.arange(128) < 127)[:, None]
    C32["abc"] = abc.reshape(128, 128)
    a1 = np.zeros((128, 16, 32))
    a2 = np.zeros((128, 16, 32))
    for qt in range(16):
        q = 128 * qt + np.arange(128)[:, None]
        j = np.arange(32)[None, :]
        vis = (j * 64 <= q)
        cur = q // 64
        forced = (j == 0) | ((cur - j >= 0) & (cur - j < 2))
        a1[:, qt, :] = (vis & ~forced)
        a2[:, qt, :] = np.where(vis, np.where(forced, 1e4, 0.0), -1e4)
    C32["a1"] = a1.reshape(128, 512)
    C32["a2"] = a2.reshape(128, 512)
    ab2 = np.zeros((128, 16, 8))
    for rel in range(-15, 1):
        for h in range(8):
            ab2[:, rel + 15, h] = sl[h] * (np.arange(128) + 128 * rel)
    C32["ab2"] = ab2.reshape(128, 128)
    ii = np.arange(128)
    C32["ut"] = (ii[:, None] <= ii[None, :]).astype(f32)
    C32["ones"] = np.ones((128, 128))
    C32["e4"] = np.tile(np.eye(4).reshape(1, 16), (128, 1))
    C32["mskl"] = np.tile(np.where(ii[:, None] <= ii[None, :], 30000.0, 0.0), (1, 4))
    C32["msklt"] = np.tile(np.where(ii[:, None] > ii[None, :], -30000.0, 0.0), (1, 4))
    CB["maskc"] = maskc.reshape(128, 2048)
    cs = np.arange(127) * 16
    ss = np.arange(32) * 64
    ov = np.minimum(cs[:, None] + 32, ss[None] + 64) - np.maximum(cs[:, None], ss[None])
    sm = np.zeros((128, 32))
    sm[:127] = np.clip(ov, 0, None) / 32.0
    CB["selmap"] = sm
    ex = np.zeros((128, 2048))
    ex[:32] = (np.arange(2048)[None, :] // 64 == np.arange(32)[:, None])
    CB["expd"] = ex
    k = np.arange(128)[:, None]
    i = np.arange(128)[None, :]
    CB["caus"] = np.tile(np.where(k > i, NBIG, 0.0), (1, 4))
    CB["winm"] = np.tile(np.where(k < i, NBIG, 0.0), (1, 4))
    CB["identb"] = np.eye(128)
    CB["onesb"] = np.ones((128, 2))
    o32, ob = {}, {}
    off = 0
    for kname, v in C32.items():
        o32[kname] = (off, v.shape[1])
        off += v.shape[1]
    n32 = off
    off = 0
    for kname, v in CB.items():
        ob[kname] = (off, v.shape[1])
        off += v.shape[1]
    nb = off
    c32 = np.concatenate([v for v in C32.values()], axis=1).astype(f32)
    cb = np.concatenate([v for v in CB.values()], axis=1).astype(f32)
    return c32, cb, o32, ob, n32, nb


def host_consts_s():
    f32 = np.float32
    sl = np.array(SLOPES, dtype=np.float64)
    C = {}
    abs_ = np.full((128, 8, 8), -30000.0)
    for it in range(8):
        n = 128 * it + np.arange(128)
        kend = 16 * n + 31
        for h in range(8):
            abs_[:, it, h] = np.where(n <= 1022, -sl[h] * (16384 - kend), -30000.0)
    C["abs"] = abs_.reshape(128, 64)
    pp = np.arange(128)[:, None]
    tl = np.arange(128)[None, :]
    C["distd"] = (16384 - (512 * (tl // 4) + 4 * pp + (tl % 4))).astype(np.float64)
    C["distw"] = (512 - (128 * np.arange(4)[None, :] + pp)).astype(np.float64)
    rowc = np.zeros((128, 514 + 1024))
    j = np.arange(257)
    forced = (j == 0) | (j == 255) | (j == 256)
    rowc[:, 0:257] = 1.0 - forced
    rowc[:, 257:514] = 1e4 * forced
    for c_ in range(8):
        rowc[:, 514 + 128 * c_:514 + 128 * (c_ + 1)] = (np.arange(128) // 16 == c_)
    C["rowc"] = rowc
    C["iotap"] = (np.arange(128) % 32).astype(np.float64)[:, None]
    off = 0
    o = {}
    for kn, v in C.items():
        o[kn] = (off, v.shape[1])
        off += v.shape[1]
    cs32 = np.concatenate(list(C.values()), axis=1).astype(f32)
    n = np.arange(1024)
    cs_ = 16 * n
    ss = 64 * np.arange(257)
    ov = np.minimum(cs_[:, None] + 32, ss[None] + 64) - np.maximum(cs_[:, None], ss[None])
    sm = np.clip(ov, 0, None) / 32.0
    sm[1023] = 0.0
    sels = sm.reshape(8, 128, 257).transpose(1, 0, 2).reshape(128, 8 * 257).astype(f32)
    return cs32, sels, o, off


_HS = host_consts_s()

_HC = host_consts()
DEBUG = False
PHASES = {"A", "NSA", "GDN", "SAMPLE", "F"}


def build():
    kb = KB()
    nc = kb.nc
    c32_h, cb_h, o32, ob, n32, nb = _HC
    cs32_h, sels_h, os32, ns32 = _HS
    mult, add, sub = ALU.mult, ALU.add, ALU.subtract

    def inp(name, shape, dt=F32):
        return nc.dram_tensor(name, list(shape), dt, kind="ExternalInput")

    def outp(name, shape, dt=F32):
        return nc.dram_tensor(name, list(shape), dt, kind="ExternalOutput")

    x_p = inp("x_p", [2 * T, DM])
    x_s = inp("x_s", [4, DM])
    w_in = inp("w_in", [DM, INW])
    norm_w = inp("norm_w", [1, DM])
    c32_d = inp("c32", [128, n32])
    cb_d = inp("cbs", [128, nb])
    qnw_d = inp("q_norm_w", [1, 128])
    knwc_d = inp("k_norm_cmp_w", [1, 128])
    knws_d = inp("k_norm_slc_w", [1, 128])
    knww_d = inp("k_norm_win_w", [1, 128])
    w1k_d = inp("cmp_w1_k", [32, 128, 128])
    w1v_d = inp("cmp_w1_v", [32, 128, 128])
    b1k_d = inp("cmp_b1_k", [1, 128])
    b1v_d = inp("cmp_b1_v", [1, 128])
    w2k_d = inp("cmp_w2_k", [128, 128])
    w2v_d = inp("cmp_w2_v", [128, 128])
    convw_d = inp("conv_w", [1, 4 * 3072])
    alog_d = inp("a_log", [1, 8])
    dtb_d = inp("dt_bias", [1, 8])
    gnw_d = inp("gdn_norm_w", [1, 128])
    wpa_d = inp("w_pa", [1024, 2048])
    wpb_d = inp("w_pb", [1024, 2048])
    wo_d = inp("w_o", [2048, 2048])
    ccmp_d = inp("ccmp", [5120 * 32, 2048])
    cslc_d = inp("cslc", [5120 * 32, 2048])
    stwin_d = inp("stwin", [4 * 512, 512])
    pt_d = inp("ptab", [4, 128], I32)
    stconv_d = inp("stconv", [4, 3, 3072])
    S0_d = inp("S0", [4, 8, 128, 128])
    cs32_d = inp("cs32", [128, ns32])
    csb_d = inp("csb", [128, 2056])

    y_p = outp("y_p", [2 * T, DM])
    cmp_p = outp("cmp_p", [2 * T, 512])
    slc_p = outp("slc_p", [2 * T, 512])
    win_p = outp("win_p", [2 * 512, 512])
    conv_p = outp("conv_p", [2 * 3, 3072])
    S_p = outp("S_p", [2, 8, 128, 128])
    y_s = outp("y_s", [4, DM])
    cmp_s = outp("cmp_s", [4, 512])
    slc_s = outp("slc_s", [4, 512])
    win_s = outp("win_s", [4 * 512, 512])
    conv_s = outp("conv_s", [4, 3, 3072])
    S_s = outp("S_s", [4, 8, 128, 128])
    NK = nc.dram_tensor("NKs", [4, 1536], F32, kind="Internal")
    USC = nc.dram_tensor("USCs", [4, 3, 8, 128], F32, kind="Internal")
    H = nc.dram_tensor("Hs", [NTOK, INW], F32, kind="Internal")
    OA = nc.dram_tensor("OAs", [NTOK, 1024], F32, kind="ExternalOutput" if DEBUG else "Internal")
    OB = nc.dram_tensor("OBs", [NTOK, 1024], F32, kind="ExternalOutput" if DEBUG else "Internal")

    def cp(eng, out, in_, r, w):
        if eng == "act":
            kb.op("act", lambda e: e.copy(out=out, in_=in_), r, w)
        else:
            kb.op(eng, lambda e: e.tensor_copy(out=out, in_=in_), r, w)

    def tt(eng, out, a, b, op, r, w):
        kb.op(eng, lambda e: e.tensor_tensor(out=out, in0=a, in1=b, op=op), r, w)

    def ts(eng, out, a, s1, s2, op0, op1, r, w):
        if s2 is None:
            kb.op(eng, lambda e: e.tensor_scalar(out=out, in0=a, scalar1=s1, scalar2=None, op0=op0), r, w)
        else:
            kb.op(eng, lambda e: e.tensor_scalar(out=out, in0=a, scalar1=s1, scalar2=s2, op0=op0, op1=op1), r, w)

    def stt(eng, out, a, s, b, op0, op1, r, w):
        kb.op(eng, lambda e: e.scalar_tensor_tensor(out=out, in0=a, scalar=s, in1=b, op0=op0, op1=op1), r, w)

    def actf(out, in_, func, r, w, bias=None, scale=1.0):
        if bias is None:
            kb.op("act", lambda e: e.activation(out=out, in_=in_, func=func, scale=scale), r, w)
        else:
            kb.op("act", lambda e: e.activation(out=out, in_=in_, func=func, bias=bias, scale=scale), r, w)

    def red(out, in_, r, w, op=ALU.add):
        kb.op("dve", lambda e: e.tensor_reduce(out=out, in_=in_, axis=AX.X, op=op), r, w)

    def recip(out, in_, r, w):
        kb.op("dve", lambda e: e.reciprocal(out=out, in_=in_), r, w)

    def rsq(ss, tmp, out, c, B):
        ts("dve", tmp, ss, c, EPS, mult, add, [B], [B])
        actf(tmp, tmp, AF.Sqrt, [B], [B])
        recip(out, tmp, [B], [B])

    def mm(out, lhsT, rhs, start=True, stop=True):
        if rhs is ident_full[0] and start and stop:
            return lambda e: e.transpose(out, lhsT, rhs)
        return lambda e: e.matmul(out, lhsT=lhsT, rhs=rhs, start=start, stop=stop)
    ident_full = [None]

    def dma(eng, out, in_, r, w, chan):
        kb.dma(eng, lambda e: e.dma_start(out=out, in_=in_), r, w, chan)

    Bc = kb.buf("consts")
    c32 = kb.sb("c32s", [128, n32])
    cbt = kb.sb("cbts", [128, nb], BF16)
    dma("sp", c32[:], c32_d.ap(), [], [Bc], "c0")

    def C(name):
        o, n = o32[name]
        return c32[:, o:o + n]

    def CBv(name):
        o, n = ob[name]
        return cbt[:, o:o + n]
    ident = C("ident")
    ident_full[0] = ident
    identb = CBv("identb")
    qnw = kb.sb("qnw", [128, 1])
    b1k = kb.sb("b1k", [128, 1])
    b1v = kb.sb("b1v", [128, 1])
    knwc = kb.sb("knwc", [128, 128])
    knws = kb.sb("knws", [128, 128])
    knww = kb.sb("knww", [128, 128])
    w1k = kb.sb("w1k", [128, 32, 128], BF16)
    w1v = kb.sb("w1v", [128, 32, 128], BF16)
    w2k = kb.sb("w2k", [128, 128], BF16)
    w2v = kb.sb("w2v", [128, 128], BF16)
    dma("sp", qnw[:], qnw_d.ap().rearrange("o d -> d o"), [], [Bc], "c0")
    dma("sp", b1k[:], b1k_d.ap().rearrange("o d -> d o"), [], [Bc], "c0")
    dma("sp", b1v[:], b1v_d.ap().rearrange("o d -> d o"), [], [Bc], "c0")
    dma("sp", knwc[:], knwc_d.ap().partition_broadcast(128), [], [Bc], "c0")
    dma("sp", knws[:], knws_d.ap().partition_broadcast(128), [], [Bc], "c0")
    dma("sp", knww[:], knww_d.ap().partition_broadcast(128), [], [Bc], "c0")
    gnw = kb.sb("gnw", [128, 128])
    nea = kb.sb("nea", [128, 8])
    dtb = kb.sb("dtb", [128, 8])
    dma("sp", gnw[:], gnw_d.ap().partition_broadcast(128), [], [Bc], "c0")
    dma("sp", nea[:], alog_d.ap().partition_broadcast(128), [], [Bc], "c0")
    dma("sp", dtb[:], dtb_d.ap().partition_broadcast(128), [], [Bc], "c0")
    kb.op("act", lambda e: e.activation(out=nea[:], in_=nea[:], func=AF.Exp), [Bc], [Bc])
    kb.op("dve", lambda e: e.tensor_scalar(out=nea[:], in0=nea[:], scalar1=-1.0, scalar2=None, op0=ALU.mult), [Bc], [Bc])

    ARW = 43008
    arena_t = kb.sb("arena", [128, ARW])
    ar = Arena(arena_t, ARW)
    PS = [kb.ps("ps%d" % i, [128, 512]) for i in range(8)]
    PB = kb.bufs(8)

    ar.reset()
    stg = ar.get([nb])
    Bstg = kb.buf()
    dma("sp", stg, cb_d.ap(), [], [Bstg], "c1")
    cp("dve", cbt[:], stg, [Bstg], [Bc])
    stw = ar.get([32, 128])
    for (wd, wt_) in ((w1k_d, w1k), (w1v_d, w1v)):
        dma("sp", stw, wd.ap().rearrange("s d e -> d s e"), [], [Bstg], "c1")
        cp("dve", wt_[:], stw, [Bstg], [Bc])
    for (wd, wt_) in ((w2k_d, w2k), (w2v_d, w2v)):
        dma("sp", stw[:, 0, :], wd.ap(), [], [Bstg], "c1")
        cp("dve", wt_[:], stw[:, 0, :], [Bstg], [Bc])
    kb.barrier()

    B_H = kb.buf("H")
    B_out = kb.buf("outs")
    B_OA = kb.buf("OA")
    B_OB = kb.buf("OB")

    def phase_A(seq):
        ar.reset()
        ntile = NT + (1 if seq == 0 else 0)
        ncol = ntile * 128
        xnT = ar.get([16, ncol], BF16)
        B_xnT = kb.buf()
        nwb = ar.get([DM])
        B_nwb = kb.buf()
        dma("sp", nwb, norm_w.ap().partition_broadcast(128), [], [B_nwb], "c0")
        xt = [ar.get([DM]) for _ in range(2)]
        B_xt = kb.bufs(2)
        junk = ar.get([DM], BF16)
        B_junk = kb.buf()
        st = [ar.get([4]) for _ in range(2)]
        B_st = kb.bufs(2)
        diag = [ar.get([128]) for _ in range(2)]
        B_dg = kb.bufs(2)
        wst = [ar.get([4, 512]) for _ in range(2)]
        B_wst = kb.bufs(2)
        wb = [ar.get([16, 512], BF16) for _ in range(2)]
        B_wb = kb.bufs(2)
        ho = [ar.get([512]) for _ in range(4)]
        B_ho = kb.bufs(4)
        for tti in range(ntile):
            i = tti % 2
            n_ = 128 if tti < NT else 4
            src = x_p.ap()[seq * T + tti * 128: seq * T + tti * 128 + 128, :] if tti < NT else x_s.ap()
            dma("sp", xt[i][0:n_, :], src, [], [B_xt[i]], "xt%d" % i)
            kb.op("dve", lambda e, i=i: e.memset(st[i][:, 0:1], 0.0), [], [B_st[i]])
            kb.op("act", lambda e, i=i, n_=n_: e.activation(out=junk[0:n_, :], in_=xt[i][0:n_, :], func=AF.Square,
                                                             accum_out=st[i][0:n_, 0:1]),
                  [B_xt[i], B_st[i]], [B_junk, B_st[i]])
            rsq(st[i][0:n_, 0:1], st[i][0:n_, 1:2], st[i][0:n_, 3:4], 1.0 / DM, B_st[i])
            ts("dve", diag[i][0:n_, 0:n_], ident[0:n_, 0:n_], st[i][0:n_, 3:4], None, mult, None, [B_st[i], Bc], [B_dg[i]])
            tt("pool", xt[i][0:n_, :], xt[i][0:n_, :], nwb[0:n_, :], mult, [B_xt[i], B_nwb], [B_xt[i]])
            for j in range(4):
                pj = j
                kb.op("pe", [mm(PS[pj][:, k * n_:(k + 1) * n_], xt[i][0:n_, (4 * j + k) * 128:(4 * j + k + 1) * 128],
                                diag[i][0:n_, 0:n_]) for k in range(4)],
                      [B_xt[i], B_dg[i]], [PB[pj]])
                dst = xnT[:, 4 * j:4 * j + 4, tti * 128:tti * 128 + n_]
                srcp = PS[pj][:, 0:4 * n_].rearrange("p (a b) -> p a b", a=4)
                cp("act" if j % 2 == 0 else "dve", dst, srcp, [PB[pj]], [B_xnT])
        ncb = (INW + 511) // 512
        hcnt = 0
        for cb in range(ncb):
            c0 = cb * 512
            cw = min(512, INW - c0)
            wi = cb % 2
            for qf in range(4):
                hf = qf % 2
                srcw = w_in.ap()[qf * 512:(qf + 1) * 512, c0:c0 + cw].rearrange("(kc p) n -> p kc n", p=128)
                dma("sp" if hf == 0 else "act", wst[hf][:, :, 0:cw], srcw, [], [B_wst[hf]], "wst%d" % hf)
                cp("dve" if hf == 0 else "pool", wb[wi][:, qf * 4:(qf + 1) * 4, 0:cw], wst[hf][:, :, 0:cw], [B_wst[hf]], [B_wb[wi]])
            for tti in range(ntile):
                n_ = 128 if tti < NT else 4
                pj = 4 + (hcnt % 4)
                hi = hcnt % 4
                hcnt += 1
                kb.op("pe", [mm(PS[pj][0:n_, 0:cw], xnT[:, kc, tti * 128:tti * 128 + n_], wb[wi][:, kc, 0:cw],
                                start=(kc == 0), stop=(kc == 15)) for kc in range(16)],
                      [B_xnT, B_wb[wi]], [PB[pj]])
                cp("act" if hcnt % 2 == 0 else "dve", ho[hi][0:n_, 0:cw], PS[pj][0:n_, 0:cw], [PB[pj]], [B_ho[hi]])
                r0 = seq * T + tti * 128 if tti < NT else 2 * T
                dma("sp", H.ap()[r0:r0 + n_, c0:c0 + cw], ho[hi][0:n_, 0:cw], [B_ho[hi]], [B_H], "ho%d" % hi)

    def phase_NSA(seq):
        ar.reset()
        QT = ar.get([8, T], BF16)
        CT = ar.get([4, T], BF16)
        KsT = ar.get([2, T], BF16)
        KwT = ar.get([2, T], BF16)
        VS = ar.get([NT, 2, 128], BF16)
        VW = ar.get([NT, 2, 128], BF16)
        GS = ar.get([NT, 24])
        B_QT = kb.bufs(NT)
        B_CT = kb.buf()
        B_KV = kb.bufs(NT)
        hq = [ar.get([2584]) for _ in range(2)]
        B_hq = kb.bufs(2)
        sqb = ar.get([1536])
        B_sq = kb.buf()
        stt_ = [ar.get([40]) for _ in range(2)]
        B_stt = kb.bufs(2)
        KcT = ar.get([2, 128], BF16)
        VCX = ar.get([2, 162], BF16)
        B_KC = kb.buf()
        for t_ in range(NT):
            i = t_ % 2
            r0 = seq * T + t_ * 128
            c_ = slice(t_ * 128, (t_ + 1) * 128)
            h = hq[i]
            Bh = B_hq[i]
            S = stt_[i]
            Bs = B_stt[i]
            dma("sp", h, H.ap()[r0:r0 + 128, 0:2584], [B_H], [Bh], "hq%d" % i)
            dma("sp", cmp_p.ap()[r0:r0 + 128, :], h[:, O_KVC:O_KVC + 512], [Bh], [B_out], "o%d" % i)
            tt("dve", sqb[:, 0:1024], h[:, 0:1024], h[:, 0:1024], mult, [Bh], [B_sq])
            red(S[:, 0:8], sqb[:, 0:1024].rearrange("p (a d) -> p a d", a=8), [B_sq], [Bs])
            tt("pool", sqb[:, 1024:1280], h[:, O_KVS:O_KVS + 256], h[:, O_KVS:O_KVS + 256], mult, [Bh], [B_sq])
            tt("pool", sqb[:, 1280:1536], h[:, O_KVW:O_KVW + 256], h[:, O_KVW:O_KVW + 256], mult, [Bh], [B_sq])
            red(S[:, 8:12], sqb[:, 1024:1536].rearrange("p (a d) -> p a d", a=4), [B_sq], [Bs])
            rsq(S[:, 0:12], S[:, 12:24], S[:, 24:36], 1.0 / 128, Bs)
            tt("dve", h[:, 0:1024].rearrange("p (a d) -> p a d", a=8), h[:, 0:1024].rearrange("p (a d) -> p a d", a=8),
               S[:, 24:32].unsqueeze(2).to_broadcast([128, 8, 128]), mult, [Bh, Bs], [Bh])
            for g in range(2):
                stt("dve", h[:, O_KVS + 128 * g:O_KVS + 128 * g + 128], h[:, O_KVS + 128 * g:O_KVS + 128 * g + 128],
                    S[:, 32 + g:33 + g], knws[:], mult, mult, [Bh, Bs, Bc], [Bh])
                stt("dve", h[:, O_KVW + 128 * g:O_KVW + 128 * g + 128], h[:, O_KVW + 128 * g:O_KVW + 128 * g + 128],
                    S[:, 34 + g:35 + g], knww[:], mult, mult, [Bh, Bs, Bc], [Bh])
            dma("sp", slc_p.ap()[r0:r0 + 128, :], h[:, O_KVS:O_KVS + 512], [Bh], [B_out], "o%d" % i)
            if t_ >= NT - 4:
                rw = seq * 512 + (t_ - (NT - 4)) * 128
                dma("sp", win_p.ap()[rw:rw + 128, :], h[:, O_KVW:O_KVW + 512], [Bh], [B_out], "o%d" % i)
            for j in range(2):
                kb.op("pe", [mm(PS[j][:, k * 128:(k + 1) * 128], h[:, (4 * j + k) * 128:(4 * j + k + 1) * 128], ident) for k in range(4)],
                      [Bh, Bc], [PB[j]])
                ts("dve", QT[:, 4 * j:4 * j + 4, c_], PS[j][:].rearrange("p (a b) -> p a b", a=4), qnw[:, 0:1], None, mult, None,
                   [PB[j], Bc], [B_QT[t_]])
            kb.op("pe", [mm(PS[2][:, k * 128:(k + 1) * 128], h[:, O_KVC + k * 128:O_KVC + (k + 1) * 128], ident) for k in range(4)],
                  [Bh, Bc], [PB[2]])
            cp("act", CT[:, 0:4, c_], PS[2][:].rearrange("p (a b) -> p a b", a=4), [PB[2]], [B_CT])
            kb.op("pe", [mm(PS[3][:, k * 128:(k + 1) * 128], h[:, (O_KVS if k < 2 else O_KVW) + (k % 2) * 128:(O_KVS if k < 2 else O_KVW) + (k % 2) * 128 + 128], ident)
                         for k in range(4)], [Bh, Bc], [PB[3]])
            cp("act", KsT[:, 0:2, c_], PS[3][:, 0:256].rearrange("p (a b) -> p a b", a=2), [PB[3]], [B_KV[t_]])
            cp("act", KwT[:, 0:2, c_], PS[3][:, 256:512].rearrange("p (a b) -> p a b", a=2), [PB[3]], [B_KV[t_]])
            cp("pool", VS[:, t_, :, :], h[:, O_KVS + 256:O_KVS + 512].rearrange("p (a b) -> p a b", a=2), [Bh], [B_KV[t_]])
            cp("pool", VW[:, t_, :, :], h[:, O_KVW + 256:O_KVW + 512].rearrange("p (a b) -> p a b", a=2), [Bh], [B_KV[t_]])
            actf(GS[:, t_, :], h[:, O_GN:O_GN + 24], AF.Sigmoid, [Bh], [B_KV[t_]])
        hs = ar.get([127], BF16)
        B_hs = kb.buf()
        kcw = ar.get([128])
        B_kcw = kb.buf()
        cst = ar.get([8])
        B_cst = kb.buf()
        for kvg in range(4):
            isk = kvg < 2
            g = kvg % 2
            w1 = w1k if isk else w1v
            kb.op("pe", [mm(PS[4][:, 0:127], w1[:, s_, :], CT[:, kvg, s_:s_ + 2017:16], start=(s_ == 0), stop=(s_ == 31)) for s_ in range(32)],
                  [B_CT, Bc], [PB[4]])
            actf(hs[:, :], PS[4][:, 0:127], AF.Silu, [PB[4], Bc], [B_hs], bias=(b1k if isk else b1v)[:, 0:1])
            kb.op("pe", mm(PS[5][0:127, 0:128], hs[:, :], (w2k if isk else w2v)[:]), [B_hs, Bc], [PB[5]])
            if isk:
                cp("act", kcw[0:127, :], PS[5][0:127, 0:128], [PB[5]], [B_kcw])
                tt("dve", sqb[0:127, 0:128], kcw[0:127, :], kcw[0:127, :], mult, [B_kcw], [B_sq])
                red(cst[0:127, 0:1], sqb[0:127, 0:128], [B_sq], [B_cst])
                rsq(cst[0:127, 0:1], cst[0:127, 1:2], cst[0:127, 2:3], 1.0 / 128, B_cst)
                stt("dve", kcw[0:127, :], kcw[0:127, :], cst[0:127, 2:3], knwc[0:127, :], mult, mult, [B_kcw, B_cst, Bc], [B_kcw])
                kb.op("pe", mm(PS[6][:, 0:127], kcw[0:127, :], ident[0:127, 0:127]), [B_kcw, Bc], [PB[6]])
                cp("act", KcT[:, g, 0:127], PS[6][:, 0:127], [PB[6]], [B_KC])
            else:
                cp("act", VCX[0:127, g, 0:128], PS[5][0:127, 0:128], [PB[5]], [B_KC])
                cp("dve", VCX[0:127, g, 128:160], CBv("selmap")[0:127, :], [Bc], [B_KC])
                cp("dve", VCX[0:127, g, 160:162], CBv("onesb")[0:127, :], [Bc], [B_KC])
        oacc = ar.get([8, 128])
        B_oacc = kb.buf()
        ec = ar.get([4, 128], BF16)
        B_ec = kb.buf()
        es = [ar.get([NT, 4, 128], BF16) for _ in range(2)]
        B_es = kb.bufs(2)
        ew = ar.get([5, 4, 128], BF16)
        B_ew = kb.buf()
        NS = ar.get([2, 4, 128], BF16)
        B_NS = kb.buf()
        kb.op("pool", lambda e: e.memset(NS, 0.0), [], [B_NS])
        za = ar.get([1024])
        B_za = kb.buf()
        sm = ar.get([256])
        B_sm = kb.buf()
        imp = sm[:, 0:64].rearrange("p (a b) -> p a b", a=2)
        sc = sm[:, 64:128].rearrange("p (a b) -> p a b", a=2)
        sc2 = sm[:, 128:160]
        m8 = sm[:, 160:176]
        selv = sm[:, 176:208]
        rd = sm[:, 208:216]
        fac = sm[:, 216:224]
        abc = C("abc").rearrange("p (a b) -> p a b", a=16)
        ab2 = C("ab2").rearrange("p (a b) -> p a b", a=16)
        a1 = C("a1").rearrange("p (a b) -> p a b", a=16)
        a2 = C("a2").rearrange("p (a b) -> p a b", a=16)
        maskc = CBv("maskc").rearrange("p (a b) -> p a b", a=16)
        expd = CBv("expd")
        caus = CBv("caus").rearrange("p (a b) -> p a b", a=4)
        winm = CBv("winm").rearrange("p (a b) -> p a b", a=4)
        onesb = CBv("onesb")
        for qt in range(NT):
            q_ = slice(qt * 128, (qt + 1) * 128)
            r0 = seq * T + qt * 128
            dma("sp", za, H.ap()[r0:r0 + 128, O_ZA:O_ZA + 1024], [B_H], [B_za], "za")
            actf(za, za, AF.Silu, [B_za], [B_za])
            for g in range(2):
                kb.op("pe", mm(PS[0][0:127, :], KcT[:, g, 0:127], QT[:, 4 * g:4 * g + 4, q_]), [B_KC, B_QT[qt]], [PB[0]])
                for r_ in range(4):
                    actf(ec[0:127, r_, :], PS[0][0:127, r_ * 128:(r_ + 1) * 128], AF.Exp, [PB[0], Bc], [B_ec],
                         bias=abc[0:127, qt, 4 * g + r_:4 * g + r_ + 1], scale=SCALE)
                tt("pool", ec[0:127, :, :], ec[0:127, :, :], maskc[0:127, qt:qt + 1, :].to_broadcast([127, 4, 128]), mult, [B_ec, Bc], [B_ec])
                for half in range(2):
                    pj = 1 + half
                    kb.op("pe", [mm(PS[pj][:, k * 162:k * 162 + 162], ec[0:127, 2 * half + k, :], VCX[0:127, g, :]) for k in range(2)],
                          [B_ec, B_KC], [PB[pj]])
                    for k in range(2):
                        r_ = 2 * half + k
                        hh = 4 * g + r_
                        u = PS[pj][:, k * 162:k * 162 + 162]
                        ts("dve", rd[:, 0:1], u[:, 160:161], 1e-30, None, ALU.max, None, [PB[pj]], [B_sm])
                        recip(rd[:, 0:1], rd[:, 0:1], [B_sm], [B_sm])
                        tt("dve", fac[:, 0:1], rd[:, 0:1], GS[:, qt, hh:hh + 1], mult, [B_sm, B_KV[qt]], [B_sm])
                        ts("dve", oacc[:, hh, :], u[:, 0:128], fac[:, 0:1], None, mult, None, [PB[pj], B_sm], [B_oacc])
                        if r_ == 0:
                            ts("dve", imp[:, g, :], u[:, 128:160], rd[:, 0:1], None, mult, None, [PB[pj], B_sm], [B_sm])
                        else:
                            stt("dve", imp[:, g, :], u[:, 128:160], rd[:, 0:1], imp[:, g, :], mult, add, [PB[pj], B_sm], [B_sm])
            for g in range(2):
                tt("dve", sc[:, g, :], imp[:, g, :], a1[:, qt, :], mult, [B_sm, Bc], [B_sm])
                tt("dve", sc[:, g, :], sc[:, g, :], a2[:, qt, :], add, [B_sm, Bc], [B_sm])
                kb.op("dve", lambda e, g=g: e.max(out=m8[:, 0:8], in_=sc[:, g, :]), [B_sm], [B_sm])
                kb.op("dve", lambda e, g=g: e.match_replace(out=sc2, in_to_replace=m8[:, 0:8], in_values=sc[:, g, :], imm_value=-30000.0),
                      [B_sm], [B_sm])
                kb.op("dve", lambda e: e.max(out=m8[:, 8:16], in_=sc2), [B_sm], [B_sm])
                ts("dve", selv, sc[:, g, :], m8[:, 15:16], None, ALU.is_ge, None, [B_sm], [B_sm])
                ts("dve", selv, selv, -1.0, -NBIG, add, mult, [B_sm], [B_sm])
                kb.op("pe", mm(PS[3][0:32, 0:128], selv, ident), [B_sm, Bc], [PB[3]])
                cp("act", NS[0:32, g, :, :], PS[3][0:32, 0:128].unsqueeze(1).to_broadcast([32, 4, 128]), [PB[3]], [B_NS])
            for g in range(2):
                E = es[g]
                Be = B_es[g]
                for kt in range(qt + 1):
                    pj = 4 + (kt % 2)
                    k_ = slice(kt * 128, (kt + 1) * 128)
                    ops = [mm(PS[pj][:, :], KsT[:, g, k_], QT[:, 4 * g:4 * g + 4, q_], start=True, stop=False),
                           mm(PS[pj][:, :], expd[:, k_], NS[:, g, :, :], start=False, stop=(kt != qt))]
                    if kt == qt:
                        ops.append(mm(PS[pj][:, :], identb, caus, start=False, stop=True))
                    kb.op("pe", ops, [B_KV[kt], B_QT[qt], B_NS, Bc], [PB[pj]])
                    for r_ in range(4):
                        hh = 4 * g + r_
                        actf(E[:, kt, r_, :], PS[pj][:, r_ * 128:(r_ + 1) * 128], AF.Exp, [PB[pj], Bc], [Be],
                             bias=ab2[:, kt - qt + 15, hh:hh + 1], scale=SCALE)
                kb.op("pe", [mm(PS[6][:, r_ * 128:(r_ + 1) * 128], E[:, kt, r_, :], VS[:, kt, g, :], start=(kt == 0), stop=(kt == qt))
                             for r_ in range(4) for kt in range(qt + 1)], [Be] + [B_KV[kt] for kt in range(qt + 1)], [PB[6]])
                kb.op("pe", [mm(PS[7][:, r_ * 2:(r_ + 1) * 2], E[:, kt, r_, :], onesb, start=(kt == 0), stop=(kt == qt))
                             for r_ in range(4) for kt in range(qt + 1)], [Be, Bc], [PB[7]])
                recip(rd[:, 0:4], PS[7][:, 0:8:2], [PB[7]], [B_sm])
                tt("dve", fac[:, 0:4], rd[:, 0:4], GS[:, qt, 8 + 4 * g:12 + 4 * g], mult, [B_sm, B_KV[qt]], [B_sm])
                for r_ in range(4):
                    hh = 4 * g + r_
                    stt("dve", oacc[:, hh, :], PS[6][:, r_ * 128:(r_ + 1) * 128], fac[:, r_:r_ + 1], oacc[:, hh, :], mult, add,
                        [PB[6], B_sm, B_oacc], [B_oacc])
                kts = list(range(max(0, qt - 4), qt + 1))
                for ki, kt in enumerate(kts):
                    pj = 4 + (ki % 2)
                    k_ = slice(kt * 128, (kt + 1) * 128)
                    ops = [mm(PS[pj][:, :], KwT[:, g, k_], QT[:, 4 * g:4 * g + 4, q_], start=True, stop=(kt != qt and kt != qt - 4))]
                    if kt == qt:
                        ops.append(mm(PS[pj][:, :], identb, caus, start=False, stop=True))
                    if kt == qt - 4:
                        ops.append(mm(PS[pj][:, :], identb, winm, start=False, stop=True))
                    kb.op("pe", ops, [B_KV[kt], B_QT[qt], Bc], [PB[pj]])
                    for r_ in range(4):
                        hh = 4 * g + r_
                        actf(ew[:, ki, r_, :], PS[pj][:, r_ * 128:(r_ + 1) * 128], AF.Exp, [PB[pj], Bc], [B_ew],
                             bias=ab2[:, kt - qt + 15, hh:hh + 1], scale=SCALE)
                nk = len(kts)
                kb.op("pe", [mm(PS[6][:, r_ * 128:(r_ + 1) * 128], ew[:, ki, r_, :], VW[:, kts[ki], g, :], start=(ki == 0), stop=(ki == nk - 1))
                             for r_ in range(4) for ki in range(nk)], [B_ew] + [B_KV[kt] for kt in kts], [PB[6]])
                kb.op("pe", [mm(PS[7][:, r_ * 2:(r_ + 1) * 2], ew[:, ki, r_, :], onesb, start=(ki == 0), stop=(ki == nk - 1))
                             for r_ in range(4) for ki in range(nk)], [B_ew, Bc], [PB[7]])
                recip(rd[:, 0:4], PS[7][:, 0:8:2], [PB[7]], [B_sm])
                tt("dve", fac[:, 0:4], rd[:, 0:4], GS[:, qt, 16 + 4 * g:20 + 4 * g], mult, [B_sm, B_KV[qt]], [B_sm])
                for r_ in range(4):
                    hh = 4 * g + r_
                    stt("dve", oacc[:, hh, :], PS[6][:, r_ * 128:(r_ + 1) * 128], fac[:, r_:r_ + 1], oacc[:, hh, :], mult, add,
                        [PB[6], B_sm, B_oacc], [B_oacc])
            tt("pool", za, za, oacc.rearrange("p a b -> p (a b)"), mult, [B_za, B_oacc], [B_za])
            dma("sp", OA.ap()[r0:r0 + 128, :], za, [B_za], [B_OA], "oa")

    def phase_GDN(seq):
        ar.reset()
        CW = ar.get([4, 3072])
        B_CW = kb.buf()
        dma("sp", CW.rearrange("p a b -> p (a b)"), convw_d.ap().partition_broadcast(128), [], [B_CW], "c0")
        hw = ar.get([4, 1024])
        B_hw = kb.bufs(4)
        cq = ar.get([8, 128])
        ck = ar.get([8, 128])
        cv = ar.get([8, 128])
        B_c = kb.bufs(3)
        kT = ar.get([8, 128])
        qT = ar.get([8, 128])
        B_kT = kb.buf()
        B_qT = kb.buf()
        S = ar.get([8, 128])
        B_S = kb.buf()
        L = ar.get([8, 128])
        LT = ar.get([8, 128])
        B_L = kb.buf()
        B_LT = kb.buf()
        Pb = [ar.get([8, 128]) for _ in range(2)]
        Qb = [ar.get([8, 128]) for _ in range(2)]
        B_P = kb.bufs(2)
        B_Q = kb.bufs(2)
        Rm = ar.get([8, 128])
        B_R = kb.buf()
        vb = ar.get([8, 128])
        kbd = ar.get([8, 128])
        kd = ar.get([8, 128])
        B_vb, B_kbd, B_kd = kb.bufs(3)
        nwT = ar.get([8, 128])
        B_nwT = kb.buf()
        vn = ar.get([8, 128])
        B_vn = kb.buf()
        attT = ar.get([8, 128])
        B_att = kb.buf()
        qSs = ar.get([8, 128])
        B_qSs = kb.buf()
        zb = ar.get([1024])
        B_zb = kb.buf()
        BD = ar.get([8, 128])
        B_BD = kb.buf()
        decT = ar.get([128])
        B_decT = kb.buf()
        st = ar.get([160])
        B_st = kb.buf()
        ab = st[:, 0:16]
        beta = st[:, 16:24]
        nbeta = st[:, 24:32]
        gg = st[:, 32:40]
        dec = st[:, 40:48]
        ndec = st[:, 48:56]
        edec = st[:, 56:64]
        edl = st[:, 64:72]
        edld = st[:, 72:80]
        bed = st[:, 80:88]
        ss = st[:, 88:112]
        tmp = st[:, 112:136]
        rn = st[:, 136:160]
        ut = C("ut")
        ones = C("ones")
        mskl = C("mskl")
        msklt = C("msklt")
        pc = [0]

        def pair():
            p = pc[0] % 4
            pc[0] += 1
            return (2 * p, 2 * p + 1)

        def v3(x):
            return x.rearrange("p (a b) -> p a b", a=4)

        def bc8(x):
            return x.unsqueeze(2).to_broadcast([128, 8, 128])

        def permm(lhs_fn, rhs_fn, reads, extra=None):
            pa = pair()
            for half in range(2):
                ops = []
                for k in range(4):
                    h = 4 * half + k
                    o_ = PS[pa[half]][:, k * 128:(k + 1) * 128]
                    if extra is None:
                        ops.append(mm(o_, lhs_fn(h), rhs_fn(h)))
                    else:
                        ops.append(mm(o_, lhs_fn(h), rhs_fn(h), start=True, stop=False))
                        ops.append(mm(o_, extra[0](h), extra[1](h), start=False, stop=True))
                kb.op("pe", ops, reads, [PB[pa[half]]])
            return pa

        def evac(pa, dst, Bd, engs=("act", "dve")):
            for half in range(2):
                cp(engs[half], dst[:, 4 * half:4 * half + 4, :], v3(PS[pa[half]][:, :]), [PB[pa[half]]], [Bd])

        kb.op("pool", lambda e: e.memset(S, 0.0), [], [B_S])
        kb.op("pool", lambda e: e.memset(BD, 0.0), [], [B_BD])
        for t_ in range(NT):
            r0 = seq * T + t_ * 128
            for grp, (dst, Bd) in enumerate(((cq, B_c[0]), (ck, B_c[1]), (cv, B_c[2]))):
                c0 = O_QKV + grp * 1024
                for i in range(4):
                    sh = 3 - i
                    if t_ == 0 and sh > 0:
                        kb.op("pool", lambda e, i=i, sh=sh: e.memset(hw[0:sh, i, :], 0.0), [], [B_hw[i]])
                        dma("sp", hw[sh:128, i, :], H.ap()[r0:r0 + 128 - sh, c0:c0 + 1024], [B_H], [B_hw[i]], "hw%d" % i)
                    else:
                        dma("sp", hw[:, i, :], H.ap()[r0 - sh:r0 - sh + 128, c0:c0 + 1024], [B_H], [B_hw[i]], "hw%d" % i)
                    tt("pool", hw[:, i, :], hw[:, i, :], CW[:, i, grp * 1024:(grp + 1) * 1024], mult, [B_hw[i], B_CW], [B_hw[i]])
                d2 = dst.rearrange("p a b -> p (a b)")
                tt("dve", d2, hw[:, 0, :], hw[:, 1, :], add, [B_hw[0], B_hw[1]], [Bd])
                tt("dve", d2, d2, hw[:, 2, :], add, [Bd, B_hw[2]], [Bd])
                tt("dve", d2, d2, hw[:, 3, :], add, [Bd, B_hw[3]], [Bd])
                actf(d2, d2, AF.Silu, [Bd], [Bd])
            for n_, (src, Bs_) in enumerate(((cq, B_c[0]), (ck, B_c[1]))):
                tt("pool", zb.rearrange("p (a b) -> p a b", a=8), src, src, mult, [Bs_], [B_zb])
                red(ss[:, 8 * n_:8 * n_ + 8], zb.rearrange("p (a b) -> p a b", a=8), [B_zb], [B_st])
            rsq(ss[:, 0:16], tmp[:, 0:16], rn[:, 0:16], 1.0, B_st)
            ts("dve", rn[:, 0:8], rn[:, 0:8], 128 ** -0.5, None, mult, None, [B_st], [B_st])
            tt("dve", cq, cq, bc8(rn[:, 0:8]), mult, [B_c[0], B_st], [B_c[0]])
            tt("dve", ck, ck, bc8(rn[:, 8:16]), mult, [B_c[1], B_st], [B_c[1]])
            dma("sp", ab, H.ap()[r0:r0 + 128, O_A:O_A + 16], [B_H], [B_st], "ab")
            actf(beta, ab[:, 8:16], AF.Sigmoid, [B_st], [B_st])
            ts("dve", nbeta, beta, -1.0, None, mult, None, [B_st], [B_st])
            tt("dve", gg, ab[:, 0:8], dtb[:], add, [B_st, Bc], [B_st])
            actf(gg, gg, AF.Exp, [B_st], [B_st])
            actf(gg, gg, AF.Ln, [B_st], [B_st], bias=1.0)
            tt("dve", gg, gg, nea[:], mult, [B_st, Bc], [B_st])
            kb.op("pe", [mm(PS[6][:, 0:8], ut, gg), mm(PS[6][:, 8:16], ones, gg)], [B_st, Bc], [PB[6]])
            cp("dve", dec, PS[6][:, 0:8], [PB[6]], [B_st])
            ts("dve", ndec, dec, -1.0, None, mult, None, [B_st], [B_st])
            actf(edec, dec, AF.Exp, [B_st], [B_st])
            actf(edl, PS[6][:, 8:16], AF.Exp, [PB[6]], [B_st])
            tt("dve", edld, PS[6][:, 8:16], dec, sub, [PB[6], B_st], [B_st])
            actf(edld, edld, AF.Exp, [B_st], [B_st])
            tt("dve", bed, beta, edec, mult, [B_st], [B_st])
            kb.op("pe", mm(PS[7][0:8, 0:128], dec, ident), [B_st, Bc], [PB[7]])
            cp("dve", decT[0:8, :], PS[7][0:8, 0:128], [PB[7]], [B_decT])
            tt("dve", BD[0:8, :, :], decT[0:8, :].unsqueeze(1).to_broadcast([8, 8, 128]),
               ident[0:8, 0:8].unsqueeze(2).to_broadcast([8, 8, 128]), mult, [B_decT, Bc], [B_BD])
            for (msk, dst, Bd, sc_, bcol) in ((mskl, L, B_L, -1.0, dec), (msklt, LT, B_LT, 1.0, ndec)):
                pa = pair()
                for half in range(2):
                    kb.op("pe", [mm(PS[pa[half]][:, :], ones, BD[:, 4 * half:4 * half + 4, :], start=True, stop=False),
                                 mm(PS[pa[half]][:, :], ident, msk, start=False, stop=True)], [B_BD, Bc], [PB[pa[half]]])
                    for k in range(4):
                        h = 4 * half + k
                        actf(dst[:, h, :], PS[pa[half]][:, k * 128:(k + 1) * 128], AF.Exp, [PB[pa[half]], B_st], [Bd],
                             bias=bcol[:, h:h + 1], scale=sc_)
            pa = permm(lambda h: ck[:, h, :], lambda h: ident, [B_c[1], Bc])
            evac(pa, kT, B_kT)
            pa = permm(lambda h: cq[:, h, :], lambda h: ident, [B_c[0], Bc])
            evac(pa, qT, B_qT)
            pa = permm(lambda h: kT[:, h, :], lambda h: kT[:, h, :], [B_kT])
            X = Qb[0]
            for h in range(8):
                stt("dve", X[:, h, :], PS[pa[h // 4]][:, (h % 4) * 128:(h % 4 + 1) * 128], nbeta[:, h:h + 1], L[:, h, :], mult, mult,
                    [PB[pa[h // 4]], B_st, B_L], [B_Q[0]])
            pa = permm(lambda h: kT[:, h, :], lambda h: qT[:, h, :], [B_kT, B_qT])
            for half in range(2):
                tt("dve", attT[:, 4 * half:4 * half + 4, :], v3(PS[pa[half]][:, :]), LT[:, 4 * half:4 * half + 4, :], mult,
                   [PB[pa[half]], B_LT], [B_att])
            pa = permm(lambda h: X[:, h, :], lambda h: ident, [B_Q[0], Bc])
            evac(pa, Pb[0], B_P[0])
            tt("pool", Rm, Pb[0], ident.unsqueeze(1).to_broadcast([128, 8, 128]), add, [B_P[0], Bc], [B_R])
            cur = 0
            for lev in range(1, 7):
                nxt = 1 - cur
                Pc, Qc, Pn, Qn = Pb[cur], Qb[cur], Pb[nxt], Qb[nxt]
                pa = permm(lambda h: Pc[:, h, :], lambda h: Qc[:, h, :], [B_P[cur], B_Q[cur]])
                evac(pa, Qn, B_Q[nxt])
                if lev <= 5:
                    pa = permm(lambda h: Qc[:, h, :], lambda h: Pc[:, h, :], [B_P[cur], B_Q[cur]])
                    evac(pa, Pn, B_P[nxt])
                pa = permm(lambda h: Qn[:, h, :], lambda h: Rm[:, h, :], [B_Q[nxt], B_R])
                for half in range(2):
                    tt("dve", Rm[:, 4 * half:4 * half + 4, :], Rm[:, 4 * half:4 * half + 4, :], v3(PS[pa[half]][:, :]), add,
                       [PB[pa[half]], B_R], [B_R])
                cur = nxt
            tt("pool", vb, cv, bc8(beta), mult, [B_c[2], B_st], [B_vb])
            tt("pool", kbd, ck, bc8(bed), mult, [B_c[1], B_st], [B_kbd])
            tt("pool", kd, ck, bc8(edld), mult, [B_c[1], B_st], [B_kd])
            pa = permm(lambda h: kbd[:, h, :], lambda h: Rm[:, h, :], [B_kbd, B_R])
            for half in range(2):
                ts("dve", nwT[:, 4 * half:4 * half + 4, :], v3(PS[pa[half]][:, :]), -1.0, None, mult, None, [PB[pa[half]]], [B_nwT])
            pa = permm(lambda h: Rm[:, h, :], lambda h: vb[:, h, :], [B_R, B_vb, B_nwT, B_S],
                       extra=(lambda h: nwT[:, h, :], lambda h: S[:, h, :]))
            evac(pa, vn, B_vn)
            pa = permm(lambda h: qT[:, h, :], lambda h: S[:, h, :], [B_qT, B_S])
            for half in range(2):
                tt("dve", qSs[:, 4 * half:4 * half + 4, :], v3(PS[pa[half]][:, :]),
                   edec[:, 4 * half:4 * half + 4].unsqueeze(2).to_broadcast([128, 4, 128]), mult, [PB[pa[half]], B_st], [B_qSs])
            pa = permm(lambda h: attT[:, h, :], lambda h: vn[:, h, :], [B_att, B_vn])
            for half in range(2):
                tt("dve", qSs[:, 4 * half:4 * half + 4, :], qSs[:, 4 * half:4 * half + 4, :], v3(PS[pa[half]][:, :]), add,
                   [PB[pa[half]], B_qSs], [B_qSs])
            pa = permm(lambda h: kd[:, h, :], lambda h: vn[:, h, :], [B_kd, B_vn])
            for h in range(8):
                stt("dve", S[:, h, :], S[:, h, :], edl[:, h:h + 1], PS[pa[h // 4]][:, (h % 4) * 128:(h % 4 + 1) * 128], mult, add,
                    [PB[pa[h // 4]], B_st, B_S], [B_S])
            tt("pool", vb, qSs, qSs, mult, [B_qSs], [B_vb])
            red(ss[:, 16:24], vb, [B_vb], [B_st])
            rsq(ss[:, 16:24], tmp[:, 16:24], rn[:, 16:24], 1.0 / 128, B_st)
            tt("dve", qSs, qSs, bc8(rn[:, 16:24]), mult, [B_qSs, B_st], [B_qSs])
            tt("pool", qSs, qSs, gnw[:].unsqueeze(1).to_broadcast([128, 8, 128]), mult, [B_qSs, Bc], [B_qSs])
            dma("sp", zb, H.ap()[r0:r0 + 128, O_ZB:O_ZB + 1024], [B_H], [B_zb], "zb")
            actf(zb, zb, AF.Silu, [B_zb], [B_zb])
            tt("dve", zb, zb, qSs.rearrange("p a b -> p (a b)"), mult, [B_zb, B_qSs], [B_zb])
            dma("sp", OB.ap()[r0:r0 + 128, :], zb, [B_zb], [B_OB], "ob")
        dma("sp", S_p.ap()[seq].rearrange("h k v -> k h v"), S, [B_S], [B_out], "o0")

    def phase_F():
        ar.reset()
        wpa = ar.get([8, 2048], BF16)
        wpb = ar.get([8, 2048], BF16)
        wo = ar.get([16, 2048], BF16)
        B_w = kb.buf()
        m = ar.get([2048])
        B_m = kb.buf()
        stg = [m[:, 0:1024], m[:, 1024:2048]]
        B_stg = kb.bufs(2)
        oa = ar.get([1024])
        ob_ = ar.get([1024])
        B_oa, B_ob = kb.bufs(2)
        oT = ar.get([16, 128], BF16)
        B_oT = kb.buf()
        mT = ar.get([16, 128], BF16)
        B_mT = kb.buf()
        ga = [ar.get([512]) for _ in range(2)]
        gb = [ar.get([512]) for _ in range(2)]
        B_ga = kb.bufs(2)
        B_gb = kb.bufs(2)
        xy = [ar.get([512]) for _ in range(2)]
        B_xy = kb.bufs(2)
        n = 0
        for (wd, wt_, nk) in ((wpa_d, wpa, 8), (wpb_d, wpb, 8), (wo_d, wo, 16)):
            for kc in range(nk):
                for hf in range(2):
                    i = n % 2
                    n += 1
                    dma("sp" if i == 0 else "act", stg[i], wd.ap()[kc * 128:(kc + 1) * 128, hf * 1024:(hf + 1) * 1024], [], [B_stg[i]], "fst%d" % i)
                    cp("dve" if i == 0 else "pool", wt_[:, kc, hf * 1024:(hf + 1) * 1024], stg[i], [B_stg[i]], [B_w])
        kb.barrier()
        for tile in range(2 * NT + 1):
            n_ = 128 if tile < 2 * NT else 4
            r0 = tile * 128
            dma("sp", oa[0:n_, :], OA.ap()[r0:r0 + n_, :], [B_OA], [B_oa], "foa")
            dma("act", ob_[0:n_, :], OB.ap()[r0:r0 + n_, :], [B_OB], [B_ob], "fob")
            for wh, (src, Bs_) in enumerate(((oa, B_oa), (ob_, B_ob))):
                for j in range(2):
                    pj = 2 * wh + j
                    kb.op("pe", [mm(PS[pj][:, k * n_:(k + 1) * n_], src[0:n_, (4 * j + k) * 128:(4 * j + k + 1) * 128], (ident if n_ == 128 else ident[0:n_, 0:n_]))
                                 for k in range(4)], [Bs_, Bc], [PB[pj]])
                    cp("act" if j == 0 else "dve", oT[:, 8 * wh + 4 * j:8 * wh + 4 * j + 4, 0:n_],
                       PS[pj][:, 0:4 * n_].rearrange("p (a b) -> p a b", a=4), [PB[pj]], [B_oT])
            for cb in range(4):
                i = cb % 2
                c_ = slice(cb * 512, (cb + 1) * 512)
                rr = slice(r0, r0 + n_)
                dma("sp", ga[i][0:n_, :], H.ap()[rr, O_GA + cb * 512:O_GA + (cb + 1) * 512], [B_H], [B_ga[i]], "fga%d" % i)
                dma("act", gb[i][0:n_, :], H.ap()[rr, O_GB + cb * 512:O_GB + (cb + 1) * 512], [B_H], [B_gb[i]], "fgb%d" % i)
                actf(ga[i][0:n_, :], ga[i][0:n_, :], AF.Sigmoid, [B_ga[i]], [B_ga[i]])
                actf(gb[i][0:n_, :], gb[i][0:n_, :], AF.Sigmoid, [B_gb[i]], [B_gb[i]])
                pa_, pb_ = 4 + i, 6 + i
                kb.op("pe", [mm(PS[pa_][0:n_, :], oT[:, kc, 0:n_], wpa[:, kc, c_], start=(kc == 0), stop=(kc == 7)) for kc in range(8)],
                      [B_oT, B_w], [PB[pa_]])
                kb.op("pe", [mm(PS[pb_][0:n_, :], oT[:, 8 + kc, 0:n_], wpb[:, kc, c_], start=(kc == 0), stop=(kc == 7)) for kc in range(8)],
                      [B_oT, B_w], [PB[pb_]])
                tt("dve", m[0:n_, c_], ga[i][0:n_, :], PS[pa_][0:n_, :], mult, [B_ga[i], PB[pa_]], [B_m])
                tt("dve", gb[i][0:n_, :], gb[i][0:n_, :], PS[pb_][0:n_, :], mult, [B_gb[i], PB[pb_]], [B_gb[i]])
                tt("pool", m[0:n_, c_], m[0:n_, c_], gb[i][0:n_, :], add, [B_m, B_gb[i]], [B_m])
            for j in range(4):
                pj = j
                kb.op("pe", [mm(PS[pj][:, k * n_:(k + 1) * n_], m[0:n_, (4 * j + k) * 128:(4 * j + k + 1) * 128], (ident if n_ == 128 else ident[0:n_, 0:n_]))
                             for k in range(4)], [B_m, Bc], [PB[pj]])
                cp("act" if j % 2 == 0 else "dve", mT[:, 4 * j:4 * j + 4, 0:n_],
                   PS[pj][:, 0:4 * n_].rearrange("p (a b) -> p a b", a=4), [PB[pj]], [B_mT])
            for cb in range(4):
                i = cb % 2
                c_ = slice(cb * 512, (cb + 1) * 512)
                if tile < 2 * NT:
                    dma("sp", xy[i][0:n_, :], x_p.ap()[r0:r0 + n_, c_], [], [B_xy[i]], "fx%d" % i)
                else:
                    dma("sp", xy[i][0:n_, :], x_s.ap()[:, c_], [], [B_xy[i]], "fx%d" % i)
                pj = 4 + i
                kb.op("pe", [mm(PS[pj][0:n_, :], mT[:, kc, 0:n_], wo[:, kc, c_], start=(kc == 0), stop=(kc == 15)) for kc in range(16)],
                      [B_mT, B_w], [PB[pj]])
                tt("dve", xy[i][0:n_, :], xy[i][0:n_, :], PS[pj][0:n_, :], add, [B_xy[i], PB[pj]], [B_xy[i]])
                if tile < 2 * NT:
                    dma("sp", y_p.ap()[r0:r0 + n_, c_], xy[i][0:n_, :], [B_xy[i]], [B_out], "fy%d" % i)
                else:
                    dma("sp", y_s.ap()[:, c_], xy[i][0:n_, :], [B_xy[i]], [B_out], "fy%d" % i)

    def phase_SNSA():
        ar.reset()
        R0 = 2 * T
        big = ar.get([32768])
        SEG = big.bitcast(BF16).rearrange("p (a b) -> p a b", a=4)
        KsTs = big[:, 0:16384].bitcast(BF16).rearrange("p (a b) -> p a b", a=2)
        VSs = big[:, 16384:32768].bitcast(BF16).rearrange("p (a b c) -> p a b c", a=128, b=2)
        B_big = kb.buf()
        cs = ar.get([ns32])
        B_cs = kb.buf()
        sels = ar.get([8, 257], BF16)
        dma("sp", cs, cs32_d.ap(), [], [B_cs], "c0")
        dma("sp", big[:, 0:2056], csb_d.ap(), [], [B_big], "c1")
        cp("dve", sels.rearrange("p a b -> p (a b)"), big[:, 0:2056], [B_big], [B_cs])

        def CS(name):
            o, n = os32[name]
            return cs[:, o:o + n]
        abs_ = CS("abs").rearrange("p (a b) -> p a b", a=8)
        distd = CS("distd")
        distw = CS("distw")
        rowc = CS("rowc")
        a1s = rowc[0:1, 0:257]
        a2s = rowc[0:1, 257:514]
        hsel = [rowc[0:1, 514 + 128 * c_:514 + 128 * (c_ + 1)] for c_ in range(8)]
        iotap = CS("iotap")
        ptb = big[:, 4200:4328].bitcast(I32)
        ptf = big[:, 4800:4928]
        IDX = ar.get([128], I32)
        B_IDX = kb.buf()
        for c_ in range(4):
            dma("sp", ptb[32 * c_:32 * c_ + 32, :], pt_d.ap()[c_:c_ + 1, :].partition_broadcast(32), [], [B_IDX], "c0")
        cp("dve", ptf, ptb, [B_IDX], [B_IDX])
        ts("dve", ptf, ptf, 32.0, iotap[:, 0:1], mult, add, [B_IDX, B_cs], [B_IDX])
        cp("dve", IDX, ptf, [B_IDX], [B_IDX])
        hs4 = big[:, 0:2584]
        B_h4 = kb.buf()
        sq4b = big[:, 2600:4136]
        sq4 = ar.get([128])
        S4 = ar.get([40])
        B_S4 = kb.buf()
        gs4 = ar.get([24])
        qTs = ar.get([8, 4], BF16)
        kTn = ar.get([4, 4], BF16)
        B_qTs = kb.buf()
        r1 = ar.get([2048])
        r2 = ar.get([2048])
        pg4 = [r1, r2]
        pg = [r1[:, 0:512], r1[:, 512:1024]]
        B_pg = kb.bufs(2)
        hsb = r1[:, 1024:1536].bitcast(BF16)
        B_hsb = kb.buf()
        KcTs = r2[:, 0:1024].bitcast(BF16).rearrange("p (a b) -> p a b", a=2)
        VCs = r2[:, 1024:2048].bitcast(BF16).rearrange("p (a b c) -> p a b c", a=8, b=2)
        B_KCs = kb.buf()
        kcw = ar.get([128])
        B_kcw = kb.buf()
        cst = ar.get([8])
        B_cst = kb.buf()
        tmpc = ar.get([8, 4])
        ecs = ar.get([8, 4], BF16)
        B_ecs = kb.buf()
        impn = ar.get([258])
        rows = ar.get([800])
        B_rows = kb.buf()
        score = rows[0:1, 0:257]
        sc2 = rows[0:1, 258:515]
        m8 = rows[0:1, 516:532]
        negsel = rows[0:1, 534:791]
        negm = [ar.get([32]) for _ in range(2)]
        B_negm = kb.bufs(2)
        tmpd = r1[:, 1536:2048].rearrange("p (a b) -> p a b", a=128)
        esd = ar.get([128, 4], BF16)
        B_esd = kb.buf()
        enew = ar.get([4], BF16)
        vnew = ar.get([128], BF16)
        vnf = ar.get([128])
        B_new = kb.buf()
        usb = ar.get([128])
        rdn = ar.get([4])
        B_usb = kb.buf()
        usc_t = big[:, 0:3072].rearrange("p (a b c) -> p a b c", a=3, b=8)
        za4 = big[:, 3072:4096]
        B_fin = kb.buf()
        B_NK = kb.buf()
        B_USC = kb.buf()
        onesb = CBv("onesb")
        ones = C("ones")
        kb.op("pool", lambda e: e.memset(enew, 0.0), [], [B_new])
        kb.op("pool", lambda e: e.memset(vnew, 0.0), [], [B_new])
        kb.barrier()
        h = hs4[0:4, :]
        dma("sp", h, H.ap()[R0:R0 + 4, 0:2584], [B_H], [B_h4], "hq0")
        dma("sp", cmp_s.ap(), h[:, O_KVC:O_KVC + 512], [B_h4], [B_out], "o0")
        tt("dve", sq4b[0:4, 0:1024], h[:, 0:1024], h[:, 0:1024], mult, [B_h4], [B_S4])
        red(S4[0:4, 0:8], sq4b[0:4, 0:1024].rearrange("p (a d) -> p a d", a=8), [B_S4], [B_S4])
        tt("dve", sq4b[0:4, 1024:1280], h[:, O_KVS:O_KVS + 256], h[:, O_KVS:O_KVS + 256], mult, [B_h4], [B_S4])
        tt("dve", sq4b[0:4, 1280:1536], h[:, O_KVW:O_KVW + 256], h[:, O_KVW:O_KVW + 256], mult, [B_h4], [B_S4])
        red(S4[0:4, 8:12], sq4b[0:4, 1024:1536].rearrange("p (a d) -> p a d", a=4), [B_S4], [B_S4])
        rsq(S4[0:4, 0:12], S4[0:4, 12:24], S4[0:4, 24:36], 1.0 / 128, B_S4)
        tt("dve", h[:, 0:1024].rearrange("p (a d) -> p a d", a=8), h[:, 0:1024].rearrange("p (a d) -> p a d", a=8),
           S4[0:4, 24:32].unsqueeze(2).to_broadcast([4, 8, 128]), mult, [B_h4, B_S4], [B_h4])
        for g in range(2):
            stt("dve", h[:, O_KVS + 128 * g:O_KVS + 128 * g + 128], h[:, O_KVS + 128 * g:O_KVS + 128 * g + 128],
                S4[0:4, 32 + g:33 + g], knws[0:4, :], mult, mult, [B_h4, B_S4, Bc], [B_h4])
            stt("dve", h[:, O_KVW + 128 * g:O_KVW + 128 * g + 128], h[:, O_KVW + 128 * g:O_KVW + 128 * g + 128],
                S4[0:4, 34 + g:35 + g], knww[0:4, :], mult, mult, [B_h4, B_S4, Bc], [B_h4])
        dma("sp", slc_s.ap(), h[:, O_KVS:O_KVS + 512], [B_h4], [B_out], "o0")
        dma("sp", NK.ap(), h[:, O_KVC:O_KVC + 1536], [B_h4], [B_NK], "o1")
        dma("sp", win_s.ap().rearrange("(b t) c -> b t c", b=4)[:, 511, :], h[:, O_KVW:O_KVW + 512], [B_h4], [B_out], "o0")
        for b in range(4):
            dma("act", win_s.ap()[b * 512:b * 512 + 511, :], stwin_d.ap()[b * 512 + 1:b * 512 + 512, :], [], [B_out], "o0")
        kb.op("pe", [mm(PS[0][:, k * 4:(k + 1) * 4], h[:, k * 128:(k + 1) * 128], ident[0:4, 0:4]) for k in range(8)], [B_h4, Bc], [PB[0]])
        ts("dve", qTs, PS[0][:, 0:32].rearrange("p (a b) -> p a b", a=8), qnw[:, 0:1], None, mult, None, [PB[0], Bc], [B_qTs])
        kb.op("pe", [mm(PS[1][:, k * 4:(k + 1) * 4], h[:, (O_KVS if k < 2 else O_KVW) + (k % 2) * 128:(O_KVS if k < 2 else O_KVW) + (k % 2) * 128 + 128],
                        ident[0:4, 0:4]) for k in range(4)], [B_h4, Bc], [PB[1]])
        cp("act", kTn, PS[1][:, 0:16].rearrange("p (a b) -> p a b", a=4), [PB[1]], [B_qTs])
        actf(gs4[0:4, :], h[:, O_GN:O_GN + 24], AF.Sigmoid, [B_h4], [B_fin])

        kb.barrier()

        def gather_pages(b, cache_d, ntr, cmp_mode):
            kb.barrier()
            for q in range(32):
                i = q % 2
                kb.dma("pool", lambda e, i=i, col=b * 32 + q: e.indirect_dma_start(
                    out=pg4[i], out_offset=None, in_=cache_d.ap(),
                    in_offset=bass.IndirectOffsetOnAxis(ap=IDX[:, col:col + 1], axis=0)),
                    [B_IDX], [B_pg[i]], "g%d" % i)
                for j in range(4):
                    pj = 6 + (j % 2)
                    kb.op("pe", [mm(PS[pj][:, k * 128:(k + 1) * 128], pg4[i][:, j * 512 + k * 128:j * 512 + (k + 1) * 128], ident)
                                 for k in range(ntr)], [B_pg[i], Bc], [PB[pj]])
                    if cmp_mode:
                        dst = SEG[:, 0:4, 512 * q + j:512 * q + 512:4]
                    else:
                        dst = KsTs[:, 0:2, (4 * q + j) * 128:(4 * q + j + 1) * 128]
                    cp("act" if j % 2 == 0 else "dve", dst, PS[pj][:, 0:ntr * 128].rearrange("p (a b) -> p a b", a=ntr), [PB[pj]], [B_big])
                    if not cmp_mode:
                        cp("pool", VSs[:, 4 * q + j, :, :], pg4[i][:, j * 512 + 256:j * 512 + 512].rearrange("p (a b) -> p a b", a=2),
                           [B_pg[i]], [B_big])
            kb.barrier()

        def dense(b, br, npg, ABv, negms, kcol0, nk_off):
            for g in range(2):
                kb.op("pe", [mm(PS[0][:, p * 4:(p + 1) * 4], KsTs[:, g, p * 128:(p + 1) * 128], qTs[:, 4 * g:4 * g + 4, b]) for p in range(npg)],
                      [B_big, B_qTs], [PB[0]])
                td = tmpd[:, 0:npg, :]
                ts("dve", td, PS[0][:, 0:npg * 4].rearrange("p (a b) -> p a b", a=npg), SCALE, None, mult, None, [PB[0]], [B_esd])
                for r_ in range(4):
                    stt("dve", td[:, :, r_], ABv[:, 0:npg], -SLOPES[4 * g + r_], td[:, :, r_], mult, add, [B_esd, B_cs], [B_esd])
                if negms is not None:
                    td4 = tmpd.rearrange("p (q j) r -> p q j r", j=4)
                    tt("dve", td4, td4, negms[g].unsqueeze(2).unsqueeze(3).to_broadcast([128, 32, 4, 4]), add, [B_esd, B_negm[g]], [B_esd])
                actf(esd[:, 0:npg, :], td, AF.Exp, [B_esd], [B_esd])
                kb.op("pe", mm(PS[1][0:1, 0:4], kTn[:, kcol0 + g, b:b + 1], qTs[:, 4 * g:4 * g + 4, b]), [B_qTs], [PB[1]])
                actf(enew[0:1, 0:4], PS[1][0:1, 0:4], AF.Exp, [PB[1]], [B_new], scale=SCALE)
                dma("sp", vnf[0:1, :], NK.ap()[b:b + 1, nk_off + 256 + 128 * g:nk_off + 256 + 128 * g + 128], [B_NK], [B_new], "vn")
                cp("dve", vnew[0:1, :], vnf[0:1, :], [B_new], [B_new])
                kb.op("pe", [mm(PS[2][0:4, 0:128], enew, vnew, start=True, stop=False)] +
                      [mm(PS[2][0:4, 0:128], esd[:, p, :], VSs[:, p, g, :], start=False, stop=(p == npg - 1)) for p in range(npg)],
                      [B_new, B_esd, B_big], [PB[2]])
                kb.op("pe", [mm(PS[3][0:4, 0:2], enew, onesb, start=True, stop=False)] +
                      [mm(PS[3][0:4, 0:2], esd[:, p, :], onesb, start=False, stop=(p == npg - 1)) for p in range(npg)],
                      [B_new, B_esd, Bc], [PB[3]])
                recip(rdn[0:4, 0:1], PS[3][0:4, 0:1], [PB[3]], [B_usb])
                ts("dve", usb[0:4, :], PS[2][0:4, 0:128], rdn[0:4, 0:1], None, mult, None, [PB[2], B_usb], [B_usb])
                dma("sp", USC.ap()[b, br, 4 * g:4 * g + 4, :], usb[0:4, :], [B_usb], [B_USC], "us")

        for b in range(4):
            gather_pages(b, ccmp_d, 4, True)
            kb.op("pool", lambda e: e.memset(KcTs[:, :, 1023:1024], 0.0), [], [B_KCs])
            kb.op("pool", lambda e: e.memset(VCs[96:128, 7, :, :], 0.0), [], [B_KCs])
            for kvg in range(4):
                isk = kvg < 2
                g = kvg % 2
                w1 = w1k if isk else w1v
                for nbk in range(2):
                    N = 512 if nbk == 0 else 511
                    base = nbk * 8192
                    kb.op("pe", [mm(PS[2 + nbk][:, 0:N], w1[:, s_, :], SEG[:, kvg, base + s_:base + s_ + 16 * (N - 1) + 1:16],
                                    start=(s_ == 0), stop=(s_ == 31)) for s_ in range(32)], [B_big, Bc], [PB[2 + nbk]])
                    actf(hsb[:, nbk * 512:nbk * 512 + N], PS[2 + nbk][:, 0:N], AF.Silu, [PB[2 + nbk], Bc], [B_hsb],
                         bias=(b1k if isk else b1v)[:, 0:1])
                for it in range(8):
                    nb_ = 128 if it < 7 else 127
                    pj = 4 + (it % 2)
                    kb.op("pe", mm(PS[pj][0:nb_, 0:128], hsb[:, it * 128:it * 128 + nb_], (w2k if isk else w2v)[:]), [B_hsb, Bc], [PB[pj]])
                    if isk:
                        cp("act", kcw[0:nb_, :], PS[pj][0:nb_, 0:128], [PB[pj]], [B_kcw])
                        tt("dve", sq4[0:nb_, 0:128], kcw[0:nb_, :], kcw[0:nb_, :], mult, [B_kcw], [B_S4])
                        red(cst[0:nb_, 0:1], sq4[0:nb_, 0:128], [B_S4], [B_cst])
                        rsq(cst[0:nb_, 0:1], cst[0:nb_, 1:2], cst[0:nb_, 2:3], 1.0 / 128, B_cst)
                        stt("dve", kcw[0:nb_, :], kcw[0:nb_, :], cst[0:nb_, 2:3], knwc[0:nb_, :], mult, mult, [B_kcw, B_cst, Bc], [B_kcw])
                        kb.op("pe", mm(PS[6][:, 0:nb_], kcw[0:nb_, :], ident[0:nb_, 0:nb_]), [B_kcw, Bc], [PB[6]])
                        cp("act", KcTs[:, g, it * 128:it * 128 + nb_], PS[6][:, 0:nb_], [PB[6]], [B_KCs])
                    else:
                        cp("act", VCs[0:nb_, it, g, :], PS[pj][0:nb_, 0:128], [PB[pj]], [B_KCs])
            for g in range(2):
                kb.op("pe", [mm(PS[0][:, it * 4:(it + 1) * 4], KcTs[:, g, it * 128:(it + 1) * 128], qTs[:, 4 * g:4 * g + 4, b]) for it in range(8)],
                      [B_KCs, B_qTs], [PB[0]])
                stt("dve", tmpc, PS[0][:, 0:32].rearrange("p (a b) -> p a b", a=8), SCALE, abs_[:, :, 4 * g:4 * g + 4], mult, add,
                    [PB[0], B_cs], [B_ecs])
                actf(ecs, tmpc, AF.Exp, [B_ecs], [B_ecs])
                kb.op("pe", [mm(PS[1][0:4, 0:257], ecs[:, it, :], sels[:, it, :], start=(it == 0), stop=(it == 7)) for it in range(8)],
                      [B_ecs, B_cs], [PB[1]])
                kb.op("pe", [mm(PS[2][0:4, 0:128], ecs[:, it, :], VCs[:, it, g, :], start=(it == 0), stop=(it == 7)) for it in range(8)] +
                      [mm(PS[2][0:4, 128:130], ecs[:, it, :], onesb, start=(it == 0), stop=(it == 7)) for it in range(8)],
                      [B_ecs, B_KCs, Bc], [PB[2]])
                recip(rdn[0:4, 0:1], PS[2][0:4, 128:129], [PB[2]], [B_usb])
                ts("dve", usb[0:4, :], PS[2][0:4, 0:128], rdn[0:4, 0:1], None, mult, None, [PB[2], B_usb], [B_usb])
                dma("sp", USC.ap()[b, 0, 4 * g:4 * g + 4, :], usb[0:4, :], [B_usb], [B_USC], "us")
                ts("dve", impn[0:4, 0:257], PS[1][0:4, 0:257], rdn[0:4, 0:1], None, mult, None, [PB[1], B_usb], [B_rows])
                kb.op("pe", mm(PS[3][0:1, 0:257], ones[0:4, 0:1], impn[0:4, 0:257]), [B_rows, Bc], [PB[3]])
                tt("dve", score, PS[3][0:1, 0:257], a1s, mult, [PB[3], B_cs], [B_rows])
                tt("dve", score, score, a2s, add, [B_rows, B_cs], [B_rows])
                kb.op("dve", lambda e: e.max(out=m8[:, 0:8], in_=score), [B_rows], [B_rows])
                kb.op("dve", lambda e: e.match_replace(out=sc2, in_to_replace=m8[:, 0:8], in_values=score, imm_value=-30000.0), [B_rows], [B_rows])
                kb.op("dve", lambda e: e.max(out=m8[:, 8:16], in_=sc2), [B_rows], [B_rows])
                ts("dve", negsel, score, m8[:, 15:16], None, ALU.is_ge, None, [B_rows], [B_rows])
                ts("dve", negsel, negsel, -1.0, 30000.0, add, mult, [B_rows], [B_rows])
                kb.op("pe", [mm(PS[3][:, 0:32], hsel[c_], negsel[:, c_:256:8], start=(c_ == 0), stop=(c_ == 7)) for c_ in range(8)],
                      [B_rows, B_cs], [PB[3]])
                cp("act", negm[g], PS[3][:, 0:32], [PB[3]], [B_negm[g]])
            kb.barrier()
            gather_pages(b, cslc_d, 2, False)
            dense(b, 1, 128, distd, negm, 0, 512)
            for t_ in range(4):
                i = t_ % 2
                dma("sp", pg[i], stwin_d.ap()[b * 512 + t_ * 128:b * 512 + (t_ + 1) * 128, :], [], [B_pg[i]], "g%d" % i)
                pj = 6 + (t_ % 2)
                kb.op("pe", [mm(PS[pj][:, k * 128:(k + 1) * 128], pg[i][:, k * 128:(k + 1) * 128], ident) for k in range(2)],
                      [B_pg[i], Bc], [PB[pj]])
                cp("act", KsTs[:, 0:2, t_ * 128:(t_ + 1) * 128], PS[pj][:, 0:256].rearrange("p (a b) -> p a b", a=2), [PB[pj]], [B_big])
                cp("pool", VSs[:, t_, :, :], pg[i][:, 256:512].rearrange("p (a b) -> p a b", a=2), [B_pg[i]], [B_big])
            dense(b, 2, 4, distw, None, 2, 1024)
            kb.barrier()
        dma("sp", usc_t[0:4, :, :, :], USC.ap(), [B_USC], [B_fin], "us")
        for br in range(3):
            tt("dve", usc_t[0:4, br, :, :], usc_t[0:4, br, :, :], gs4[0:4, br * 8:(br + 1) * 8].unsqueeze(2).to_broadcast([4, 8, 128]), mult,
               [B_fin], [B_fin])
        tt("dve", usc_t[0:4, 0, :, :], usc_t[0:4, 0, :, :], usc_t[0:4, 1, :, :], add, [B_fin], [B_fin])
        tt("dve", usc_t[0:4, 0, :, :], usc_t[0:4, 0, :, :], usc_t[0:4, 2, :, :], add, [B_fin], [B_fin])
        dma("sp", za4[0:4, :], H.ap()[R0:R0 + 4, O_ZA:O_ZA + 1024], [B_H], [B_fin], "za")
        actf(za4[0:4, :], za4[0:4, :], AF.Silu, [B_fin], [B_fin])
        tt("dve", za4[0:4, :], za4[0:4, :], usc_t[0:4, 0, :, :].rearrange("p a b -> p (a b)"), mult, [B_fin], [B_fin])
        dma("sp", OA.ap()[R0:R0 + 4, :], za4[0:4, :], [B_fin], [B_OA], "oa")

    def phase_SGDN():
        ar.reset()
        R0 = 2 * T
        hc = ar.get([4, 3072])
        CW = ar.get([4, 3072])
        B_hc = kb.buf()
        B_CW = kb.buf()
        cs_ = ar.get([3072])
        B_cs = kb.buf()
        kTs = ar.get([8, 4])
        qT2 = ar.get([8, 4])
        kTz = ar.get([8, 4, 4])
        qTz = ar.get([8, 4, 4])
        B_T = kb.buf()
        S0t = ar.get([4, 8, 128])
        B_S0 = kb.buf()
        vn = ar.get([8, 128])
        o_ = ar.get([8, 128])
        B_vn = kb.buf()
        B_o = kb.buf()
        Sn = [ar.get([8, 128]) for _ in range(2)]
        B_Sn = kb.bufs(2)
        EGB = ar.get([32])
        B_EGB = kb.buf()
        st = ar.get([128])
        B_st = kb.buf()
        sqs = hc[0:4, 0, 0:2048]
        t1 = hc[0:4, 1, 0:1024].rearrange("p (a b) -> p a b", a=8)
        t2 = hc[0:4, 1, 1024:2048].rearrange("p (a b) -> p a b", a=8)
        vnm = hc[0:4, 2, 0:1024]
        zb4 = hc[0:4, 2, 1024:2048]
        ab = st[0:4, 0:16]
        beta = st[0:4, 16:24]
        gg = st[0:4, 24:32]
        eg = st[0:4, 32:40]
        ss = st[0:4, 40:64]
        tmp = st[0:4, 64:88]
        rn = st[0:4, 88:112]
        qk = st[0:4, 112:120]
        egd_t = ar.get([32])
        ones = C("ones")
        e4 = C("e4").rearrange("p (a b) -> p a b", a=4)
        dma("sp", CW[0:4, :, :].rearrange("p a b -> p (a b)"), convw_d.ap().partition_broadcast(4), [], [B_CW], "c0")
        dma("sp", hc[0:4, 0:3, :], stconv_d.ap(), [], [B_hc], "hw0")
        dma("sp", hc[0:4, 3, :], H.ap()[R0:R0 + 4, O_QKV:O_QKV + 3072], [B_H], [B_hc], "hw0")
        dma("sp", conv_s.ap(), hc[0:4, 1:4, :], [B_hc], [B_out], "o0")
        tt("pool", hc[0:4, :, :], hc[0:4, :, :], CW[0:4, :, :], mult, [B_hc, B_CW], [B_hc])
        c4 = cs_[0:4, :]
        tt("dve", c4, hc[0:4, 0, :], hc[0:4, 1, :], add, [B_hc], [B_cs])
        tt("dve", c4, c4, hc[0:4, 2, :], add, [B_cs, B_hc], [B_cs])
        tt("dve", c4, c4, hc[0:4, 3, :], add, [B_cs, B_hc], [B_cs])
        actf(c4, c4, AF.Silu, [B_cs], [B_cs])
        q3 = c4[:, 0:1024].rearrange("p (a b) -> p a b", a=8)
        k3 = c4[:, 1024:2048].rearrange("p (a b) -> p a b", a=8)
        v3_ = c4[:, 2048:3072].rearrange("p (a b) -> p a b", a=8)
        tt("dve", sqs, c4[:, 0:2048], c4[:, 0:2048], mult, [B_cs], [B_hc])
        red(ss[:, 0:16], sqs.rearrange("p (a b) -> p a b", a=16), [B_hc], [B_st])
        rsq(ss[:, 0:16], tmp[:, 0:16], rn[:, 0:16], 1.0, B_st)
        ts("dve", rn[:, 0:8], rn[:, 0:8], 128 ** -0.5, None, mult, None, [B_st], [B_st])
        tt("dve", c4[:, 0:2048].rearrange("p (a b) -> p a b", a=16), c4[:, 0:2048].rearrange("p (a b) -> p a b", a=16),
           rn[:, 0:16].unsqueeze(2).to_broadcast([4, 16, 128]), mult, [B_cs, B_st], [B_cs])
        dma("sp", ab, H.ap()[R0:R0 + 4, O_A:O_A + 16], [B_H], [B_st], "ab")
        actf(beta, ab[:, 8:16], AF.Sigmoid, [B_st], [B_st])
        tt("dve", gg, ab[:, 0:8], dtb[0:4, :], add, [B_st, Bc], [B_st])
        actf(gg, gg, AF.Exp, [B_st], [B_st])
        actf(gg, gg, AF.Ln, [B_st], [B_st], bias=1.0)
        tt("dve", gg, gg, nea[0:4, :], mult, [B_st, Bc], [B_st])
        actf(eg, gg, AF.Exp, [B_st], [B_st])
        kb.op("pe", [mm(PS[0][:, h * 4:(h + 1) * 4], c4[:, 1024 + h * 128:1024 + (h + 1) * 128], ident[0:4, 0:4]) for h in range(8)],
              [B_cs, Bc], [PB[0]])
        cp("act", kTs, PS[0][:, 0:32].rearrange("p (a b) -> p a b", a=8), [PB[0]], [B_T])
        kb.op("pe", [mm(PS[1][:, h * 4:(h + 1) * 4], c4[:, h * 128:(h + 1) * 128], ident[0:4, 0:4]) for h in range(8)],
              [B_cs, Bc], [PB[1]])
        cp("act", qT2, PS[1][:, 0:32].rearrange("p (a b) -> p a b", a=8), [PB[1]], [B_T])
        for (src, dstz) in ((kTs, kTz), (qT2, qTz)):
            tt("dve", dstz, src.unsqueeze(3).to_broadcast([128, 8, 4, 4]), e4.unsqueeze(1).to_broadcast([128, 8, 4, 4]), mult,
               [B_T, Bc], [B_T])
        for b in range(4):
            dma("sp" if b % 2 == 0 else "act", S0t[:, b, :, :], S0_d.ap()[b].rearrange("h k v -> k h v"), [], [B_S0], "s0%d" % (b % 2))
        for (Tz, banks) in ((kTz, (2, 3)), (qTz, (4, 5))):
            for half in range(2):
                ops = []
                for k in range(4):
                    h = 4 * half + k
                    for b in range(4):
                        ops.append(mm(PS[banks[half]][0:4, k * 128:(k + 1) * 128], Tz[:, h, b, :], S0t[:, b, h, :], start=(b == 0), stop=(b == 3)))
                kb.op("pe", ops, [B_T, B_S0], [PB[banks[half]]])
        v4 = vn[0:4, :, :]
        o4 = o_[0:4, :, :]
        for half in range(2):
            hs_ = slice(4 * half, 4 * half + 4)
            tt("dve", t1[:, hs_, :], PS[2 + half][0:4, :].rearrange("p (a b) -> p a b", a=4),
               eg[:, hs_].unsqueeze(2).to_broadcast([4, 4, 128]), mult, [PB[2 + half], B_st], [B_hc])
        tt("dve", v4, v3_, t1, sub, [B_cs, B_hc], [B_vn])
        tt("dve", v4, v4, beta.unsqueeze(2).to_broadcast([4, 8, 128]), mult, [B_vn, B_st], [B_vn])
        tt("dve", t2, q3, k3, mult, [B_cs], [B_hc])
        red(qk, t2, [B_hc], [B_st])
        for half in range(2):
            hs_ = slice(4 * half, 4 * half + 4)
            tt("dve", o4[:, hs_, :], PS[4 + half][0:4, :].rearrange("p (a b) -> p a b", a=4),
               eg[:, hs_].unsqueeze(2).to_broadcast([4, 4, 128]), mult, [PB[4 + half], B_st], [B_o])
        tt("dve", t2, v4, qk.unsqueeze(2).to_broadcast([4, 8, 128]), mult, [B_vn, B_st], [B_hc])
        tt("dve", o4, o4, t2, add, [B_o, B_hc], [B_o])
        tt("dve", egd_t[0:4, :].rearrange("p (a b) -> p a b", a=4), eg.unsqueeze(1).to_broadcast([4, 4, 8]),
           ident[0:4, 0:4].unsqueeze(2).to_broadcast([4, 4, 8]), mult, [B_st, Bc], [B_EGB])
        kb.op("pe", mm(PS[6][:, 0:32], ones[0:4, :], egd_t[0:4, :]), [B_EGB, Bc], [PB[6]])
        cp("act", EGB, PS[6][:, 0:32], [PB[6]], [B_EGB])
        for b in range(4):
            ts("dve", vnm, v4.rearrange("p a b -> p (a b)"), ident[0:4, b:b + 1], None, mult, None, [B_vn, Bc], [B_hc])
            Sb = Sn[b % 2]
            for half in range(2):
                pj = 2 * (b % 2) + half
                kb.op("pe", [mm(PS[pj][:, k * 128:(k + 1) * 128], c4[:, 1024 + (4 * half + k) * 128:1024 + (4 * half + k + 1) * 128],
                                vnm[:, (4 * half + k) * 128:(4 * half + k + 1) * 128]) for k in range(4)], [B_cs, B_hc], [PB[pj]])
                for k in range(4):
                    h = 4 * half + k
                    stt("dve", Sb[:, h, :], S0t[:, b, h, :], EGB[:, b * 8 + h:b * 8 + h + 1], PS[pj][:, k * 128:(k + 1) * 128], mult, add,
                        [PB[pj], B_S0, B_EGB], [B_Sn[b % 2]])
            dma("sp", S_s.ap()[b].rearrange("h k v -> k h v"), Sb, [B_Sn[b % 2]], [B_out], "ss%d" % (b % 2))
        tt("dve", t1, o4, o4, mult, [B_o], [B_hc])
        red(ss[:, 16:24], t1, [B_hc], [B_st])
        rsq(ss[:, 16:24], tmp[:, 16:24], rn[:, 16:24], 1.0 / 128, B_st)
        tt("dve", o4, o4, rn[:, 16:24].unsqueeze(2).to_broadcast([4, 8, 128]), mult, [B_o, B_st], [B_o])
        tt("dve", o4, o4, gnw[0:4, :].unsqueeze(1).to_broadcast([4, 8, 128]), mult, [B_o, Bc], [B_o])
        dma("sp", zb4, H.ap()[R0:R0 + 4, O_ZB:O_ZB + 1024], [B_H], [B_hc], "zb")
        actf(zb4, zb4, AF.Silu, [B_hc], [B_hc])
        tt("dve", zb4, zb4, o4.rearrange("p a b -> p (a b)"), mult, [B_hc, B_o], [B_hc])
        dma("sp", OB.ap()[R0:R0 + 4, :], zb4, [B_hc], [B_OB], "ob")

    for seq in range(2):
        if "A" in PHASES:
            phase_A(seq)
            kb.barrier()
        if "NSA" in PHASES:
            phase_NSA(seq)
            kb.barrier()
        if "GDN" in PHASES:
            phase_GDN(seq)
            kb.barrier()
        dma("sp", conv_p.ap()[seq * 3:seq * 3 + 3, :], H.ap()[seq * T + T - 3:seq * T + T, O_QKV:O_QKV + 3072], [B_H], [B_out], "o0")
    if "SAMPLE" in PHASES:
        phase_SNSA()
        kb.barrier()
        phase_SGDN()
        kb.barrier()
    if "F" in PHASES:
        phase_F()
    nc = kb.finish()
    return nc


_NC = None


def kernel(**inp):
    global _NC
    f32 = np.float32
    if _NC is None:
        _NC = build()
    nc = _NC
    c32_h, cb_h = _HC[0], _HC[1]
    xp = np.asarray(inp["x_prompt"], dtype=f32)
    xs = np.asarray(inp["x_sample"], dtype=f32)
    g = lambda k: np.ascontiguousarray(np.asarray(inp[k], dtype=f32)[0])
    shared = dict(
        w_in=g("w_in"), norm_w=g("norm_w").reshape(1, DM), c32=c32_h, cbs=cb_h,
        q_norm_w=g("q_norm_w").reshape(1, 128), k_norm_cmp_w=g("k_norm_cmp_w").reshape(1, 128),
        k_norm_slc_w=g("k_norm_slc_w").reshape(1, 128), k_norm_win_w=g("k_norm_win_w").reshape(1, 128),
        cmp_w1_k=g("cmp_w1_k").reshape(32, 128, 128), cmp_w1_v=g("cmp_w1_v").reshape(32, 128, 128),
        cmp_b1_k=g("cmp_b1_k").reshape(1, 128), cmp_b1_v=g("cmp_b1_v").reshape(1, 128),
        cmp_w2_k=g("cmp_w2_k"), cmp_w2_v=g("cmp_w2_v"),
        conv_w=g("conv_w").reshape(1, 4 * 3072), a_log=g("a_log").reshape(1, 8), dt_bias=g("dt_bias").reshape(1, 8),
        gdn_norm_w=g("gdn_norm_w").reshape(1, 128), w_pa=g("w_pa"), w_pb=g("w_pb"), w_o=g("w_o"),
    )
    ccmp = np.asarray(inp["cache_cmp_kv"], dtype=f32).reshape(5120 * 32, 2048)
    cslc = np.asarray(inp["cache_slc_kv"], dtype=f32).reshape(5120 * 32, 2048)
    stw = np.asarray(inp["state_win_kv"], dtype=f32).reshape(32 * 512, 512)
    ptab = np.asarray(inp["page_table"]).astype(np.int32)
    stc = np.asarray(inp["state_gdn_conv"], dtype=f32)[0]
    sS = np.asarray(inp["state_gdn_S"], dtype=f32)[0]
    shared.update(ccmp=ccmp, cslc=cslc, cs32=_HS[0], csb=_HS[1])
    in_maps = []
    for c in range(NCORES):
        m = dict(shared)
        m["stwin"] = np.ascontiguousarray(stw[4 * c * 512:(4 * c + 4) * 512])
        m["ptab"] = np.ascontiguousarray(ptab[4 * c:4 * c + 4].reshape(4, 32, 4).transpose(2, 0, 1)).reshape(4, 128)
        m["stconv"] = np.ascontiguousarray(stc[4 * c:4 * c + 4])
        m["S0"] = np.ascontiguousarray(sS[4 * c:4 * c + 4])
        m["x_p"] = np.ascontiguousarray(xp[2 * c:2 * c + 2]).reshape(2 * T, DM)
        m["x_s"] = np.ascontiguousarray(xs[4 * c:4 * c + 4]).reshape(4, DM)
        in_maps.append(m)
    res = run_bass_kernel_spmd(nc, in_maps, core_ids=list(range(NCORES)))
    R = res.results
    cat = lambda k: np.concatenate([r[k] for r in R], axis=0)
    y_p = cat("y_p").reshape(16, T, DM)
    cmp_pv = cat("cmp_p").reshape(1, 16, T, 2, 2, 128)
    slc_pv = cat("slc_p").reshape(1, 16, T, 2, 2, 128)
    win_pv = cat("win_p").reshape(1, 16, 512, 2, 2, 128)
    conv_pv = cat("conv_p").reshape(1, 16, 3, 3072)
    if DEBUG:
        global _DBG
        _DBG = dict(OA=R[0]["OAs"], OB=R[0]["OBs"])
    S_pv = np.stack([r["S_p"] for r in R], axis=0).reshape(1, 16, 8, 128, 128)
    y_sv = cat("y_s").reshape(32, 1, DM)
    cmp_sv = cat("cmp_s").reshape(1, 32, 1, 2, 2, 128)
    slc_sv = cat("slc_s").reshape(1, 32, 1, 2, 2, 128)
    win_sv = cat("win_s").reshape(1, 32, 512, 2, 2, 128)
    S_sv = cat("S_s").reshape(1, 32, 8, 128, 128)
    conv_sv = cat("conv_s").reshape(1, 32, 3, 3072)
    return (y_p, y_sv, cmp_pv, slc_pv, win_pv, S_pv, conv_pv, cmp_sv, slc_sv, win_sv, S_sv, conv_sv)
```
